# Optimizing a Trainium2 kernel written in Bass

```python
import math
import jax, jax.numpy as jnp
from jax import lax
import numpy as np

D_MODEL = 1024
BATCH = 8
SEQ = 2048
DEPTH = 4

CTX_LEN = 256
GRID_W = 64
EPS = 1e-6
NEG_INF = -1e9

HY_DIM = 256
HY_ORDER = 2
HY_SHORT = 3
HY_BANDS = 8
HY_POS_DIM = 1 + 2 * HY_BANDS
HY_FILT_HID = 64
HY_DECAY_TARGET = 1e-2
HY_FAST = 0.3
HY_SLOW = 1.5

NA_HEADS = 4
NA_HEAD_DIM = 64
NA_DIM = NA_HEADS * NA_HEAD_DIM
NA_WIN_ROWS = 8
NA_WIN_COLS = 16
NA_QCOLS = 16
NA_KCOLS = NA_QCOLS + NA_WIN_COLS
NA_NCB = GRID_W // NA_QCOLS

MLA_HEADS = 8
MLA_Q_RANK = 256
MLA_KV_RANK = 128
MLA_NOPE = 64
MLA_ROPE = 32
MLA_V = 64
MLA_QK = MLA_NOPE + MLA_ROPE
MLA_DIM = MLA_HEADS * MLA_V
MLA_QBLOCK = 128
ROPE_BASE = 10000.0
ROPE_FREQS = MLA_ROPE // 4

MIX_DIM = HY_DIM + NA_DIM + MLA_DIM
D_FF = 4 * D_MODEL
IN_HY = 3 * HY_DIM
IN_NA = 3 * NA_DIM
IN_MLA = MLA_Q_RANK + MLA_KV_RANK + MLA_ROPE
IN_DIM = IN_HY + IN_NA + IN_MLA

kernel_name = "hybrid_hyena_na_mla_prefix_dit"

F32 = jnp.float32


def _rms(x, g):
    xf = x.astype(F32)
    y = xf * lax.rsqrt(jnp.mean(xf * xf, axis=-1, keepdims=True) + EPS)
    return (y * g.astype(F32)).astype(x.dtype)


def _modulation(cond, w_mod, b_mod):
    m = jax.nn.silu(cond) @ w_mod + b_mod
    return jnp.split(m[:, None, :], 6, axis=-1)


def _adaln(h, g, shift, scale):
    return _rms(h, g) * (1.0 + scale) + shift


def _heads(t, n):
    b, l, _ = t.shape
    return t.reshape(b, l, n, -1).transpose(0, 2, 1, 3)


def _merge(t):
    b, n, l, d = t.shape
    return t.transpose(0, 2, 1, 3).reshape(b, l, n * d)


def _dense_attend(q, k, v):
    s = jnp.einsum('bhqd,bhkd->bhqk', q, k).astype(F32) * (q.shape[-1] ** -0.5)
    p = jax.nn.softmax(s, axis=-1).astype(v.dtype)
    return jnp.einsum('bhqk,bhkd->bhqd', p, v)


def _short_conv(u, w, b):
    y = lax.conv_general_dilated(u, w[:, None, :].astype(u.dtype), window_strides=(1,), padding='SAME',
                                 dimension_numbers=('NWC', 'WIO', 'NWC'), feature_group_count=u.shape[-1])
    return y + b


def _hyena_filters(L, w1, b1, w2, b2, w3, b3, freq):
    t = jnp.linspace(0.0, 1.0, L, dtype=F32)[:, None]
    w = (2.0 * math.pi / L) * jnp.arange(L, dtype=F32)[:, None]
    bands = jnp.linspace(1e-4, HY_BANDS - 1, HY_BANDS, dtype=F32)
    z = jnp.concatenate([t, jnp.cos(bands * w), -jnp.sin(bands * w)], axis=-1)
    freq = freq.astype(F32)
    hdn = jnp.sin(freq[0] * (z @ w1.astype(F32) + b1.astype(F32)))
    hdn = jnp.sin(freq[1] * (hdn @ w2.astype(F32) + b2.astype(F32)))
    h = (hdn @ w3.astype(F32) + b3.astype(F32)).reshape(L, HY_ORDER, 2, HY_DIM)
    deltas = jnp.linspace(math.log(HY_DECAY_TARGET) / HY_SLOW, math.log(HY_DECAY_TARGET) / HY_FAST,
                          HY_DIM, dtype=F32)
    decay = jnp.exp(-t * jnp.abs(deltas))
    h = h * decay[:, None, None, :]
    return h / (jnp.sum(jnp.abs(h), axis=(0, 2), keepdims=True) + EPS)


def _bidir_fftconv(u, h, bias):
    L, C = u.shape[1], u.shape[2]
    k = jnp.concatenate([h[:, 0], jnp.zeros((1, C), h.dtype), h[:0:-1, 1]], axis=0)
    uf = u.astype(F32)
    y = jnp.fft.irfft(jnp.fft.rfft(uf, n=2 * L, axis=1) * jnp.fft.rfft(k, axis=0), n=2 * L, axis=1)[:, :L]
    return (y + uf * bias.astype(F32)).astype(u.dtype)


def _hyena(p, conv_w, conv_b, filt, bias):
    p = _short_conv(p, conv_w, conv_b)
    v, x1, x2 = jnp.split(p, 3, axis=-1)
    z = v
    for o, gate in enumerate((x1, x2)):
        z = gate * _bidir_fftconv(z, filt[:, o], bias[o])
    return z


def _na_qkv(p, g_q, g_k):
    q, k, v = jnp.split(p, 3, axis=-1)
    return _rms(_heads(q, NA_HEADS), g_q), _rms(_heads(k, NA_HEADS), g_k), _heads(v, NA_HEADS)


def _na_latent(q, k, v, kc, vc, rpb):
    B, H, S, dh = q.shape
    R = S // GRID_W
    wr = min(NA_WIN_ROWS, R)
    r = jnp.arange(R)
    row_idx = jnp.clip(r - NA_WIN_ROWS // 2, 0, R - wr)[:, None] + jnp.arange(wr)
    qcol = jnp.arange(GRID_W).reshape(NA_NCB, NA_QCOLS)
    bstart = jnp.clip(jnp.arange(NA_NCB) * NA_QCOLS - NA_WIN_COLS // 2, 0, GRID_W - NA_KCOLS)
    col_idx = bstart[:, None] + jnp.arange(NA_KCOLS)
    ri = row_idx[:, None, :, None]
    ci = col_idx[None, :, None, :]

    def gather(t):
        tg = t.reshape(B, H, R, GRID_W, t.shape[-1])[:, :, ri, ci]
        return tg.reshape(B, H, R, NA_NCB, wr * NA_KCOLS, t.shape[-1])

    kb, vb = gather(k), gather(v)
    qg = q.reshape(B, H, R, NA_NCB, NA_QCOLS, dh)
    cstart = jnp.clip(qcol - NA_WIN_COLS // 2, 0, GRID_W - NA_WIN_COLS)
    kcol = col_idx[:, None, :]
    valid = (kcol >= cstart[..., None]) & (kcol < cstart[..., None] + NA_WIN_COLS)
    dr = row_idx - r[:, None] + NA_WIN_ROWS - 1
    dc = jnp.clip(kcol - qcol[..., None] + NA_WIN_COLS - 1, 0, 2 * NA_WIN_COLS - 2)
    bias = rpb.astype(F32)[:, dr[:, None, None, :, None], dc[None, :, :, None, :]]
    bias = jnp.where(valid[None, None, :, :, None, :], bias, NEG_INF)
    bias = bias.reshape(H, R, NA_NCB, NA_QCOLS, wr * NA_KCOLS)
    scale = dh ** -0.5
    s_loc = jnp.einsum('bhrjqd,bhrjkd->bhrjqk', qg, kb).astype(F32) * scale + bias[None]
    s_ctx = jnp.einsum('bhrjqd,bhcd->bhrjqc', qg, kc).astype(F32) * scale
    n_loc = s_loc.shape[-1]
    p = jax.nn.softmax(jnp.concatenate([s_loc, s_ctx], axis=-1), axis=-1).astype(v.dtype)
    o = (jnp.einsum('bhrjqk,bhrjkd->bhrjqd', p[..., :n_loc], vb)
         + jnp.einsum('bhrjqc,bhcd->bhrjqd', p[..., n_loc:], vc))
    return o.reshape(B, H, S, dh)


def _axial_rope_tables(S):
    t = jnp.arange(S)
    pos = jnp.stack([t // GRID_W, t % GRID_W], axis=-1).astype(F32)
    inv = ROPE_BASE ** (-jnp.arange(ROPE_FREQS, dtype=F32) / ROPE_FREQS)
    ang = pos[:, :, None] * inv
    return jnp.cos(ang), jnp.sin(ang)


def _axial_rope(x, cos, sin):
    xs = x.reshape(x.shape[:-1] + (2, 2, ROPE_FREQS))
    x1, x2 = xs[..., 0, :], xs[..., 1, :]
    c = cos[:, None].astype(x.dtype)
    s = sin[:, None].astype(x.dtype)
    return jnp.stack([x1 * c - x2 * s, x2 * c + x1 * s], axis=-2).reshape(x.shape)


def _mla_qkv(p, g_qa, g_kva, w_q_up, w_kv_up, g_q, g_k, rope):
    B, L, _ = p.shape
    cq, ckv, kr = jnp.split(p, [MLA_Q_RANK, MLA_Q_RANK + MLA_KV_RANK], axis=-1)
    q = (_rms(cq, g_qa) @ w_q_up).reshape(B, L, MLA_HEADS, MLA_QK)
    kv = (_rms(ckv, g_kva) @ w_kv_up).reshape(B, L, MLA_HEADS, MLA_NOPE + MLA_V)
    k = jnp.concatenate([kv[..., :MLA_NOPE],
                         jnp.broadcast_to(kr[:, :, None, :], (B, L, MLA_HEADS, MLA_ROPE))], axis=-1)
    v = kv[..., MLA_NOPE:]
    q = _rms(q, g_q)
    k = _rms(k, g_k)
    if rope is not None:
        cos, sin = rope
        q = jnp.concatenate([q[..., :MLA_NOPE], _axial_rope(q[..., MLA_NOPE:], cos, sin)], axis=-1)
        k = jnp.concatenate([k[..., :MLA_NOPE], _axial_rope(k[..., MLA_NOPE:], cos, sin)], axis=-1)
    return q.transpose(0, 2, 1, 3), k.transpose(0, 2, 1, 3), v.transpose(0, 2, 1, 3)


def _mla_latent(q, k, v, kc, vc):
    B, H, S, dq = q.shape
    kall = jnp.concatenate([k, kc], axis=2)
    vall = jnp.concatenate([v, vc], axis=2)
    nb = S // MLA_QBLOCK
    qb = q.reshape(B, H, nb, MLA_QBLOCK, dq).transpose(2, 0, 1, 3, 4)
    scale = dq ** -0.5

    def block(qi):
        s = jnp.einsum('bhqd,bhkd->bhqk', qi, kall).astype(F32) * scale
        return jnp.einsum('bhqk,bhkd->bhqd', jax.nn.softmax(s, axis=-1).astype(vall.dtype), vall)

    o = lax.map(block, qb)
    return o.transpose(1, 2, 0, 3, 4).reshape(B, H, S, v.shape[-1])


def _sqrelu_mlp(h, w1, b1, w2, b2):
    a = jax.nn.relu(h @ w1 + b1)
    return (a * a) @ w2 + b2


def setup_inputs(seed: int = 0) -> dict:
    key = jax.random.key(seed)
    ks = iter(jax.random.split(key, 40))

    def nrm(shape, scale):
        return jax.random.normal(next(ks), shape, F32) * scale

    def gain(shape):
        return 1.0 + nrm(shape, 0.02)

    L = DEPTH
    return {
        "x": nrm((BATCH, SEQ, D_MODEL), 1.0),
        "c": nrm((BATCH, D_MODEL), 1.0),
        "ctx": nrm((BATCH, CTX_LEN, D_MODEL), 1.0),
        "c_ctx": nrm((D_MODEL,), 1.0),
        "w_mod": nrm((L, D_MODEL, 6 * D_MODEL), 0.5 * D_MODEL ** -0.5),
        "b_mod": nrm((L, 6 * D_MODEL), 0.02),
        "g_norm1": gain((L, D_MODEL)),
        "w_in": nrm((L, D_MODEL, IN_DIM), D_MODEL ** -0.5),
        "hy_conv_w": nrm((L, HY_SHORT, IN_HY), HY_SHORT ** -0.5),
        "hy_conv_b": nrm((L, IN_HY), 0.02),
        "hy_f_w1": nrm((L, HY_POS_DIM, HY_FILT_HID), HY_POS_DIM ** -0.5),
        "hy_f_b1": nrm((L, HY_FILT_HID), 0.02),
        "hy_f_w2": nrm((L, HY_FILT_HID, HY_FILT_HID), HY_FILT_HID ** -0.5),
        "hy_f_b2": nrm((L, HY_FILT_HID), 0.02),
        "hy_f_w3": nrm((L, HY_FILT_HID, HY_ORDER * 2 * HY_DIM), HY_FILT_HID ** -0.5),
        "hy_f_b3": nrm((L, HY_ORDER * 2 * HY_DIM), 0.02),
        "hy_freq": gain((L, 2, HY_FILT_HID)),
        "hy_bias": nrm((L, HY_ORDER, HY_DIM), 0.5),
        "na_g_q": gain((L, NA_HEAD_DIM)),
        "na_g_k": gain((L, NA_HEAD_DIM)),
        "na_rpb": nrm((L, NA_HEADS, 2 * NA_WIN_ROWS - 1, 2 * NA_WIN_COLS - 1), 0.02),
        "mla_g_qa": gain((L, MLA_Q_RANK)),
        "mla_g_kva": gain((L, MLA_KV_RANK)),
        "mla_w_q_up": nrm((L, MLA_Q_RANK, MLA_HEADS * MLA_QK), MLA_Q_RANK ** -0.5),
        "mla_w_kv_up": nrm((L, MLA_KV_RANK, MLA_HEADS * (MLA_NOPE + MLA_V)), MLA_KV_RANK ** -0.5),
        "mla_g_q": gain((L, MLA_QK)),
        "mla_g_k": gain((L, MLA_QK)),
        "w_out": nrm((L, MIX_DIM, D_MODEL), MIX_DIM ** -0.5),
        "g_norm2": gain((L, D_MODEL)),
        "w_ff1": nrm((L, D_MODEL, D_FF), D_MODEL ** -0.5),
        "b_ff1": nrm((L, D_FF), 0.02),
        "w_ff2": nrm((L, D_FF, D_MODEL), D_FF ** -0.5),
        "b_ff2": nrm((L, D_MODEL), 0.02),
    }


def reference(x, c, ctx, c_ctx, w_mod, b_mod, g_norm1, w_in, hy_conv_w, hy_conv_b, hy_f_w1, hy_f_b1,
              hy_f_w2, hy_f_b2, hy_f_w3, hy_f_b3, hy_freq, hy_bias, na_g_q, na_g_k, na_rpb, mla_g_qa,
              mla_g_kva, mla_w_q_up, mla_w_kv_up, mla_g_q, mla_g_k, w_out, g_norm2, w_ff1, b_ff1, w_ff2, b_ff2):
    S = x.shape[1]
    Lc = ctx.shape[1]
    rope = _axial_rope_tables(S)
    cx = ctx
    for i in range(DEPTH):
        last = i == DEPTH - 1
        mx = _modulation(c, w_mod[i], b_mod[i])
        mc = _modulation(c_ctx[None], w_mod[i], b_mod[i])
        px = _adaln(x, g_norm1[i], mx[0], mx[1]) @ w_in[i]
        pc = _adaln(cx, g_norm1[i], mc[0], mc[1]) @ w_in[i]
        px_hy, px_na, px_mla = jnp.split(px, [IN_HY, IN_HY + IN_NA], axis=-1)
        pc_hy, pc_na, pc_mla = jnp.split(pc, [IN_HY, IN_HY + IN_NA], axis=-1)

        qx, kx, vx = _na_qkv(px_na, na_g_q[i], na_g_k[i])
        qc, kc, vc = _na_qkv(pc_na, na_g_q[i], na_g_k[i])
        o_na = _na_latent(qx, kx, vx, kc, vc, na_rpb[i])

        mqx, mkx, mvx = _mla_qkv(px_mla, mla_g_qa[i], mla_g_kva[i], mla_w_q_up[i], mla_w_kv_up[i],
                                 mla_g_q[i], mla_g_k[i], rope)
        mqc, mkc, mvc = _mla_qkv(pc_mla, mla_g_qa[i], mla_g_kva[i], mla_w_q_up[i], mla_w_kv_up[i],
                                 mla_g_q[i], mla_g_k[i], None)
        o_mla = _mla_latent(mqx, mkx, mvx, mkc, mvc)

        filt_x = _hyena_filters(S, hy_f_w1[i], hy_f_b1[i], hy_f_w2[i], hy_f_b2[i], hy_f_w3[i], hy_f_b3[i], hy_freq[i])
        o_hy = _hyena(px_hy, hy_conv_w[i], hy_conv_b[i], filt_x, hy_bias[i])

        mix = jnp.concatenate([o_hy, _merge(o_na), _merge(o_mla)], axis=-1) @ w_out[i]
        x = x + mx[2] * mix
        x = x + mx[5] * _sqrelu_mlp(_adaln(x, g_norm2[i], mx[3], mx[4]), w_ff1[i], b_ff1[i], w_ff2[i], b_ff2[i])

        if not last:
            filt_c = _hyena_filters(Lc, hy_f_w1[i], hy_f_b1[i], hy_f_w2[i], hy_f_b2[i], hy_f_w3[i], hy_f_b3[i],
                                    hy_freq[i])
            oc_hy = _hyena(pc_hy, hy_conv_w[i], hy_conv_b[i], filt_c, hy_bias[i])
            oc_na = _dense_attend(qc, kc, vc)
            oc_mla = _dense_attend(mqc, mkc, mvc)
            mix_c = jnp.concatenate([oc_hy, _merge(oc_na), _merge(oc_mla)], axis=-1) @ w_out[i]
            cx = cx + mc[2] * mix_c
            cx = cx + mc[5] * _sqrelu_mlp(_adaln(cx, g_norm2[i], mc[3], mc[4]), w_ff1[i], b_ff1[i], w_ff2[i],
                                          b_ff2[i])
    return x
```

```python
import math
import contextlib
import numpy as np
import ml_dtypes
import concourse.bass as bass
import concourse.mybir as mybir
from concourse.bass_utils import run_bass_kernel_spmd

F32 = mybir.dt.float32
BF16 = mybir.dt.bfloat16
ALU = mybir.AluOpType
AF = mybir.ActivationFunctionType

D = 1024; S = 2048; LC = 256; T = S + LC; NL = 4
IN_DIM = 1952; DFF = 4096
EPS = 1e-6
TB = [(0, 512), (512, 512), (1024, 512), (1536, 512), (2048, 256)]
PI = math.pi
RPB_PAD = 64 + 4 * 15 * 31 + 64
HSTOP = 0

ENGS = ("pe", "act", "dve", "pool", "sp")
N_DSEM = 6


class Buf:
    __slots__ = ("name", "w", "r")

    def __init__(self, name):
        self.name = name; self.w = []; self.r = []


class Prog:
    def __init__(self, nc):
        self.nc = nc
        self.ops = {e: [] for e in ENGS}
        self.cnt = {e: 0 for e in ENGS}
        self.known = {e: {} for e in ENGS}
        self.dcnt = {}; self.dq = {e: 0 for e in ENGS}; self.dlast = {}
        self.sems = {}

    def _need(self, eng, toks):
        best = {}
        for t in toks:
            if t is None:
                continue
            k, v = t
            if k == eng and eng == "pe":
                continue
            if self.known[eng].get(k, 0) >= v:
                continue
            if best.get(k, 0) < v:
                best[k] = v
        for k, v in best.items():
            self.known[eng][k] = v
        return list(best.items())

    def _track(self, tok, reads, writes):
        for b in reads:
            b.r.append(tok)
            if len(b.r) > 64:
                b.r = _compress(b.r)
        for b in writes:
            b.w = [tok]; b.r = []

    def op(self, eng, fn, reads=(), writes=()):
        toks = []
        for b in reads:
            toks.extend(b.w)
        for b in writes:
            toks.extend(b.w); toks.extend(b.r)
        waits = self._need(eng, toks)
        self.cnt[eng] += 1
        tok = (eng, self.cnt[eng])
        self.ops[eng].append((waits, fn, ("e", eng)))
        self._track(tok, reads, writes)
        return tok

    def dma(self, q, fn, reads=(), writes=()):
        i = self.dq[q]; self.dq[q] += 1
        key = "d_%s_%d" % (q, i % N_DSEM)
        toks = [self.dlast.get(key)]
        for b in reads:
            toks.extend(b.w)
        for b in writes:
            toks.extend(b.w); toks.extend(b.r)
        waits = self._need(q, toks)
        self.dcnt[key] = self.dcnt.get(key, 0) + 16
        tok = (key, self.dcnt[key])
        self.dlast[key] = tok
        self.ops[q].append((waits, fn, ("d", key)))
        self._track(tok, reads, writes)
        return tok

    def retire(self, old, new):
        toks = []
        for b in old:
            toks.extend(b.w); toks.extend(b.r)
        toks = _compress(toks)
        for b in new:
            b.w = _compress(list(b.w) + toks)
            b.r = _compress(list(b.r) + toks)

    def final_wait(self, eng, bufs):
        toks = []
        for b in bufs:
            toks.extend(b.w); toks.extend(b.r)
        waits = self._need(eng, toks)
        self.ops[eng].append((waits, None, None))

    def emit(self):
        nc = self.nc
        keys = list(ENGS) + sorted(self.dcnt.keys())
        with contextlib.ExitStack() as st:
            for k in keys:
                self.sems[k] = st.enter_context(nc.semaphore("s_" + k))
            block = st.enter_context(nc.Block())
            sems = self.sems

            def run(e, name):
                for waits, fn, inc in self.ops[name]:
                    for k, v in waits:
                        e.wait_ge(sems[k], v)
                    if fn is None:
                        continue
                    ins = fn(e)
                    ins.then_inc(sems[inc[1]], 1 if inc[0] == "e" else 16)

            @block.tensor
            def _(e):
                run(e, "pe")

            @block.scalar
            def _(e):
                run(e, "act")

            @block.vector
            def _(e):
                run(e, "dve")

            @block.gpsimd
            def _(e):
                run(e, "pool")

            @block.sync
            def _(e):
                run(e, "sp")


def _compress(toks):
    best = {}
    for k, v in toks:
        if best.get(k, 0) < v:
            best[k] = v
    return list(best.items())


_CONSTS = None


def _dft_tiles(L):
    N = 2 * L; ntc = L // 128; GJ = min(4, ntc); G = ntc // GJ
    a = np.arange(L, dtype=np.float64)
    th = np.pi * np.outer(2 * a + 1, 2 * a + 1) / (2 * N)
    M = np.stack([np.cos(th), np.sin(th)])
    M6 = M.reshape(2, ntc, 128, G, GJ, 128)
    Dm = np.ascontiguousarray(M6.transpose(3, 2, 0, 1, 4, 5))
    Dm = Dm.reshape(G, 128, GJ * 2 * ntc * 128).astype(ml_dtypes.bfloat16)
    f = np.arange(L, dtype=np.float64)
    al = np.pi * (2 * f + 1) / (2 * N)
    ca = np.cos(al).reshape(ntc, 128).T; sa = np.sin(al).reshape(ntc, 128).T
    alpha = np.ascontiguousarray(np.stack([ca, sa, -sa], axis=1)).astype(np.float32)
    return Dm, alpha


def _filt_consts(L):
    t = np.linspace(0.0, 1.0, L, dtype=np.float32)[:, None]
    w = (np.float32(2.0 * math.pi / L) * np.arange(L, dtype=np.float32))[:, None]
    bands = np.linspace(1e-4, 7, 8, dtype=np.float32)
    z = np.concatenate([t, np.cos(bands * w), -np.sin(bands * w)], axis=-1).astype(np.float32)
    deltas = np.linspace(math.log(1e-2) / 1.5, math.log(1e-2) / 0.3, 256, dtype=np.float32)
    dec = np.exp(-t * np.abs(deltas)).astype(np.float32)
    decs = np.zeros_like(dec); decs[:-1] = dec[1:]
    zT = np.zeros((17, L), np.float32); zT[:] = z.T
    return np.ascontiguousarray(zT), np.ascontiguousarray(np.stack([dec, decs], axis=1))


def _consts():
    global _CONSTS
    if _CONSTS is not None:
        return _CONSTS
    c = {}
    c["dftL"], c["alphaL"] = _dft_tiles(S)
    c["dftC"], c["alphaC"] = _dft_tiles(LC)
    c["zTL"], c["decL"] = _filt_consts(S)
    c["zTC"], c["decC"] = _filt_consts(LC)
    Ct = np.ones((96, T), np.float32); St = np.zeros((96, T), np.float32)
    tt = np.arange(S)
    pos = np.stack([tt // 64, tt % 64], axis=-1).astype(np.float32)
    inv = (10000.0 ** (-np.arange(8, dtype=np.float32) / 8)).astype(np.float32)
    ang = pos[:, :, None] * inv
    for a in range(2):
        for h in range(2):
            for f in range(8):
                Ct[64 + a * 16 + h * 8 + f, :S] = np.cos(ang[:, a, f])
                St[64 + a * 16 + h * 8 + f, :S] = np.sin(ang[:, a, f])
    c["ropeC"] = Ct; c["ropeS"] = St
    Pm = np.zeros((96, 96), np.float32)
    for a in range(2):
        for f in range(8):
            Pm[64 + a * 16 + 8 + f, 64 + a * 16 + f] = -1.0
            Pm[64 + a * 16 + f, 64 + a * 16 + 8 + f] = 1.0
    c["permT"] = Pm.astype(ml_dtypes.bfloat16)
    qc = np.arange(64)[:, None]; kc = np.arange(64)[None, :]
    cs = np.clip(qc - 8, 0, 48)
    valid = ((kc >= cs) & (kc < cs + 16)).astype(np.float32)
    m = np.stack([valid * 8.0, (1.0 - valid) * (-240000.0)], axis=1)
    m = m[::-1]
    c["namask"] = np.ascontiguousarray(np.concatenate([m, m], axis=0)).astype(np.float32)
    idb = np.zeros((128, 128), np.float32)
    idb[:64, :64] = np.eye(64)[::-1]; idb[64:, :64] = np.eye(64)[::-1]
    c["identf"] = np.eye(128, dtype=np.float32)
    c["ident2"] = idb.astype(ml_dtypes.bfloat16)
    c["identb"] = np.eye(128, dtype=np.float32).astype(ml_dtypes.bfloat16)
    _CONSTS = c
    return c


CONST_SPECS = None


def _const_specs():
    c = _consts()
    return {k: (list(v.shape), BF16 if v.dtype == ml_dtypes.bfloat16 else F32) for k, v in c.items()}


W_SPECS = {
    "w_mod": [NL, D, 6 * D], "b_mod": [NL, 6 * D], "g_norm1": [NL, D], "w_in": [NL, D, IN_DIM],
    "hy_conv_w": [NL, 3, 768], "hy_conv_b": [NL, 768], "hy_f_w1": [NL, 17, 64], "hy_f_b1": [NL, 64],
    "hy_f_w2": [NL, 64, 64], "hy_f_b2": [NL, 64], "hy_f_w3": [NL, 64, 1024], "hy_f_b3": [NL, 1024],
    "hy_freq": [NL, 2, 64], "hy_bias": [NL, 2, 256], "na_g_q": [NL, 64], "na_g_k": [NL, 64],
    "na_rpb": [NL, RPB_PAD], "mla_g_qa": [NL, 256], "mla_g_kva": [NL, 128], "mla_w_q_up": [NL, 256, 768],
    "mla_w_kv_up": [NL, 128, 1024], "mla_g_q": [NL, 96], "mla_g_k": [NL, 96], "w_out": [NL, D, D],
    "g_norm2": [NL, D], "w_ff1": [NL, D, DFF], "b_ff1": [NL, DFF], "w_ff2": [NL, DFF, D], "b_ff2": [NL, D],
}


def build_nc(n_layers=NL, debug=False, stop=None):
    nc = bass.Bass("TRN2", target_bir_lowering=False)
    P = Prog(nc)
    di = {}
    di["x"] = nc.dram_tensor("x", [S, D], F32, kind="ExternalInput").ap()
    di["ctx"] = nc.dram_tensor("ctx", [LC, D], F32, kind="ExternalInput").ap()
    di["cvec"] = nc.dram_tensor("cvec", [2, D], F32, kind="ExternalInput").ap()
    for k, shp in W_SPECS.items():
        di[k] = nc.dram_tensor(k, shp, F32, kind="ExternalInput").ap()
    for k, (shp, dt) in _const_specs().items():
        di[k] = nc.dram_tensor(k, shp, dt, kind="ExternalInput").ap()
    out_d = nc.dram_tensor("out", [S, D], F32, kind="ExternalOutput").ap()
    skind = "ExternalOutput" if debug else "Internal"
    xT_d = nc.dram_tensor("xT_d", [D, T], F32, kind=skind).ap()
    pT_d = nc.dram_tensor("pT_d", [IN_DIM, T], F32, kind=skind).ap()
    vna_d = nc.dram_tensor("vna_d", [T, 4, 128], BF16, kind=skind).ap()
    mixT_d = nc.dram_tensor("mixT_d", [D, T], BF16, kind=skind).ap()
    dbg_d = nc.dram_tensor("dbg_d", [2, 96, T], BF16, kind=skind).ap(); B_dbg = Buf("dbg")
    B_x = Buf("xT_d"); B_p = Buf("pT_d"); B_vna = Buf("vna_d"); B_mix = Buf("mixT_d"); B_out = Buf("out")

    st = contextlib.ExitStack()
    with st:
        def sbt(name, shape, dt):
            return st.enter_context(nc.sbuf_tensor(name, shape, dt))

        XA = sbt("XA", [128, 18432], F32)
        HB = sbt("HB", [128, 9216], F32)
        MC = sbt("MC", [128, 9216], F32)
        WB = sbt("WB", [128, 8192], F32)
        SG = sbt("SG", [128, 2, 2312], F32)
        MI = sbt("MI", [128, 3072], F32)
        PS_ALL = st.enter_context(nc.psum_tensor("psall", [128, 4096], F32))
        ps_t = [PS_ALL[:, i * 512:(i + 1) * 512] for i in range(8)]
        PSB = [Buf("ps%d" % i) for i in range(8)]

        def view(ar, off, nw, dt=F32, pat=None, **kw):
            v = ar[:, off:off + nw]
            if dt == BF16:
                v = v.bitcast(BF16)
            if pat:
                v = v.rearrange(pat, **kw)
            return v

        mi_off = [0]

        def mi(nw, dt=F32, pat=None, **kw):
            o = mi_off[0]; mi_off[0] += nw
            assert mi_off[0] <= 3072
            return view(MI, o, nw, dt, pat, **kw)

        identf = mi(128); B_identf = Buf("identf")
        identb = mi(64, BF16); B_identb = Buf("identb")
        ident2 = mi(64, BF16); B_ident2 = Buf("ident2")
        ones_b = mi(64, BF16); B_ones = Buf("ones")
        blk64 = mi(64, BF16); B_blk64 = Buf("blk64")
        o96 = mi(64, BF16); B_o96 = Buf("o96")
        permT = mi(48, BF16); B_perm = Buf("permT")
        scT = mi(8, BF16, "p (c r) -> p c r", r=2); B_sc = Buf("scT")
        modTs = [mi(96, F32, "p (j r) -> p j r", r=2), mi(96, F32, "p (j r) -> p j r", r=2)]
        B_mods = [Buf("modT0"), Buf("modT1")]
        modT = modTs[0]; B_mod = B_mods[0]
        gsc1 = mi(16, F32, "p (c r) -> p c r", r=2); gsc2 = mi(16, F32, "p (c r) -> p c r", r=2)
        gb2 = mi(16, F32, "p (c r) -> p c r", r=2); B_mder = Buf("modder")
        g12 = mi(16, F32, "p (k c) -> p k c", c=8); B_g12 = Buf("g12")
        b2t = mi(8); B_b2 = Buf("b2")
        b1ts = [mi(32), mi(32)]; B_b1s = [Buf("b1a"), Buf("b1b")]
        hcw = mi(24, F32, "p (c j) -> p c j", j=4); B_hcw = Buf("hcw")
        hyb = mi(4, F32, "p (o c) -> p o c", c=2); B_hyb = Buf("hyb")
        nag = mi(2); B_nag = Buf("nag")
        mlg = mi(5); B_mlg = Buf("mlg")
        filp = mi(8); B_filp = Buf("filp")
        namask = mi(128, F32, "p (a k) -> p a k", k=64); B_namask = Buf("namask")
        eps_t = mi(1); B_eps = Buf("eps")
        cvt = mi(16, F32, "p (c r) -> p c r", r=2)
        alphaL = mi(48, F32, "p (a c) -> p a c", c=16); alphaC = mi(6, F32, "p (a c) -> p a c", c=2); B_alpha = Buf("alpha")
        rn_t = mi(512, F32, "p (o c) -> p o c", c=256); B_rn = Buf("rn")

        def ld(q, dst, src, wb, rb=()):
            P.dma(q, lambda e: e.dma_start(out=dst, in_=src), reads=rb, writes=[wb])

        ld("sp", identf, di["identf"], B_identf)
        ld("sp", identb, di["identb"], B_identb)
        ld("sp", ident2, di["ident2"], B_ident2)
        ld("sp", permT[0:96, 0:96], di["permT"], B_perm)
        ld("sp", namask, di["namask"], B_namask)
        ld("sp", alphaL, di["alphaL"], B_alpha)
        ld("sp", alphaC, di["alphaC"], B_alpha)
        P.op("pool", lambda e: e.memset(ones_b, 1.0), writes=[B_ones])
        P.op("pool", lambda e: e.memset(blk64, 0.0), writes=[B_blk64])
        P.op("pool", lambda e: e.memset(blk64[0:64, 0:64], 1.0 / 64), writes=[B_blk64])
        P.op("pool", lambda e: e.memset(blk64[64:128, 64:128], 1.0 / 64), writes=[B_blk64])
        P.op("pool", lambda e: e.memset(o96, 1.0 / 96), writes=[B_o96])
        P.op("pool", lambda e: e.memset(eps_t, EPS), writes=[B_eps])

        def rstd_from_ps(ps_ap, out_ap, psb, outb, scale, p0=0, p1=128):
            P.op("act", lambda e: e.activation(out=out_ap, in_=ps_ap, func=AF.Ln, bias=eps_t[p0:p1, 0:1], scale=scale),
                 reads=[psb, B_eps], writes=[outb])
            P.op("act", lambda e: e.activation(out=out_ap, in_=out_ap, func=AF.Exp, scale=-0.5), reads=[outb], writes=[outb])

        xT = view(XA, 0, 18432, F32, "p (c t) -> p c t", t=T); B_xT = Buf("xT")
        sg = [SG[:, 0, :], SG[:, 1, :]]; B_sg = [Buf("sg0"), Buf("sg1")]
        for ti in range(18):
            src = di["x"][ti * 128:(ti + 1) * 128, :] if ti < 16 else di["ctx"][(ti - 16) * 128:(ti - 15) * 128, :]
            s_ap = sg[ti % 2][:, 0:1024]; sb_ = B_sg[ti % 2]
            ld("sp", s_ap, src, sb_)
            for c in range(8):
                pi = c % 4 + (ti % 2) * 4
                P.op("pe", lambda e, c=c, s_ap=s_ap, pi=pi: e.transpose(out=ps_t[pi][:, 0:128], in_=s_ap[:, c * 128:(c + 1) * 128], identity=identf),
                     reads=[sb_, B_identf], writes=[PSB[pi]])
                eng = "act" if c % 2 == 0 else "dve"
                if eng == "act":
                    P.op("act", lambda e, c=c, pi=pi, ti=ti: e.copy(out=xT[:, c, ti * 128:(ti + 1) * 128], in_=ps_t[pi][:, 0:128]), reads=[PSB[pi]], writes=[B_xT])
                else:
                    P.op("dve", lambda e, c=c, pi=pi, ti=ti: e.tensor_copy(out=xT[:, c, ti * 128:(ti + 1) * 128], in_=ps_t[pi][:, 0:128]), reads=[PSB[pi]], writes=[B_xT])
        xTd_v = xT_d.rearrange("(c p) t -> p c t", p=128)
        P.dma("sp", lambda e: e.dma_start(out=xTd_v, in_=xT), reads=[B_xT], writes=[B_x])
        for r_ in range(2):
            P.dma("sp", lambda e, r_=r_: e.dma_start(out=cvt[:, :, r_], in_=di["cvec"][r_].rearrange("(c p) -> p c", p=128), allow_slow_non_contiguous=True), writes=[B_sc])
        P.op("act", lambda e: e.activation(out=scT, in_=cvt, func=AF.Silu), reads=[B_sc], writes=[B_sc])

        def small_loads(l):
            def nld(dst, src, wb):
                P.dma("sp", lambda e: e.dma_start(out=dst, in_=src, allow_slow_non_contiguous=True), writes=[wb])
            nld(g12[:, 0, :], di["g_norm1"][l].rearrange("(c p) -> p c", p=128), B_g12)
            nld(g12[:, 1, :], di["g_norm2"][l].rearrange("(c p) -> p c", p=128), B_g12)
            nld(b2t, di["b_ff2"][l].rearrange("(c p) -> p c", p=128), B_b2)
            nld(b1ts[l % 2], di["b_ff1"][l].rearrange("(c p) -> p c", p=128), B_b1s[l % 2])
            for j_ in range(3):
                nld(hcw[:, :, j_], di["hy_conv_w"][l, j_].rearrange("(c p) -> p c", p=128), B_hcw)
            nld(hcw[:, :, 3], di["hy_conv_b"][l].rearrange("(c p) -> p c", p=128), B_hcw)
            for o_ in range(2):
                nld(hyb[:, o_, :], di["hy_bias"][l, o_].rearrange("(c p) -> p c", p=128), B_hyb)
            for hh in range(2):
                nld(nag[hh * 64:(hh + 1) * 64, 0:1], di["na_g_q"][l].rearrange("(p o) -> p o", o=1), B_nag)
                nld(nag[hh * 64:(hh + 1) * 64, 1:2], di["na_g_k"][l].rearrange("(p o) -> p o", o=1), B_nag)
            nld(mlg[:, 0:2], di["mla_g_qa"][l].rearrange("(c p) -> p c", p=128), B_mlg)
            nld(mlg[:, 2:3], di["mla_g_kva"][l].rearrange("(p o) -> p o", o=1), B_mlg)
            nld(mlg[0:96, 3:4], di["mla_g_q"][l].rearrange("(p o) -> p o", o=1), B_mlg)
            nld(mlg[0:96, 4:5], di["mla_g_k"][l].rearrange("(p o) -> p o", o=1), B_mlg)
            nld(filp[0:64, 0:1], di["hy_f_b1"][l].rearrange("(p o) -> p o", o=1), B_filp)
            nld(filp[0:64, 1:2], di["hy_f_b2"][l].rearrange("(p o) -> p o", o=1), B_filp)
            nld(filp[0:64, 2:4], di["hy_freq"][l].rearrange("k p -> p k"), B_filp)
            P.op("dve", lambda e: e.tensor_tensor(out=filp[0:64, 4:6], in0=filp[0:64, 0:2], in1=filp[0:64, 2:4], op=ALU.mult), reads=[B_filp], writes=[B_filp])

        mw_buf = [view(MC, 3072, 2048, BF16, "p (k n) -> p k n", n=512), view(MC, 5120, 2048, BF16, "p (k n) -> p k n", n=512)]
        B_mw = [Buf("mw0"), Buf("mw1")]

        def mod_begin(l, mT, B_m):
            P.dma("sp", lambda e: e.dma_start(out=mT[:, :, 0], in_=di["b_mod"][l].rearrange("(j p) -> p j", p=128), allow_slow_non_contiguous=True), writes=[B_m])
            P.op("dve", lambda e: e.tensor_copy(out=mT[:, :, 1], in_=mT[:, :, 0]), reads=[B_m], writes=[B_m])
            for q in range(2):
                mod_dma(l, q)

        def mod_dma(l, q):
            wv = di["w_mod"][l].rearrange("(kc p) n -> p kc n", p=128)
            wt = mw_buf[q % 2]
            P.dma("pool", lambda e: e.dma_start(out=wt, in_=wv[:, :, q * 512:(q + 1) * 512]), writes=[B_mw[q % 2]])

        def mod_piece(l, q, pi, mT, B_m):
            wt = mw_buf[q % 2]; wbb = B_mw[q % 2]
            for jj in range(4):
                for kc in range(8):
                    P.op("pe", lambda e, jj=jj, kc=kc: e.matmul(ps_t[pi][:, jj * 2:jj * 2 + 2], lhsT=wt[:, kc, jj * 128:(jj + 1) * 128], rhs=scT[:, kc, :], start=(kc == 0), stop=(kc == 7)),
                         reads=[wbb, B_sc], writes=[PSB[pi]])
            j0 = q * 4
            P.op("dve", lambda e: e.tensor_tensor(out=mT[:, j0:j0 + 4, :], in0=mT[:, j0:j0 + 4, :], in1=ps_t[pi][:, 0:8].rearrange("p (j r) -> p j r", r=2), op=ALU.add),
                 reads=[PSB[pi], B_m], writes=[B_m])
            if q + 2 < 12:
                mod_dma(l, q + 2)

        def modulation_all(l, mT, B_m):
            P.retire([B_MC], B_mw)
            mod_begin(l, mT, B_m)
            for q in range(12):
                mod_piece(l, q, q % 2, mT, B_m)
            P.retire(B_mw, [B_MC])

        def mod_finish(mT, B_m):
            for k, gt, so in ((0, gsc1, 8), (1, gsc2, 32)):
                for r in range(2):
                    P.op("dve", lambda e, k=k, gt=gt, so=so, r=r: e.scalar_tensor_tensor(out=gt[:, :, r], in0=mT[:, so:so + 8, r], scalar=1.0, in1=g12[:, k, :], op0=ALU.add, op1=ALU.mult),
                         reads=[B_m, B_g12], writes=[B_mder])
            for r in range(2):
                P.op("dve", lambda e, r=r: e.tensor_tensor(out=gb2[:, :, r], in0=mT[:, 40:48, r], in1=b2t, op=ALU.mult), reads=[B_m, B_b2], writes=[B_mder])

        B_wb = [Buf("wb0"), Buf("wb1")]
        B_HB = Buf("HB"); B_MC = Buf("MC")

        def norm_adaln(which):
            hT = view(HB, 0, 9216, BF16, "p (c t) -> p c t", t=T)
            mT = modT; B_m = B_mod
            gt = gsc1 if which == 0 else gsc2
            sh0 = 0 if which == 0 else 24
            sq = [view(MC, 0, 2048, BF16, "p (c t) -> p c t", t=512), view(MC, 2048, 2048, BF16, "p (c t) -> p c t", t=512)]
            rs = [view(MC, 4096, 512), view(MC, 4608, 512)]
            B_sq = [Buf("sq0"), Buf("sq1")]; B_rs = [Buf("rs0"), Buf("rs1")]
            P.retire([B_MC], B_sq + B_rs + B_nt)
            def stage1(bi):
                t0, n = TB[bi]; k = bi % 2
                for c in range(8):
                    if c % 2 == 0:
                        P.op("act", lambda e, c=c: e.activation(out=sq[k][:, c, 0:n], in_=xT[:, c, t0:t0 + n], func=AF.Square), reads=[B_xT], writes=[B_sq[k]])
                    else:
                        P.op("dve", lambda e, c=c: e.tensor_tensor(out=sq[k][:, c, 0:n], in0=xT[:, c, t0:t0 + n], in1=xT[:, c, t0:t0 + n], op=ALU.mult), reads=[B_xT], writes=[B_sq[k]])
                for c in range(8):
                    P.op("pe", lambda e, c=c: e.matmul(ps_t[k][:, 0:n], lhsT=ones_b, rhs=sq[k][:, c, 0:n], start=(c == 0), stop=(c == 7)), reads=[B_sq[k], B_ones], writes=[PSB[k]])

            def stage2(bi):
                t0, n = TB[bi]; k = bi % 2
                rstd_from_ps(ps_t[k][:, 0:n], rs[k][:, 0:n], PSB[k], B_rs[k], 1.0 / D)
                r = 0 if t0 < S else 1
                for c in range(8):
                    kk = c % 2
                    P.op("dve", lambda e, c=c, kk=kk: e.tensor_tensor(out=nrm_tmp[kk][:, 0:n], in0=xT[:, c, t0:t0 + n], in1=rs[k][:, 0:n], op=ALU.mult),
                         reads=[B_xT, B_rs[k]], writes=[B_nt[kk]])
                    P.op("act", lambda e, c=c, kk=kk: e.activation(out=hT[:, c, t0:t0 + n], in_=nrm_tmp[kk][:, 0:n], func=AF.Identity, bias=mT[:, sh0 + c, r:r + 1], scale=gt[:, c, r:r + 1]),
                         reads=[B_nt[kk], B_m, B_mder], writes=[B_HB])

            stage1(0)
            for bi in range(len(TB)):
                if bi + 1 < len(TB):
                    stage1(bi + 1)
                stage2(bi)
            P.retire(B_sq + B_rs + B_nt, [B_MC])
            return hT

        nrm_tmp = [view(MC, 5120, 512), view(MC, 5632, 512)]; B_nt = [Buf("nt0"), Buf("nt1")]

        win = view(WB, 0, 7808, BF16, "p (k n) -> p k n", n=IN_DIM)
        B_win = Buf("win")

        def prefetch_win(l):
            wv = di["w_in"][l].rearrange("(kc p) n -> p kc n", p=128)
            P.retire(B_wb, [B_win])
            for (c0, c1) in ((0, 512), (512, 1024), (1024, 1536), (1536, IN_DIM)):
                P.dma("pool", lambda e, c0=c0, c1=c1: e.dma_start(out=win[:, :, c0:c1], in_=wv[:, :, c0:c1]), writes=[B_win])

        def in_proj(l, hT):
            stg = [SG[:, 0, :], SG[:, 1, :]]
            cvb = [view(MC, 0, 2312), view(MC, 2312, 2312)]; B_cv = [Buf("cv0"), Buf("cv1")]
            P.retire([B_MC], B_cv)
            pidx = 0
            for cc in range(16):
                m = 128 if cc < 15 else 32
                k = cc % 2
                sgt = stg[k]; sgb = B_sg[k]
                for bi, (t0, n) in enumerate(TB):
                    pi = pidx % 8; pidx += 1
                    for kc in range(8):
                        P.op("pe", lambda e, cc=cc, m=m, kc=kc, t0=t0, n=n, pi=pi: e.matmul(ps_t[pi][0:m, 0:n], lhsT=win[:, kc, cc * 128:cc * 128 + m], rhs=hT[:, kc, t0:t0 + n], start=(kc == 0), stop=(kc == 7)),
                             reads=[B_win, B_HB], writes=[PSB[pi]])
                    o0 = 1 + t0 if t0 < S else 3 + t0
                    if bi % 2 == 0:
                        P.op("act", lambda e, m=m, sgt=sgt, o0=o0, n=n, pi=pi: e.copy(out=sgt[0:m, o0:o0 + n], in_=ps_t[pi][0:m, 0:n]), reads=[PSB[pi]], writes=[sgb])
                    else:
                        P.op("dve", lambda e, m=m, sgt=sgt, o0=o0, n=n, pi=pi: e.tensor_copy(out=sgt[0:m, o0:o0 + n], in_=ps_t[pi][0:m, 0:n]), reads=[PSB[pi]], writes=[sgb])
                if cc < 6:
                    cv = cvb[k]; cb = B_cv[k]
                    for pz in (0, 2049, 2050, 2307):
                        P.op("pool", lambda e, sgt=sgt, pz=pz: e.memset(sgt[:, pz:pz + 1], 0.0), writes=[sgb])
                    P.op("dve", lambda e, cc=cc, sgt=sgt, cv=cv: e.tensor_scalar(out=cv[:, 1:2307], in0=sgt[:, 1:2307], scalar1=hcw[:, cc, 1:2], scalar2=hcw[:, cc, 3:4], op0=ALU.mult, op1=ALU.add),
                         reads=[sgb, B_hcw], writes=[cb])
                    P.op("dve", lambda e, cc=cc, sgt=sgt, cv=cv: e.scalar_tensor_tensor(out=cv[:, 1:2307], in0=sgt[:, 0:2306], scalar=hcw[:, cc, 0:1], in1=cv[:, 1:2307], op0=ALU.mult, op1=ALU.add),
                         reads=[sgb, B_hcw, cb], writes=[cb])
                    P.op("dve", lambda e, cc=cc, sgt=sgt, cv=cv: e.scalar_tensor_tensor(out=cv[:, 1:2307], in0=sgt[:, 2:2308], scalar=hcw[:, cc, 2:3], in1=cv[:, 1:2307], op0=ALU.mult, op1=ALU.add),
                         reads=[sgb, B_hcw, cb], writes=[cb])
                    src_t = cv; src_b = cb
                else:
                    src_t = sgt; src_b = sgb
                P.dma("sp", lambda e, cc=cc, m=m, src_t=src_t: e.dma_start(out=pT_d[cc * 128:cc * 128 + m, 0:S], in_=src_t[0:m, 1:1 + S]), reads=[src_b], writes=[B_p])
                P.dma("sp", lambda e, cc=cc, m=m, src_t=src_t: e.dma_start(out=pT_d[cc * 128:cc * 128 + m, S:T], in_=src_t[0:m, 2051:2051 + LC]), reads=[src_b], writes=[B_p])
            vst = [view(MC, 4624, 256, BF16, "p (h j) -> p h j", j=128), view(MC, 4880, 256, BF16, "p (h j) -> p h j", j=128)]
            B_vst = [Buf("vst0"), Buf("vst1")]
            P.retire([B_MC], B_vst)
            for k in range(2):
                P.op("pool", lambda e, k=k: e.memset(vst[k], 1.0), writes=[B_vst[k]])
            for ti in range(18):
                pi = pidx % 8; pidx += 1
                k = ti % 2
                for kc in range(8):
                    P.op("pe", lambda e, kc=kc, ti=ti, pi=pi: e.matmul(ps_t[pi][:, 0:256], lhsT=hT[:, kc, ti * 128:(ti + 1) * 128], rhs=win[:, kc, 1280:1536], start=(kc == 0), stop=(kc == 7)),
                         reads=[B_win, B_HB], writes=[PSB[pi]])
                for h in range(4):
                    o = 0 if h % 2 == 0 else 64
                    if h < 2:
                        P.op("act", lambda e, h=h, o=o, k=k, pi=pi: e.copy(out=vst[k][:, h, o:o + 64], in_=ps_t[pi][:, h * 64:(h + 1) * 64]), reads=[PSB[pi]], writes=[B_vst[k]])
                    else:
                        P.op("dve", lambda e, h=h, o=o, k=k, pi=pi: e.tensor_copy(out=vst[k][:, h, o:o + 64], in_=ps_t[pi][:, h * 64:(h + 1) * 64]), reads=[PSB[pi]], writes=[B_vst[k]])
                P.dma("sp", lambda e, ti=ti, k=k: e.dma_start(out=vna_d[ti * 128:(ti + 1) * 128, :, :], in_=vst[k]), reads=[B_vst[k]], writes=[B_vna])
            P.retire([B_win], B_wb)
            P.retire(B_cv + B_vst, [B_MC])

        wo = view(HB, 4096, 4096, BF16, "p (k n) -> p k n", n=D)
        B_wo = Buf("wo")

        def prefetch_wo(l):
            wv = di["w_out"][l].rearrange("(kc p) n -> p kc n", p=128)
            P.retire([B_HB], [B_wo])
            for kh in range(2):
                P.dma("pool", lambda e, kh=kh: e.dma_start(out=wo[:, kh * 4:(kh + 1) * 4, :], in_=wv[:, kh * 4:(kh + 1) * 4, :]), writes=[B_wo])

        def out_proj_residual(l):
            mT = modT; B_m = B_mod
            mixT = view(MC, 0, 9216, BF16, "p (c t) -> p c t", t=T)
            P.dma("sp", lambda e: e.dma_start(out=mixT, in_=mixT_d.rearrange("(c p) t -> p c t", p=128)), reads=[B_mix], writes=[B_MC])
            pidx = 0
            for dc in range(8):
                for (t0, n) in TB:
                    pi = pidx % 8; pidx += 1
                    r = 0 if t0 < S else 1
                    for kc in range(8):
                        P.op("pe", lambda e, dc=dc, kc=kc, t0=t0, n=n, pi=pi: e.matmul(ps_t[pi][:, 0:n], lhsT=wo[:, kc, dc * 128:(dc + 1) * 128], rhs=mixT[:, kc, t0:t0 + n], start=(kc == 0), stop=(kc == 7)),
                             reads=[B_wo, B_MC], writes=[PSB[pi]])
                    P.op("dve", lambda e, dc=dc, t0=t0, n=n, pi=pi, r=r: e.scalar_tensor_tensor(out=xT[:, dc, t0:t0 + n], in0=ps_t[pi][:, 0:n], scalar=mT[:, 16 + dc, r:r + 1], in1=xT[:, dc, t0:t0 + n], op0=ALU.mult, op1=ALU.add),
                         reads=[PSB[pi], B_m, B_xT], writes=[B_xT])

        w1b = [view(WB, 0, 2048, BF16, "p (k n) -> p k n", n=512), view(WB, 2048, 2048, BF16, "p (k n) -> p k n", n=512)]
        w2b = [view(WB, 4096, 2048, BF16, "p (k n) -> p k n", n=D), view(WB, 6144, 2048, BF16, "p (k n) -> p k n", n=D)]
        B_w1 = [Buf("w1a"), Buf("w1b")]; B_w2 = [Buf("w2a"), Buf("w2b")]

        def ffn_wdma(l, j):
            w1v = di["w_ff1"][l].rearrange("(kc p) n -> p kc n", p=128)
            w2v = di["w_ff2"][l].rearrange("(hc p) n -> p hc n", p=128)
            k = j % 2
            P.dma("pool", lambda e: e.dma_start(out=w1b[k], in_=w1v[:, :, j * 512:(j + 1) * 512]), writes=[B_w1[k]])
            P.dma("pool", lambda e: e.dma_start(out=w2b[k], in_=w2v[:, j * 4:(j + 1) * 4, :]), writes=[B_w2[k]])

        def prefetch_ffn0(l):
            P.retire(B_wb, B_w1 + B_w2)
            ffn_wdma(l, 0)

        def ffn(l, hT, nxt=None):
            mT = modT; B_m = B_mod
            b1l = b1ts[l % 2]; B_b1l = B_b1s[l % 2]
            aT = [view(MC, 0, 1024, BF16, "p (c t) -> p c t", t=512), view(MC, 1024, 1024, BF16, "p (c t) -> p c t", t=512)]
            rl = [view(MC, 2048, 512), view(MC, 2560, 512)]
            B_a = [Buf("aT0"), Buf("aT1")]; B_rl = [Buf("rl0"), Buf("rl1")]
            P.retire([B_MC], B_a + B_rl)
            pidx = [0]
            items = [(j, t0, n) for j in range(8) for (t0, n) in TB]

            def ff1(idx):
                j, t0, n = items[idx]
                k = j % 2; a = idx % 2
                if t0 == 0 and j > 0:
                    ffn_wdma(l, j)
                for hc in range(4):
                    pi = pidx[0] % 8; pidx[0] += 1
                    for kc in range(8):
                        P.op("pe", lambda e, k=k, hc=hc, kc=kc, t0=t0, n=n, pi=pi: e.matmul(ps_t[pi][:, 0:n], lhsT=w1b[k][:, kc, hc * 128:(hc + 1) * 128], rhs=hT[:, kc, t0:t0 + n], start=(kc == 0), stop=(kc == 7)),
                             reads=[B_w1[k], B_HB], writes=[PSB[pi]])
                    rr = hc % 2
                    P.op("dve", lambda e, j=j, hc=hc, n=n, pi=pi, rr=rr: e.tensor_scalar(out=rl[rr][:, 0:n], in0=ps_t[pi][:, 0:n], scalar1=b1l[:, j * 4 + hc:j * 4 + hc + 1], scalar2=0.0, op0=ALU.add, op1=ALU.max),
                         reads=[PSB[pi], B_b1l], writes=[B_rl[rr]])
                    P.op("act", lambda e, a=a, hc=hc, n=n, rr=rr: e.activation(out=aT[a][:, hc, 0:n], in_=rl[rr][:, 0:n], func=AF.Square),
                         reads=[B_rl[rr]], writes=[B_a[a]])

            def ff2(idx):
                j, t0, n = items[idx]
                k = j % 2; a = idx % 2
                r = 0 if t0 < S else 1
                for dc in range(8):
                    pi = pidx[0] % 8; pidx[0] += 1
                    for hc in range(4):
                        P.op("pe", lambda e, k=k, a=a, dc=dc, hc=hc, n=n, pi=pi: e.matmul(ps_t[pi][:, 0:n], lhsT=w2b[k][:, hc, dc * 128:(dc + 1) * 128], rhs=aT[a][:, hc, 0:n], start=(hc == 0), stop=(hc == 3)),
                             reads=[B_w2[k], B_a[a]], writes=[PSB[pi]])
                    P.op("dve", lambda e, dc=dc, t0=t0, n=n, pi=pi, r=r: e.scalar_tensor_tensor(out=xT[:, dc, t0:t0 + n], in0=ps_t[pi][:, 0:n], scalar=mT[:, 40 + dc, r:r + 1], in1=xT[:, dc, t0:t0 + n], op0=ALU.mult, op1=ALU.add),
                         reads=[PSB[pi], B_m, B_xT], writes=[B_xT])
                    if j == 0:
                        P.op("dve", lambda e, dc=dc, t0=t0, n=n, r=r: e.tensor_scalar(out=xT[:, dc, t0:t0 + n], in0=xT[:, dc, t0:t0 + n], scalar1=gb2[:, dc, r:r + 1], scalar2=None, op0=ALU.add),
                             reads=[B_xT, B_mder], writes=[B_xT])

            if nxt is not None:
                P.retire([B_MC], B_mw)
                mod_begin(nxt[0], nxt[1], nxt[2])
            qn_ = [0]
            ff1(0)
            for idx in range(len(items)):
                if idx + 1 < len(items):
                    ff1(idx + 1)
                ff2(idx)
                if nxt is not None and items[idx][1] == S and qn_[0] < 12:
                    for _ in range(2):
                        pi = pidx[0] % 8; pidx[0] += 1
                        mod_piece(nxt[0], qn_[0], pi, nxt[1], nxt[2]); qn_[0] += 1
            if nxt is not None:
                assert qn_[0] == 12
                P.retire(B_mw, [B_MC])
            P.retire(B_w1 + B_w2, B_wb)
            P.retire(B_a + B_rl, [B_MC])

        WBALL = B_wb

        def MM(out, lhsT, rhs, start, stop, reads, wbuf):
            P.op("pe", lambda e: e.matmul(out, lhsT=lhsT, rhs=rhs, start=start, stop=stop), reads=reads, writes=[wbuf])

        def MMH(out, lhsT, rhs, start, stop, reads, wbuf):
            P.op("pe", lambda e: e.matmul(out, lhsT=lhsT, rhs=rhs, start=start, stop=stop), reads=reads, writes=[wbuf])

        def MMN(out, lhsT, rhs, start, stop, reads, wbuf):
            P.op("pe", lambda e: e.matmul(out, lhsT=lhsT, rhs=rhs, start=start, stop=stop), reads=reads, writes=[wbuf])

        def MMF(out, lhsT, rhs, start, stop, reads, wbuf):
            P.op("pe", lambda e: e.matmul(out, lhsT=lhsT, rhs=rhs, start=start, stop=stop), reads=reads, writes=[wbuf])

        def MMI(out, lhsT, rhs, start, stop, reads, wbuf):
            P.op("pe", lambda e: e.matmul(out, lhsT=lhsT, rhs=rhs, start=start, stop=stop), reads=reads, writes=[wbuf])

        def MMS(out, lhsT, rhs, start, stop, reads, wbuf):
            P.op("pe", lambda e: e.matmul(out, lhsT=lhsT, rhs=rhs, start=start, stop=stop), reads=reads, writes=[wbuf])

        def TT(eng, out, in0, in1, op, reads, wbuf):
            P.op(eng, lambda e: e.tensor_tensor(out=out, in0=in0, in1=in1, op=op), reads=reads, writes=[wbuf])

        def STT(eng, out, in0, scalar, in1, op0, op1, reads, wbuf):
            P.op(eng, lambda e: e.scalar_tensor_tensor(out=out, in0=in0, scalar=scalar, in1=in1, op0=op0, op1=op1), reads=reads, writes=[wbuf])

        def TS(eng, out, in0, s1, s2, op0, op1, reads, wbuf):
            if s2 is None:
                P.op(eng, lambda e: e.tensor_scalar(out=out, in0=in0, scalar1=s1, scalar2=None, op0=op0), reads=reads, writes=[wbuf])
            else:
                P.op(eng, lambda e: e.tensor_scalar(out=out, in0=in0, scalar1=s1, scalar2=s2, op0=op0, op1=op1), reads=reads, writes=[wbuf])

        def ACTF(out, in_, func, reads, wbuf, **kw):
            P.op("act", lambda e: e.activation(out=out, in_=in_, func=func, **kw), reads=reads, writes=[wbuf])

        def CP(eng, out, in_, reads, wbuf):
            if eng == "act":
                P.op("act", lambda e: e.copy(out=out, in_=in_), reads=reads, writes=[wbuf])
            else:
                P.op(eng, lambda e: e.tensor_copy(out=out, in_=in_), reads=reads, writes=[wbuf])

        def sin_p1(arg_ps, psb, f_ap, fb_ap, tA, tB, tC, Bt, n):
            a = tA[0:64, 0:n]; b_ = tB[0:64, 0:n]; c_ = tC[0:64, 0:n]
            TS("dve", c_, arg_ps, f_ap, fb_ap, ALU.mult, ALU.add, [psb, B_filp], Bt)
            ACTF(a, c_, AF.Sin, [Bt], Bt, scale=0.125)
            ACTF(b_, c_, AF.Sin, [Bt], Bt, scale=0.25)

        def sin_p2(out_ap, outb, tA, tB, tC, Bt, n):
            a = tA[0:64, 0:n]; b_ = tB[0:64, 0:n]
            TT("dve", a, a, a, ALU.mult, [Bt], Bt)
            TS("dve", a, a, -2.0, 1.0, ALU.mult, ALU.add, [Bt], Bt)
            TT("dve", a, b_, a, ALU.mult, [Bt], Bt)
            TT("dve", b_, b_, b_, ALU.mult, [Bt], Bt)
            TS("dve", b_, b_, -2.0, 1.0, ALU.mult, ALU.add, [Bt], Bt)
            STT("dve", out_ap, a, 4.0, b_, ALU.mult, ALU.mult, [Bt], outb)

        def hyena(l, L, last):
            lat = (L == S)
            ntc = L // 128; GJ = min(4, ntc); G = ntc // GJ
            tok0 = 0 if lat else S
            dft = di["dftL"] if lat else di["dftC"]
            alpha = alphaL if lat else alphaC
            zT_d = di["zTL"] if lat else di["zTC"]
            dec_d = di["decL"] if lat else di["decC"]
            stw = GJ * ntc * 128
            STf = [view(XA, k * 8192, stw, BF16) for k in range(2)] if lat else [view(MC, 3072 + k * 512, stw, BF16) for k in range(2)]
            STv = [v.rearrange("p (m c j f) -> p m c j f", j=GJ, m=2, c=ntc) for v in STf]
            B_st = [Buf("st0"), Buf("st1")]
            KT = view(HB, 0, ntc * 512, BF16, "p (c m o f) -> p c m o f", c=ntc, m=2, o=2); B_kt = Buf("KT")
            AB = view(MC, 0, ntc * 512, BF16, "p (c m o f) -> p c m o f", c=ntc, m=2, o=2); B_ab = Buf("AB")
            P.retire([B_xT] if lat else [B_MC], B_st); P.retire([B_HB], [B_kt]); P.retire([B_MC], [B_ab])
            sti = [0]

            def load_st(g):
                k = sti[0] % 2; sti[0] += 1
                P.dma("sp", lambda e: e.dma_start(out=STf[k], in_=dft[g]), writes=[B_st[k]])
                return STv[k], B_st[k]

            hdn1 = view(WB, 0, 2049)
            hdn2a = view(WB, 2050, 1025, BF16)
            w3a = view(WB, 3076, 512, BF16)
            zbs = [view(WB, 3588, 512), view(WB, 4100, 512)]
            w1t = view(WB, 4612, 64); w2t = view(WB, 4676, 64)
            hd = view(WB, 4740, 1024); habs = view(WB, 5764, 512, BF16); hb1 = view(WB, 6276, 512)
            dect = [view(WB, 6788, 512, F32, "p (a f) -> p a f", a=2), view(WB, 7300, 512, F32, "p (a f) -> p a f", a=2)]
            B_f = Buf("fw"); B_h1 = Buf("hdn1"); B_h2 = Buf("hdn2"); B_zb = [Buf("zb0"), Buf("zb1")]
            B_hd = Buf("hd"); B_habs = Buf("habs"); B_hb1 = Buf("hb1"); B_dec = [Buf("dec0"), Buf("dec1")]; B_tmp = Buf("sintmp")
            P.retire(WBALL, [B_f, B_h1, B_h2, B_hd, B_habs, B_hb1, B_tmp] + B_zb + B_dec)
            ld("sp", w1t[0:17, 0:64], di["hy_f_w1"][l], B_f)
            ld("sp", w2t[0:64, 0:64], di["hy_f_w2"][l], B_f)
            P.dma("pool", lambda e: e.dma_start(out=w3a[0:64, :], in_=di["hy_f_w3"][l]), writes=[B_f])
            P.dma("pool", lambda e: e.dma_start(out=w3a[64:65, :], in_=di["hy_f_b3"][l].rearrange("(o n) -> o n", o=1)), writes=[B_f])
            P.op("pool", lambda e: e.memset(hdn2a[64:65, 0:L + 1], 1.0), writes=[B_h2])
            P.op("pool", lambda e: e.memset(hdn2a[0:64, L:L + 1], 0.0), writes=[B_h2])
            bn = min(512, L); nb = L // bn
            B_tmp2 = Buf("sintmp2"); P.retire(B_sg, [B_tmp2])
            tsets = [(hd[:, 0:512], hd[:, 512:1024], hb1, B_tmp), (SG[:, 0, 0:512], SG[:, 0, 512:1024], SG[:, 0, 1024:1536], B_tmp2)]
            B_h1b = [Buf("hdn1_%d" % i_) for i_ in range(nb)]
            P.retire([B_h1], B_h1b)
            for pr in range(0, nb, 2):
                blks = [pr] + ([pr + 1] if pr + 1 < nb else [])
                for i_, blk in enumerate(blks):
                    zb = zbs[blk % 2]; zbb = B_zb[blk % 2]
                    ld("sp", zb[0:17, 0:bn], zT_d[:, blk * bn:(blk + 1) * bn], zbb)
                    MMH(ps_t[i_][0:64, 0:bn], w1t[0:17, 0:64], zb[0:17, 0:bn], True, True, [B_f, zbb], PSB[i_])
                    sin_p1(ps_t[i_][0:64, 0:bn], PSB[i_], filp[0:64, 2:3], filp[0:64, 4:5], *tsets[i_], bn)
                for i_, blk in enumerate(blks):
                    sin_p2(hdn1[0:64, blk * bn:(blk + 1) * bn], B_h1b[blk], *tsets[i_], bn)
                for i_, blk in enumerate(blks):
                    MMH(ps_t[2 + i_][0:64, 0:bn], w2t[0:64, 0:64], hdn1[0:64, blk * bn:(blk + 1) * bn], True, True, [B_f, B_h1b[blk]], PSB[2 + i_])
                    sin_p1(ps_t[2 + i_][0:64, 0:bn], PSB[2 + i_], filp[0:64, 3:4], filp[0:64, 5:6], *tsets[i_], bn)
                for i_, blk in enumerate(blks):
                    sin_p2(hdn2a[0:64, blk * bn:(blk + 1) * bn], B_h2, *tsets[i_], bn)
            P.retire(B_h1b, [B_h1]); P.retire([B_tmp2], B_sg)
            if HSTOP == 1:
                return
            P.retire([B_tmp], [B_hd, B_hb1])
            hdS = [hd, SG[:, 0, 0:1024]]; hb1S = [hb1, SG[:, 0, 1024:1536]]; habsS = [habs, SG[:, 1, 0:512].bitcast(BF16)]
            B_hdS = [B_hd, Buf("hd2")]; B_hb1S = [B_hb1, Buf("hb12")]; B_habsS = [B_habs, Buf("habs2")]
            P.retire(B_sg, [B_hdS[1], B_hb1S[1], B_habsS[1]])
            banksets = [(2, 3, 4), (0, 1, 7)]
            pend = []

            def flush_pend():
                for (tcp, kp) in pend:
                    MMH(ps_t[5][:, 0:512], ones_b, habsS[kp][:, 0:512], tcp == 0, tcp == ntc - 1, [B_ones, B_habsS[kp]], PSB[5])
                    MMH(ps_t[6][:, 0:512], ones_b, habsS[kp][:, 512:1024], tcp == 0, tcp == ntc - 1, [B_ones, B_habsS[kp]], PSB[6])
                del pend[:]

            for tc in range(ntc):
                k = tc % 2
                b0, b1_, b2_ = banksets[k]
                hd_ = hdS[k]; hb_1 = hb1S[k]; habs_ = habsS[k]
                ld("sp", dect[k], dec_d[tc * 128:(tc + 1) * 128], B_dec[k])
                lt = hdn2a[0:65, tc * 128:(tc + 1) * 128]
                MMH(ps_t[b0][:, 0:512], lt, w3a[0:65, 0:512], True, True, [B_h2, B_f], PSB[b0])
                MMH(ps_t[b1_][:, 0:512], lt, w3a[0:65, 512:1024], True, True, [B_h2, B_f], PSB[b1_])
                lts = hdn2a[0:65, tc * 128 + 1:(tc + 1) * 128 + 1]
                for o in range(2):
                    MMH(ps_t[b2_][:, o * 256:(o + 1) * 256], lts, w3a[0:65, o * 512 + 256:(o + 1) * 512], True, True, [B_h2, B_f], PSB[b2_])
                flush_pend()
                d0 = dect[k][:, 0:1, :].to_broadcast([128, 2, 256]); d1 = dect[k][:, 1:2, :].to_broadcast([128, 2, 256])
                for hb_, bb in enumerate((b0, b1_)):
                    TT("dve", hd_[:, hb_ * 512:(hb_ + 1) * 512].rearrange("p (g f) -> p g f", f=256), ps_t[bb][:, 0:512].rearrange("p (g f) -> p g f", f=256), d0, ALU.mult,
                       [PSB[bb], B_dec[k]], B_hdS[k])
                TT("dve", hb_1.rearrange("p (g f) -> p g f", f=256), ps_t[b2_][:, 0:512].rearrange("p (g f) -> p g f", f=256), d1, ALU.mult, [PSB[b2_], B_dec[k]], B_hb1S[k])
                ACTF(habs_, hd_, AF.Abs, [B_hdS[k]], B_habsS[k])
                pend.append((tc, k))
                for o in range(2):
                    TT("pool", AB[:, tc, 0, o, :], hd_[:, o * 512:o * 512 + 256], hb_1[:, o * 256:(o + 1) * 256], ALU.add, [B_hdS[k], B_hb1S[k]], B_ab)
                    TT("dve", AB[:, tc, 1, o, :], hb_1[:, o * 256:(o + 1) * 256], hd_[:, o * 512:o * 512 + 256], ALU.subtract, [B_hdS[k], B_hb1S[k]], B_ab)
            flush_pend()
            P.retire([B_hdS[1], B_hb1S[1], B_habsS[1]], B_sg)
            for o in range(2):
                CP("dve", rn_t[:, o, :], ps_t[5 + o][:, 256:512], [PSB[5 + o]], B_rn)
                TT("dve", rn_t[:, o, :], rn_t[:, o, :], ps_t[5 + o][:, 0:256], ALU.add, [PSB[5 + o], B_rn], B_rn)
            rnf = rn_t.rearrange("p o c -> p (o c)")
            TS("dve", rnf, rnf, EPS, float(L), ALU.add, ALU.mult, [B_rn], B_rn)
            P.op("dve", lambda e: e.reciprocal(out=rnf, in_=rnf), reads=[B_rn], writes=[B_rn])
            if HSTOP == 2:
                return
            kt1 = [SG[:, 0, 0:512], SG[:, 0, 512:1024]]; B_k1 = [Buf("kt1a"), Buf("kt1b")]
            P.retire(B_sg, B_k1)
            for g in range(G):
                stv, stb = load_st(g)
                for j in range(GJ):
                    fc = g * GJ + j
                    br, bi_ = (0, 1) if fc % 2 == 0 else (2, 3)
                    for m, bk in ((0, br), (1, bi_)):
                        for tc in range(ntc):
                            MMS(ps_t[bk][:, 0:512], stv[:, m, tc, j, :], AB[:, tc, m, :, :].rearrange("p o f -> p (o f)"), tc == 0, tc == ntc - 1, [stb, B_ab], PSB[bk])
                    for mo, (a1, a2) in enumerate(((0, 2), (1, 0))):
                        kk = mo
                        TS("dve", kt1[kk], ps_t[br][:, 0:512], alpha[:, a1, fc:fc + 1], None, ALU.mult, None, [PSB[br], B_alpha], B_k1[kk])
                        STT("dve", kt1[kk], ps_t[bi_][:, 0:512], alpha[:, a2, fc:fc + 1], kt1[kk], ALU.mult, ALU.add, [PSB[bi_], B_alpha, B_k1[kk]], B_k1[kk])
                        TT("pool", KT[:, fc, mo, :, :].rearrange("p o f -> p (o f)"), kt1[kk], rnf, ALU.mult, [B_k1[kk], B_rn], B_kt)
            if HSTOP == 3:
                return
            P.retire(B_k1, B_sg)
            P.retire([B_f, B_h1, B_h2, B_hd, B_habs, B_hb1] + B_zb + B_dec, WBALL)
            u = view(WB, 0, 2 * L, F32, "p (c t) -> p c t", c=2); gate = view(WB, 4096, 2 * L, F32, "p (c t) -> p c t", c=2)
            B_u = Buf("u"); B_g = Buf("gate")
            P.retire(WBALL, [B_u, B_g])
            utm = view(MC, 0, ntc * 128, BF16, "p (c f) -> p c f", c=ntc); B_utm = Buf("utm")
            Z = view(MC, 2048, ntc * 256, BF16, "p (c m f) -> p c m f", c=ntc, m=2); B_z = Buf("Z")
            ubf = view(MC, 6144, L, BF16, "p (c t) -> p c t", c=2); B_ubf = Buf("ubf")
            P.retire([B_ab], [B_utm, B_z, B_ubf])
            T4 = [SG[:, 0, 0:1024].rearrange("p (a f) -> p a f", a=4), SG[:, 0, 1024:2048].rearrange("p (a f) -> p a f", a=4)]
            B_t4 = [Buf("t4a"), Buf("t4b")]
            etmp = [SG[:, 1, 0:512], SG[:, 1, 512:1024]]; B_et = [Buf("eta"), Buf("etb")]
            ohs = [SG[:, 1, 1024:1280].bitcast(BF16), SG[:, 1, 1280:1536].bitcast(BF16)]; B_oh = [Buf("oha"), Buf("ohb")]
            P.retire(B_sg, B_t4 + B_et + B_oh)
            P.dma("sp", lambda e: e.dma_start(out=u, in_=pT_d[0:256, tok0:tok0 + L].rearrange("(c p) t -> p c t", p=128)), reads=[B_p], writes=[B_u])
            cnt = [0]
            for o in range(2):
                P.dma("sp", lambda e, o=o: e.dma_start(out=gate, in_=pT_d[256 * (o + 1):256 * (o + 2), tok0:tok0 + L].rearrange("(c p) t -> p c t", p=128)), reads=[B_p], writes=[B_g])
                for cc in range(2):
                    CP("act" if cc == 0 else "dve", ubf[:, cc, :], u[:, cc, :], [B_u], B_ubf)
                for tc in range(ntc):
                    for cc in range(2):
                        bk = 6 + (cnt[0] % 2); cnt[0] += 1
                        pb16 = ps_t[bk][:, 0:64].bitcast(BF16)
                        P.op("pe", lambda e, pb16=pb16, tc=tc, cc=cc: e.transpose(out=pb16, in_=ubf[:, cc, tc * 128:(tc + 1) * 128], identity=identb), reads=[B_ubf, B_identb], writes=[PSB[bk]])
                        CP("act" if cc == 0 else "dve", utm[:, tc, cc * 128:(cc + 1) * 128], pb16, [PSB[bk]], B_utm)
                if HSTOP == 4:
                    return
                for g in range(G):
                    stv, stb = load_st(g)
                    for j in range(GJ):
                        fc = g * GJ + j
                        bk = 4 + fc % 2
                        for m in range(2):
                            for tc in range(ntc):
                                MMF(ps_t[bk][:, m * 256:(m + 1) * 256], stv[:, m, tc, j, :], utm[:, tc, :], tc == 0, tc == ntc - 1, [stb, B_utm], PSB[bk])
                        Pp = ps_t[bk][:, 0:256]; Qq = ps_t[bk][:, 256:512]
                        Kr = KT[:, fc, 0, o, :]; Ki = KT[:, fc, 1, o, :]
                        t4 = T4[fc % 2]; tb_ = B_t4[fc % 2]
                        TT("dve", t4[:, 0, :], Pp, Kr, ALU.mult, [PSB[bk], B_kt], tb_)
                        TT("dve", t4[:, 1, :], Qq, Ki, ALU.mult, [PSB[bk], B_kt], tb_)
                        TT("dve", t4[:, 2, :], Qq, Kr, ALU.mult, [PSB[bk], B_kt], tb_)
                        TT("dve", t4[:, 3, :], Pp, Ki, ALU.mult, [PSB[bk], B_kt], tb_)
                        TT("pool", Z[:, fc, 0, :], t4[:, 0, :], t4[:, 1, :], ALU.add, [tb_], B_z)
                        TT("pool", Z[:, fc, 1, :], t4[:, 2, :], t4[:, 3, :], ALU.subtract, [tb_], B_z)
                if HSTOP == 5:
                    return
                wdt = GJ * 128
                for g in range(G):
                    stv, stb = load_st(g)
                    for cc in range(2):
                        bk = (g * 2 + cc) % 4
                        ops = ps_t[bk][:, 0:wdt]
                        for fc in range(ntc):
                            for m in range(2):
                                MMI(ops, Z[:, fc, m, cc * 128:(cc + 1) * 128], stv[:, m, fc, :, :].rearrange("p j f -> p (j f)"), fc == 0 and m == 0, fc == ntc - 1 and m == 1, [stb, B_z], PSB[bk])
                        t0 = g * wdt
                        kk = (g * 2 + cc) % 2
                        et = etmp[kk][:, 0:wdt]
                        STT("dve", et, u[:, cc, t0:t0 + wdt], hyb[:, o, cc:cc + 1], ps_t[bk][:, 0:wdt], ALU.mult, ALU.add, [B_u, B_hyb, PSB[bk]], B_et[kk])
                        if o == 0:
                            TT("pool", u[:, cc, t0:t0 + wdt], et, gate[:, cc, t0:t0 + wdt], ALU.mult, [B_et[kk], B_g], B_u)
                        else:
                            TT("pool", ubf[:, cc, t0:t0 + wdt], et, gate[:, cc, t0:t0 + wdt], ALU.mult, [B_et[kk], B_g], B_ubf)
            P.dma("sp", lambda e: e.dma_start(out=mixT_d[0:256, tok0:tok0 + L].rearrange("(c p) t -> p c t", p=128), in_=ubf), reads=[B_ubf], writes=[B_mix])
            P.retire(B_t4 + B_et + B_oh, B_sg)
            P.retire([B_u, B_g], WBALL)
            P.retire([B_utm, B_z, B_ubf], [B_MC])
            P.retire(B_st, [B_xT] if lat else [B_MC]); P.retire([B_kt], [B_HB])

        def rms_feat(raw, rawb, c_out, outb, lhs_ones, m, scale, gain_ap, sqv, B_sqv, rsv, B_rsv, psbanks, nchunks=1, raws=None, outs=None, gains=None):
            raws = raws or [raw]; outs = outs or [c_out]; gains = gains or [gain_ap]

            def st1(bi):
                t0, n = TB[bi]; k = bi % 2
                for ci, rw in enumerate(raws):
                    ACTF(sqv[k][0:m, ci, 0:n], rw[0:m, t0:t0 + n], AF.Square, [rawb], B_sqv[k])
                bk = psbanks[k]
                for ci in range(len(raws)):
                    MM(ps_t[bk][0:m, 0:n], lhs_ones, sqv[k][0:m, ci, 0:n], ci == 0, ci == len(raws) - 1, [B_sqv[k], B_ones, B_blk64, B_o96], PSB[bk])

            def st2(bi):
                t0, n = TB[bi]; k = bi % 2
                bk = psbanks[k]
                rstd_from_ps(ps_t[bk][0:m, 0:n], rsv[k][0:m, 0:n], PSB[bk], B_rsv[k], scale, 0, m)
                for ci, rw in enumerate(raws):
                    STT("dve", outs[ci][0:m, t0:t0 + n], rw[0:m, t0:t0 + n], gains[ci], rsv[k][0:m, 0:n], ALU.mult, ALU.mult, [rawb, B_rsv[k], B_nag, B_mlg], outb)

            st1(0)
            for bi in range(len(TB)):
                if bi + 1 < len(TB):
                    st1(bi + 1)
                st2(bi)

        def attn_core(keys, kT_of, q_ap, n, v_of, scale, sbanks, obank, PTs, B_PT, ctr, kreads, vreads, look=2, tick=None):
            nk = len(keys); slots = []

            def s_part(i):
                sb_ = sbanks[ctr[0] % len(sbanks)]; pk = ctr[0] % len(PTs); ctr[0] += 1
                MM(ps_t[sb_][:, 0:n], kT_of(keys[i]), q_ap, True, True, kreads, PSB[sb_])
                ACTF(PTs[pk][:, 0:n], ps_t[sb_][:, 0:n], AF.Exp, [PSB[sb_]], B_PT[pk], scale=scale)
                slots.append(pk)

            def v_part(i):
                pk = slots[i]
                MM(ps_t[obank][:, 0:n], v_of(keys[i]), PTs[pk][:, 0:n], i == 0, i == nk - 1, vreads + [B_PT[pk]], PSB[obank])

            for i in range(min(look, nk)):
                s_part(i)
            for i in range(nk):
                if i + look < nk:
                    s_part(i + look)
                v_part(i)
                if tick is not None:
                    tick()

        def attn_core_pair(keys, kT_of, q_ap, v_of, scale, obank, PTs, B_PT, ctr, kreads, vreads, tick=None):
            n = 512
            npair = len(keys) // 2
            assert len(keys) % 2 == 0
            slots = []

            def s_part(pi_):
                pr = ctr[0] % 2; ctr[0] += 1
                b0 = 2 * pr
                for u_ in range(2):
                    MM(ps_t[b0 + u_][:, 0:n], kT_of(keys[2 * pi_ + u_]), q_ap, True, True, kreads, PSB[b0 + u_])
                pt = view(XA, PTs[0], 512, BF16) if pr == 0 else view(XA, PTs[1], 512, BF16)
                P.op("act", lambda e: e.activation(out=pt, in_=PS_ALL[:, b0 * 512:(b0 + 2) * 512], func=AF.Exp, scale=scale), reads=[PSB[b0], PSB[b0 + 1]], writes=[B_PT[2 * pr], B_PT[2 * pr + 1]])
                slots.append((pt, pr))

            def v_part(pi_):
                pt, pr = slots[pi_]
                for u_ in range(2):
                    i = 2 * pi_ + u_
                    MM(ps_t[obank][:, 0:n], v_of(keys[i]), pt[:, u_ * 512:(u_ + 1) * 512], i == 0, i == len(keys) - 1, vreads + [B_PT[2 * pr], B_PT[2 * pr + 1]], PSB[obank])
                    if tick is not None:
                        tick()

            s_part(0)
            for pi_ in range(npair):
                if pi_ + 1 < npair:
                    s_part(pi_ + 1)
                v_part(pi_)

        def normalize(obank, n, odd, out_ap, outb, rden, B_rd, use_dve=False):
            nu = slice(64, 128) if odd else slice(0, 64)
            de = slice(0, 64) if odd else slice(64, 128)
            if use_dve:
                P.op("dve", lambda e: e.reciprocal(out=rden[de, 0:n], in_=ps_t[obank][de, 0:n]), reads=[PSB[obank]], writes=[B_rd])
                TT("dve", out_ap, ps_t[obank][nu, 0:n], rden[de, 0:n], ALU.mult, [PSB[obank], B_rd], outb)
                return
            P.op("act", lambda e: e.activation(out=rden[de, 0:n], in_=ps_t[obank][de, 0:n], func=AF.Ln), reads=[PSB[obank]], writes=[B_rd])
            P.op("act", lambda e: e.activation(out=rden[de, 0:n], in_=rden[de, 0:n], func=AF.Exp, scale=-1.0), reads=[B_rd], writes=[B_rd])
            TT("dve", out_ap, ps_t[obank][nu, 0:n], rden[de, 0:n], ALU.mult, [PSB[obank], B_rd], outb)

        def na(l, last):
            qn = view(HB, 0, 2304, BF16, "p (c t) -> p c t", c=2); kn = view(HB, 2304, 2304, BF16, "p (c t) -> p c t", c=2)
            B_qk = Buf("naqk")
            Vt0 = view(MC, 0, 4608, BF16, "p (t h j) -> p t h j", t=18, h=4); Vt1 = view(MC, 4608, 3840, BF16, "p (t h j) -> p t h j", t=15, h=4)
            B_v = Buf("naV")
            Btab = view(WB, 0, 1920, BF16, "p (h k) -> p h k", h=60); Traw = view(WB, 1920, 3840, F32, "p (h k) -> p h k", h=60)
            B_bt = Buf("Btab"); B_tr = Buf("Traw")
            rsv = [view(WB, 5760, 512), view(WB, 6272, 512)]; B_rsv = [Buf("nrs0"), Buf("nrs1")]
            sqv = [view(WB, 6784, 256, BF16, "p (c t) -> p c t", c=1), view(WB, 7040, 256, BF16, "p (c t) -> p c t", c=1)]; B_sqv = [Buf("nsq0"), Buf("nsq1")]
            onaT = view(XA, 0, 2304, BF16, "p (c t) -> p c t", c=2); B_o = Buf("onaT")
            PTs = [view(XA, 2304 + 256 * i_, 256, BF16) for i_ in range(4)]; B_PT = [Buf("npt%d" % i_) for i_ in range(4)]
            rden = [view(XA, 3328, 512), view(XA, 3840, 512)]; B_rd = [Buf("nrd0"), Buf("nrd1")]; B_ptc = Buf("ptc")
            PTc = view(XA, 4352, 256, BF16)
            P.retire([B_HB], [B_qk]); P.retire([B_MC], [B_v]); P.retire(WBALL, [B_bt, B_tr] + B_rsv + B_sqv)
            P.retire([B_xT], [B_o, B_ptc] + B_PT + B_rd)
            for c in range(2):
                P.dma("sp", lambda e, c=c: e.dma_start(out=SG[:, c, 0:T], in_=pT_d[768 + c * 128:768 + (c + 1) * 128, :]), reads=[B_p], writes=[B_sg[c]])
            rp = di["na_rpb"]
            src = bass.AP(rp.tensor, l * RPB_PAD + 16, [[1, 64], [31, 60], [1, 64]])
            for hh in range(2):
                P.dma("sp", lambda e, hh=hh: e.dma_start(out=Traw[hh * 64:(hh + 1) * 64], in_=src), writes=[B_tr])
            TT("dve", Traw, Traw, namask[:, 0:1, :].to_broadcast([128, 60, 64]), ALU.mult, [B_tr, B_namask], B_tr)
            TT("dve", Btab, Traw, namask[:, 1:2, :].to_broadcast([128, 60, 64]), ALU.add, [B_tr, B_namask], B_bt)
            P.dma("sp", lambda e: e.dma_start(out=Vt0, in_=vna_d.rearrange("(t p) h j -> p t h j", p=128)), reads=[B_vna], writes=[B_v])
            P.dma("sp", lambda e: e.dma_start(out=Vt1, in_=vna_d[64:64 + 1920].rearrange("(t p) h j -> p t h j", p=128)), reads=[B_vna], writes=[B_v])
            for wi, (row0, dst) in enumerate(((768, qn), (1024, kn))):
                for c in range(2):
                    raw = SG[:, c, 0:T]; rb = B_sg[c]
                    if wi == 1:
                        P.dma("sp", lambda e, raw=raw, row0=row0, c=c: e.dma_start(out=raw, in_=pT_d[row0 + c * 128:row0 + (c + 1) * 128, :]), reads=[B_p], writes=[rb])
                    rms_feat(raw, rb, dst[:, c, :], B_qk, blk64, 128, 1.0, nag[:, wi:wi + 1], sqv, B_sqv, rsv, B_rsv, (6, 7))
            ctr = [0]; og = [0]
            sbanks = (0, 1, 2, 3)
            LOOK = 2
            for h in range(4):
                c = h // 2; pb = (h % 2) * 64; odd = (h % 2 == 1)
                ps_ = slice(pb, pb + 64)
                slots = {}

                def s_part(r, h=h, c=c, ps_=ps_, slots=slots):
                    a = min(max(r - 4, 0), 24)
                    sb_ = sbanks[ctr[0] % 4]; pk = ctr[0] % 4; ctr[0] += 1
                    qa = qn[ps_, c, r * 64:(r + 1) * 64]
                    for i in range(4):
                        kr0 = a + 2 * i; dr0 = kr0 - r + 7
                        MMN(ps_t[sb_][:, i * 64:(i + 1) * 64], kn[ps_, c, kr0 * 64:kr0 * 64 + 128], qa, True, False, [B_qk], PSB[sb_])
                        MMN(ps_t[sb_][:, i * 64:(i + 1) * 64], Btab[ps_, h * 15 + dr0:h * 15 + dr0 + 2, :].rearrange("p a k -> p (a k)"), ident2[ps_, 0:64], False, True, [B_bt, B_ident2], PSB[sb_])
                    for j in range(2):
                        MMN(ps_t[sb_][:, (4 + j) * 64:(5 + j) * 64], kn[ps_, c, S + 128 * j:S + 128 * (j + 1)], qa, True, True, [B_qk], PSB[sb_])
                    ACTF(PTs[pk][:, 0:384], ps_t[sb_][:, 0:384], AF.Exp, [PSB[sb_]], B_PT[pk], scale=0.125)
                    slots[r] = pk

                def v_part(r, h=h, c=c, ps_=ps_, odd=odd, slots=slots):
                    a = min(max(r - 4, 0), 24)
                    pk = slots[r]
                    rr = r % 8
                    obank = 4 + (og[0] % 2)
                    for i in range(6):
                        if i < 4:
                            vt = Vt0[:, a // 2 + i, h, :] if a % 2 == 0 else Vt1[:, (a - 1) // 2 + i, h, :]
                        else:
                            vt = Vt0[:, 16 + (i - 4), h, :]
                        MMN(ps_t[obank][:, rr * 64:(rr + 1) * 64], vt, PTs[pk][:, i * 64:(i + 1) * 64], i == 0, i == 5, [B_v, B_PT[pk]], PSB[obank])
                    if rr == 7:
                        rk = og[0] % 2; og[0] += 1
                        r8 = r // 8
                        normalize(obank, 512, odd, onaT[ps_, c, r8 * 512:(r8 + 1) * 512], B_o, rden[rk], B_rd[rk])

                for r in range(LOOK):
                    s_part(r)
                for r in range(32):
                    if r + LOOK < 32:
                        s_part(r + LOOK)
                    v_part(r)
                if not last:
                    obank = 4 + og[0] % 2; rk = og[0] % 2; og[0] += 1
                    sb_ = sbanks[ctr[0] % 4]; ctr[0] += 1
                    for j in range(2):
                        MMN(ps_t[sb_][:, j * 256:(j + 1) * 256], kn[ps_, c, S + 128 * j:S + 128 * (j + 1)], qn[ps_, c, S:T], True, True, [B_qk], PSB[sb_])
                    ACTF(PTc[:, 0:512], ps_t[sb_][:, 0:512], AF.Exp, [PSB[sb_]], B_ptc, scale=0.125)
                    for j in range(2):
                        MMN(ps_t[obank][:, 0:256], Vt0[:, 16 + j, h, :], PTc[:, j * 256:(j + 1) * 256], j == 0, j == 1, [B_v, B_ptc], PSB[obank])
                    normalize(obank, 256, odd, onaT[ps_, c, S:T], B_o, rden[rk], B_rd[rk])
            ncol = S if last else T
            P.dma("sp", lambda e: e.dma_start(out=mixT_d[256:512, 0:ncol].rearrange("(c p) t -> p c t", p=128), in_=onaT[:, :, 0:ncol]), reads=[B_o], writes=[B_mix])
            P.retire([B_qk], [B_HB]); P.retire([B_v], [B_MC]); P.retire([B_bt, B_tr] + B_rsv + B_sqv, WBALL)
            P.retire([B_o, B_ptc] + B_PT + B_rd, [B_xT])

        def mla(l, last):
            cqn = view(HB, 0, 2304, BF16, "p (c t) -> p c t", c=2); ckvn = view(HB, 2304, 1152, BF16)
            krf = view(HB, 4608, 2304)
            B_cq = Buf("cqn"); B_ckv = Buf("ckvn"); B_kr = Buf("krf")
            Vm = view(MC, 0, 9216, BF16, "p (t h j) -> p t h j", t=18, h=8); B_vm = Buf("Vm")
            wq = view(WB, 0, 768, BF16, "p (k n) -> p k n", k=2); wkv = view(WB, 768, 512, BF16); B_w = Buf("mlaw")
            rsv = [view(WB, 2048, 512), view(WB, 2560, 512)]; B_rsv = [Buf("mrs0"), Buf("mrs1")]
            rt = [view(WB, 3072, 512), view(WB, 3584, 512)]; B_rt = [Buf("mrt0"), Buf("mrt1")]
            sqv = [view(WB, 4096, 512, BF16, "p (c t) -> p c t", c=2), view(WB, 4608, 512, BF16, "p (c t) -> p c t", c=2)]; B_sqv = [Buf("msq0"), Buf("msq1")]
            sql = [view(WB, 5120 + 256 * i_, 256, BF16) for i_ in range(10)]; B_sql = [Buf("sql%d" % i_) for i_ in range(10)]
            ropeC = view(XA, 0, 2304); ropeS = view(XA, 2304, 2304); B_rope = Buf("rope")
            qk = [view(XA, 4608 + i * 1152, 1152, BF16) for i in range(4)]; B_qkf = [Buf("qkf%d" % i) for i in range(4)]
            xhat = [view(XA, 9216, 1152, BF16), view(XA, 10368, 1152, BF16)]; B_xh = [Buf("xh0"), Buf("xh1")]
            omT = view(XA, 11520, 4608, BF16, "p (c t) -> p c t", c=4); B_om = Buf("omT")
            PTs = [view(XA, 16128 + 256 * i_, 256, BF16) for i_ in range(4)]; B_PT = [Buf("mpt%d" % i_) for i_ in range(4)]
            rden = [view(XA, 17152, 512), view(XA, 17664, 512)]; B_rd = [Buf("mrd0"), Buf("mrd1")]
            P.retire([B_HB], [B_cq, B_ckv, B_kr]); P.retire([B_MC], [B_vm]); P.retire(WBALL, [B_w] + B_rsv + B_rt + B_sqv + B_sql)
            P.retire([B_xT], [B_rope, B_om] + B_qkf + B_xh + B_PT + B_rd)
            P.dma("pool", lambda e: e.dma_start(out=wq, in_=di["mla_w_q_up"][l].rearrange("(k p) n -> p k n", p=128)), writes=[B_w])
            P.dma("pool", lambda e: e.dma_start(out=wkv, in_=di["mla_w_kv_up"][l]), writes=[B_w])
            raws = [SG[:, 0, 0:T], SG[:, 1, 0:T]]
            for c in range(2):
                P.dma("sp", lambda e, c=c: e.dma_start(out=raws[c], in_=pT_d[1536 + c * 128:1536 + (c + 1) * 128, :]), reads=[B_p], writes=[B_sg[c]])
            ld("sp", ropeC[0:96, :], di["ropeC"], B_rope); ld("sp", ropeS[0:96, :], di["ropeS"], B_rope)
            B_raw2 = Buf("raw2"); P.retire(B_sg, [B_raw2])
            rms_feat(None, B_raw2, None, B_cq, ones_b, 128, 1.0 / 256, None, sqv, B_sqv, rsv, B_rsv, (6, 7),
                     raws=raws, outs=[cqn[:, 0, :], cqn[:, 1, :]], gains=[mlg[:, 0:1], mlg[:, 1:2]])
            P.retire([B_raw2], B_sg)
            P.dma("sp", lambda e: e.dma_start(out=raws[0], in_=pT_d[1792:1920, :]), reads=[B_p], writes=[B_sg[0]])
            rms_feat(raws[0], B_sg[0], ckvn, B_ckv, ones_b, 128, 1.0 / 128, mlg[:, 2:3], sqv, B_sqv, rsv, B_rsv, (6, 7))
            P.dma("sp", lambda e: e.dma_start(out=krf[0:32, :], in_=pT_d[1920:1952, :]), reads=[B_p], writes=[B_kr])
            P.op("pool", lambda e: e.memset(view(MC, 0, 9216, BF16), 1.0), writes=[B_vm])
            for ti in range(18):
                for hf in range(2):
                    bk = 6 + hf
                    MM(ps_t[bk][:, 0:512], ckvn[:, ti * 128:(ti + 1) * 128], wkv[:, hf * 512:(hf + 1) * 512], True, True, [B_ckv, B_w], PSB[bk])
                    for hh in range(4):
                        h = hf * 4 + hh
                        o = 0 if h % 2 == 0 else 64
                        CP("act" if hh % 2 == 0 else "dve", Vm[:, ti, h, o:o + 64], ps_t[bk][:, hh * 128 + 64:hh * 128 + 128], [PSB[bk]], B_vm)
            qraw = SG[:, 0, 0:T]; kraw = SG[:, 1, 0:T]
            rden = [view(HB, 6912, 512), view(HB, 7424, 512)]
            t2v = [view(WB, 4096, 512), view(WB, 4608, 512)]
            PTp = [view(XA, 16128 + 512 * i_, 512, BF16) for i_ in range(3)]; B_PTp = [Buf("ptp%d" % i_) for i_ in range(3)]
            P.retire(B_PT + B_rd, B_PTp)
            P.retire([B_cq], B_rd)

            B_rl = [[Buf("rl%d_%d" % (wi, bi)) for bi in range(5)] for wi in range(2)]
            B_xl = [[Buf("xl%d_%d" % (wi, bi)) for bi in range(5)] for wi in range(2)]
            P.retire(B_sg, B_rl[0] + B_rl[1]); P.retire(B_xh, B_xl[0] + B_xl[1])
            CP("dve", kraw[64:96, :], krf[0:32, :], [B_kr], B_rl[1][0])
            P.retire([B_rl[1][0]], B_rl[1][1:])
            SK = 2

            def prep(h):
                par = h % 2
                lanes = [(wi, bi, t0, n) for bi, (t0, n) in enumerate(TB) for wi in range(2)]
                raws_ = (qraw, kraw)
                for bi, (t0, n) in enumerate(TB):
                    bq = bi % 2; bk_ = 2 + bi % 2
                    for kc in range(2):
                        MM(ps_t[bq][0:96, 0:n], wq[:, kc, h * 96:(h + 1) * 96], cqn[:, kc, t0:t0 + n], kc == 0, kc == 1, [B_w, B_cq], PSB[bq])
                    CP("act", qraw[0:96, t0:t0 + n], ps_t[bq][0:96, 0:n], [PSB[bq]], B_rl[0][bi])
                    TT("pool", sql[2 * bi][0:96, 0:n], qraw[0:96, t0:t0 + n], qraw[0:96, t0:t0 + n], ALU.mult, [B_rl[0][bi]], B_sql[2 * bi])
                    MM(ps_t[bk_][0:64, 0:n], wkv[:, h * 128:h * 128 + 64], ckvn[:, t0:t0 + n], True, True, [B_w, B_ckv], PSB[bk_])
                    CP("act", kraw[0:64, t0:t0 + n], ps_t[bk_][0:64, 0:n], [PSB[bk_]], B_rl[1][bi])
                    TT("dve", sql[2 * bi + 1][0:96, 0:n], kraw[0:96, t0:t0 + n], kraw[0:96, t0:t0 + n], ALU.mult, [B_rl[1][bi]], B_sql[2 * bi + 1])

                def stage_b(li):
                    wi, bi, t0, n = lanes[li]
                    raw = raws_[wi]; xh = xhat[wi]
                    k = li % 2; bk = 4 + k
                    MM(ps_t[bk][0:96, 0:n], ones_b[0:96, 0:96], sql[li][0:96, 0:n], True, True, [B_sql[li], B_ones], PSB[bk])
                    rstd_from_ps(ps_t[bk][0:96, 0:n], rsv[k][0:96, 0:n], PSB[bk], B_rsv[k], 1.0 / 96, 0, 96)
                    STT("dve", xh[0:96, t0:t0 + n], raw[0:96, t0:t0 + n], mlg[0:96, 3 + wi:4 + wi], rsv[k][0:96, 0:n], ALU.mult, ALU.mult, [B_rl[wi][bi], B_rsv[k], B_mlg], B_xl[wi][bi])

                def stage_d(li):
                    wi, bi, t0, n = lanes[li]
                    xh = xhat[wi]; fin = qk[par * 2 + wi]; fb = B_qkf[par * 2 + wi]
                    k = li % 2; bk = li % 4
                    MM(ps_t[bk][0:96, 0:n], permT[0:96, 0:96], xh[0:96, t0:t0 + n], True, True, [B_perm, B_xl[wi][bi]], PSB[bk])
                    rtb = rt[k].bitcast(BF16); t2b = t2v[k].bitcast(BF16)
                    TT("dve", rtb[0:96, 0:n], ps_t[bk][0:96, 0:n], ropeS[0:96, t0:t0 + n], ALU.mult, [PSB[bk], B_rope], B_rt[k])
                    TT("pool", t2b[0:96, 0:n], xh[0:96, t0:t0 + n], ropeC[0:96, t0:t0 + n], ALU.mult, [B_xl[wi][bi], B_rope], B_sqv[k])
                    TT("dve" if li % 2 == 0 else "pool", fin[0:96, t0:t0 + n], rtb[0:96, 0:n], t2b[0:96, 0:n], ALU.add, [B_rt[k], B_sqv[k]], fb)

                for step in range(len(lanes) + SK):
                    if step < len(lanes):
                        stage_b(step)
                    if step >= SK:
                        stage_d(step - SK)

            sc = 96.0 ** -0.5
            blocks = TB[:4] if last else TB
            ctr = {"pair": 0, "ob": 0}
            LOOK = 2

            def attn(h):
                par = h % 2
                qf = qk[par * 2]; kf = qk[par * 2 + 1]; kr_ = [B_qkf[par * 2], B_qkf[par * 2 + 1]]
                odd = (h % 2 == 1); pb = 64 if odd else 0
                items = []
                for (t0, n) in blocks:
                    keys = list(range(18)) if t0 < S else [16, 17]
                    npair = len(keys) // 2
                    for pi_ in range(npair):
                        items.append((t0, n, keys[2 * pi_], keys[2 * pi_ + 1], pi_ == 0, pi_ == npair - 1))
                slots = {}

                def s_part(ix):
                    t0, n, k0, k1, first, lastp = items[ix]
                    sl = ctr["pair"] % 3; ctr["pair"] += 1; slots[ix] = sl
                    b0 = 2 * sl
                    for u_, kc in enumerate((k0, k1)):
                        MM(ps_t[b0 + u_][:, 0:n], kf[0:96, kc * 128:(kc + 1) * 128], qf[0:96, t0:t0 + n], True, True, kr_, PSB[b0 + u_])
                    if n == 512:
                        P.op("act", lambda e: e.activation(out=PTp[sl], in_=PS_ALL[:, b0 * 512:(b0 + 2) * 512], func=AF.Exp, scale=sc), reads=[PSB[b0], PSB[b0 + 1]], writes=[B_PTp[sl]])
                    else:
                        src = PS_ALL[:, b0 * 512:(b0 + 2) * 512].rearrange("p (b f) -> p b f", b=2)[:, :, 0:n]
                        dst = PTp[sl].rearrange("p (b f) -> p b f", b=2)[:, :, 0:n]
                        P.op("act", lambda e: e.activation(out=dst, in_=src, func=AF.Exp, scale=sc), reads=[PSB[b0], PSB[b0 + 1]], writes=[B_PTp[sl]])

                def v_part(ix):
                    t0, n, k0, k1, first, lastp = items[ix]
                    sl = slots[ix]
                    if first:
                        ctr["ob"] += 1
                    obank = 6 + ctr["ob"] % 2; rk = ctr["ob"] % 2
                    for u_, kc in enumerate((k0, k1)):
                        MM(ps_t[obank][:, 0:n], Vm[:, kc, h, :], PTp[sl][:, u_ * 512:u_ * 512 + n], first and u_ == 0, lastp and u_ == 1, [B_vm, B_PTp[sl]], PSB[obank])
                    if lastp:
                        normalize(obank, n, odd, omT[pb:pb + 64, h // 2, t0:t0 + n], B_om, rden[rk], B_rd[rk], use_dve=True)

                for ix in range(min(LOOK, len(items))):
                    s_part(ix)
                for ix in range(len(items)):
                    if ix + LOOK < len(items):
                        s_part(ix + LOOK)
                    v_part(ix)

            prep(0)
            for h in range(8):
                if h + 1 < 8:
                    prep(h + 1)
                attn(h)
            P.retire(B_PTp, B_PT + B_rd)
            P.retire(B_rl[0] + B_rl[1], B_sg); P.retire(B_xl[0] + B_xl[1], B_xh)
            if debug:
                for i_ in range(2):
                    P.dma("sp", lambda e, i_=i_: e.dma_start(out=dbg_d[i_], in_=qk[2 + i_][0:96, :]), reads=[B_qkf[2 + i_]], writes=[B_dbg])
            ncol = S if last else T
            P.dma("sp", lambda e: e.dma_start(out=mixT_d[512:1024, 0:ncol].rearrange("(c p) t -> p c t", p=128), in_=omT[:, :, 0:ncol]), reads=[B_om], writes=[B_mix])
            P.retire([B_cq, B_ckv, B_kr] + B_rd, [B_HB]); P.retire([B_vm], [B_MC]); P.retire([B_w] + B_rsv + B_rt + B_sqv + B_sql, WBALL)
            P.retire([B_rope, B_om] + B_qkf + B_xh + B_PT + B_rd, [B_xT])

        class _M:
            pass
        mix = _M()
        mix.hyena = hyena; mix.na = na; mix.mla = mla

        for l in range(n_layers):
            last = (l == NL - 1)
            modT = modTs[l % 2]; B_mod = B_mods[l % 2]
            if l == 0:
                small_loads(0)
                prefetch_win(0)
                modulation_all(0, modT, B_mod)
                mod_finish(modT, B_mod)
            else:
                prefetch_win(l)
            hT = norm_adaln(0)
            in_proj(l, hT)
            if stop == "inproj":
                break
            mix.hyena(l, S, last)
            if stop == "hyena":
                break
            mix.na(l, last)
            if stop == "na":
                break
            mix.mla(l, last)
            if stop == "mla":
                break
            prefetch_wo(l)
            P.dma("act", lambda e: e.dma_start(out=xT, in_=xTd_v), reads=[B_x], writes=[B_xT])
            if not last:
                mix.hyena(l, LC, last)
            prefetch_ffn0(l)
            out_proj_residual(l)
            P.retire([B_wo], [B_HB])
            hT = norm_adaln(1)
            if l + 1 < n_layers:
                small_loads(l + 1)
            ffn(l, hT, nxt=((l + 1, modTs[(l + 1) % 2], B_mods[(l + 1) % 2]) if l + 1 < n_layers else None))
            if l + 1 < n_layers:
                mod_finish(modTs[(l + 1) % 2], B_mods[(l + 1) % 2])
            if not last:
                P.dma("sp", lambda e: e.dma_start(out=xTd_v, in_=xT), reads=[B_xT], writes=[B_x])

        if stop is None:
            osb = [view(MC, 0, 1024), view(MC, 1024, 1024)]; B_os = [Buf("os0"), Buf("os1")]
            P.retire([B_MC], B_os)
            for ti in range(16):
                k = ti % 2
                for c in range(8):
                    pi = c
                    P.op("pe", lambda e, c=c, ti=ti, pi=pi: e.transpose(out=ps_t[pi][:, 0:128], in_=xT[:, c, ti * 128:(ti + 1) * 128], identity=identf), reads=[B_xT, B_identf], writes=[PSB[pi]])
                    if c % 2 == 0:
                        P.op("act", lambda e, c=c, k=k, pi=pi: e.copy(out=osb[k][:, c * 128:(c + 1) * 128], in_=ps_t[pi][:, 0:128]), reads=[PSB[pi]], writes=[B_os[k]])
                    else:
                        P.op("dve", lambda e, c=c, k=k, pi=pi: e.tensor_copy(out=osb[k][:, c * 128:(c + 1) * 128], in_=ps_t[pi][:, 0:128]), reads=[PSB[pi]], writes=[B_os[k]])
                P.dma("sp", lambda e, ti=ti, k=k: e.dma_start(out=out_d[ti * 128:(ti + 1) * 128, :], in_=osb[k]), reads=[B_os[k]], writes=[B_out])
        P.final_wait("sp", [B_out, B_x, B_p, B_vna, B_mix, B_dbg])
        P.emit()
    return nc


def make_in_maps(inputs):
    c = _consts()
    shared = {}
    for k in W_SPECS:
        a = np.ascontiguousarray(np.asarray(inputs[k], dtype=np.float32))
        if k == "na_rpb":
            a = np.ascontiguousarray(np.pad(a.reshape(NL, -1), ((0, 0), (64, 64))))
        shared[k] = a
    shared.update(c)
    maps = []
    x = np.asarray(inputs["x"], dtype=np.float32); ctx = np.asarray(inputs["ctx"], dtype=np.float32)
    cc = np.asarray(inputs["c"], dtype=np.float32); c_ctx = np.asarray(inputs["c_ctx"], dtype=np.float32)
    for b in range(x.shape[0]):
        m = dict(shared)
        m["x"] = np.ascontiguousarray(x[b]); m["ctx"] = np.ascontiguousarray(ctx[b])
        m["cvec"] = np.ascontiguousarray(np.stack([cc[b], c_ctx], axis=0))
        maps.append(m)
    return maps


_NC = None


def kernel(**inputs):
    global _NC
    if _NC is None:
        _NC = build_nc()
    maps = make_in_maps(inputs)
    res = run_bass_kernel_spmd(_NC, maps, core_ids=list(range(8)))
    return np.stack([np.asarray(r["out"], dtype=np.float32) for r in res.results], axis=0)
```

```python
import math
import contextlib
import numpy as np
import ml_dtypes
import concourse.bass as bass
import concourse.mybir as mybir
from concourse.bass_utils import run_bass_kernel_spmd

F32 = mybir.dt.float32
BF16 = mybir.dt.bfloat16
ALU = mybir.AluOpType
AF = mybir.ActivationFunctionType

D = 1024; S = 2048; LC = 256; T = S + LC; NL = 4
IN_DIM = 1952; DFF = 4096
EPS = 1e-6
TB = [(0, 512), (512, 512), (1024, 512), (1536, 512), (2048, 256)]
PI = math.pi
RPB_PAD = 64 + 4 * 15 * 31 + 64
HSTOP = 0

ENGS = ("pe", "act", "dve", "pool", "sp")
N_DSEM = 6


class Buf:
    __slots__ = ("name", "w", "r")

    def __init__(self, name):
        self.name = name; self.w = []; self.r = []


class Prog:
    def __init__(self, nc):
        self.nc = nc
        self.ops = {e: [] for e in ENGS}
        self.cnt = {e: 0 for e in ENGS}
        self.known = {e: {} for e in ENGS}
        self.dcnt = {}; self.dq = {e: 0 for e in ENGS}; self.dlast = {}
        self.sems = {}

    def _need(self, eng, toks):
        best = {}
        for t in toks:
            if t is None:
                continue
            k, v = t
            if k == eng and eng == "pe":
                continue
            if self.known[eng].get(k, 0) >= v:
                continue
            if best.get(k, 0) < v:
                best[k] = v
        for k, v in best.items():
            self.known[eng][k] = v
        return list(best.items())

    def _track(self, tok, reads, writes):
        for b in reads:
            b.r.append(tok)
            if len(b.r) > 64:
                b.r = _compress(b.r)
        for b in writes:
            b.w = [tok]; b.r = []

    def op(self, eng, fn, reads=(), writes=()):
        toks = []
        for b in reads:
            toks.extend(b.w)
        for b in writes:
            toks.extend(b.w); toks.extend(b.r)
        waits = self._need(eng, toks)
        self.cnt[eng] += 1
        tok = (eng, self.cnt[eng])
        self.ops[eng].append((waits, fn, ("e", eng)))
        self._track(tok, reads, writes)
        return tok

    def dma(self, q, fn, reads=(), writes=()):
        i = self.dq[q]; self.dq[q] += 1
        key = "d_%s_%d" % (q, i % N_DSEM)
        toks = [self.dlast.get(key)]
        for b in reads:
            toks.extend(b.w)
        for b in writes:
            toks.extend(b.w); toks.extend(b.r)
        waits = self._need(q, toks)
        self.dcnt[key] = self.dcnt.get(key, 0) + 16
        tok = (key, self.dcnt[key])
        self.dlast[key] = tok
        self.ops[q].append((waits, fn, ("d", key)))
        self._track(tok, reads, writes)
        return tok

    def retire(self, old, new):
        toks = []
        for b in old:
            toks.extend(b.w); toks.extend(b.r)
        toks = _compress(toks)
        for b in new:
            b.w = _compress(list(b.w) + toks)
            b.r = _compress(list(b.r) + toks)

    def final_wait(self, eng, bufs):
        toks = []
        for b in bufs:
            toks.extend(b.w); toks.extend(b.r)
        waits = self._need(eng, toks)
        self.ops[eng].append((waits, None, None))

    def emit(self):
        nc = self.nc
        keys = list(ENGS) + sorted(self.dcnt.keys())
        with contextlib.ExitStack() as st:
            for k in keys:
                self.sems[k] = st.enter_context(nc.semaphore("s_" + k))
            block = st.enter_context(nc.Block())
            sems = self.sems

            def run(e, name):
                for waits, fn, inc in self.ops[name]:
                    for k, v in waits:
                        e.wait_ge(sems[k], v)
                    if fn is None:
                        continue
                    ins = fn(e)
                    ins.then_inc(sems[inc[1]], 1 if inc[0] == "e" else 16)

            @block.tensor
            def _(e):
                run(e, "pe")

            @block.scalar
            def _(e):
                run(e, "act")

            @block.vector
            def _(e):
                run(e, "dve")

            @block.gpsimd
            def _(e):
                run(e, "pool")

            @block.sync
            def _(e):
                run(e, "sp")


def _compress(toks):
    best = {}
    for k, v in toks:
        if best.get(k, 0) < v:
            best[k] = v
    return list(best.items())


_CONSTS = None


def _dft_tiles(L):
    N = 2 * L; ntc = L // 128; GJ = min(4, ntc); G = ntc // GJ
    a = np.arange(L, dtype=np.float64)
    th = np.pi * np.outer(2 * a + 1, 2 * a + 1) / (2 * N)
    M = np.stack([np.cos(th), np.sin(th)])
    M6 = M.reshape(2, ntc, 128, G, GJ, 128)
    Dm = np.ascontiguousarray(M6.transpose(3, 2, 0, 1, 4, 5))
    Dm = Dm.reshape(G, 128, GJ * 2 * ntc * 128).astype(ml_dtypes.bfloat16)
    f = np.arange(L, dtype=np.float64)
    al = np.pi * (2 * f + 1) / (2 * N)
    ca = np.cos(al).reshape(ntc, 128).T; sa = np.sin(al).reshape(ntc, 128).T
    alpha = np.ascontiguousarray(np.stack([ca, sa, -sa], axis=1)).astype(np.float32)
    return Dm, alpha


def _filt_consts(L):
    t = np.linspace(0.0, 1.0, L, dtype=np.float32)[:, None]
    w = (np.float32(2.0 * math.pi / L) * np.arange(L, dtype=np.float32))[:, None]
    bands = np.linspace(1e-4, 7, 8, dtype=np.float32)
    z = np.concatenate([t, np.cos(bands * w), -np.sin(bands * w)], axis=-1).astype(np.float32)
    deltas = np.linspace(math.log(1e-2) / 1.5, math.log(1e-2) / 0.3, 256, dtype=np.float32)
    dec = np.exp(-t * np.abs(deltas)).astype(np.float32)
    decs = np.zeros_like(dec); decs[:-1] = dec[1:]
    zT = np.zeros((17, L), np.float32); zT[:] = z.T
    return np.ascontiguousarray(zT), np.ascontiguousarray(np.stack([dec, decs], axis=1))


def _consts():
    global _CONSTS
    if _CONSTS is not None:
        return _CONSTS
    c = {}
    c["dftL"], c["alphaL"] = _dft_tiles(S)
    c["dftC"], c["alphaC"] = _dft_tiles(LC)
    c["zTL"], c["decL"] = _filt_consts(S)
    c["zTC"], c["decC"] = _filt_consts(LC)
    Ct = np.ones((96, T), np.float32); St = np.zeros((96, T), np.float32)
    tt = np.arange(S)
    pos = np.stack([tt // 64, tt % 64], axis=-1).astype(np.float32)
    inv = (10000.0 ** (-np.arange(8, dtype=np.float32) / 8)).astype(np.float32)
    ang = pos[:, :, None] * inv
    for a in range(2):
        for h in range(2):
            for f in range(8):
                Ct[64 + a * 16 + h * 8 + f, :S] = np.cos(ang[:, a, f])
                St[64 + a * 16 + h * 8 + f, :S] = np.sin(ang[:, a, f])
    c["ropeC"] = Ct; c["ropeS"] = St
    Pm = np.zeros((96, 96), np.float32)
    for a in range(2):
        for f in range(8):
            Pm[64 + a * 16 + 8 + f, 64 + a * 16 + f] = -1.0
            Pm[64 + a * 16 + f, 64 + a * 16 + 8 + f] = 1.0
    c["permT"] = Pm.astype(ml_dtypes.bfloat16)
    qc = np.arange(64)[:, None]; kc = np.arange(64)[None, :]
    cs = np.clip(qc - 8, 0, 48)
    valid = ((kc >= cs) & (kc < cs + 16)).astype(np.float32)
    m = np.stack([valid * 8.0, (1.0 - valid) * (-240000.0)], axis=1)
    m = m[::-1]
    c["namask"] = np.ascontiguousarray(np.concatenate([m, m], axis=0)).astype(np.float32)
    idb = np.zeros((128, 128), np.float32)
    idb[:64, :64] = np.eye(64)[::-1]; idb[64:, :64] = np.eye(64)[::-1]
    c["identf"] = np.eye(128, dtype=np.float32)
    c["ident2"] = idb.astype(ml_dtypes.bfloat16)
    c["identb"] = np.eye(128, dtype=np.float32).astype(ml_dtypes.bfloat16)
    _CONSTS = c
    return c


CONST_SPECS = None


def _const_specs():
    c = _consts()
    return {k: (list(v.shape), BF16 if v.dtype == ml_dtypes.bfloat16 else F32) for k, v in c.items()}


W_SPECS = {
    "w_mod": [NL, D, 6 * D], "b_mod": [NL, 6 * D], "g_norm1": [NL, D], "w_in": [NL, D, IN_DIM],
    "hy_conv_w": [NL, 3, 768], "hy_conv_b": [NL, 768], "hy_f_w1": [NL, 17, 64], "hy_f_b1": [NL, 64],
    "hy_f_w2": [NL, 64, 64], "hy_f_b2": [NL, 64], "hy_f_w3": [NL, 64, 1024], "hy_f_b3": [NL, 1024],
    "hy_freq": [NL, 2, 64], "hy_bias": [NL, 2, 256], "na_g_q": [NL, 64], "na_g_k": [NL, 64],
    "na_rpb": [NL, RPB_PAD], "mla_g_qa": [NL, 256], "mla_g_kva": [NL, 128], "mla_w_q_up": [NL, 256, 768],
    "mla_w_kv_up": [NL, 128, 1024], "mla_g_q": [NL, 96], "mla_g_k": [NL, 96], "w_out": [NL, D, D],
    "g_norm2": [NL, D], "w_ff1": [NL, D, DFF], "b_ff1": [NL, DFF], "w_ff2": [NL, DFF, D], "b_ff2": [NL, D],
}


def build_nc(n_layers=NL, debug=False, stop=None):
    nc = bass.Bass("TRN2", target_bir_lowering=False)
    P = Prog(nc)
    di = {}
    di["x"] = nc.dram_tensor("x", [S, D], F32, kind="ExternalInput").ap()
    di["ctx"] = nc.dram_tensor("ctx", [LC, D], F32, kind="ExternalInput").ap()
    di["cvec"] = nc.dram_tensor("cvec", [2, D], F32, kind="ExternalInput").ap()
    for k, shp in W_SPECS.items():
        di[k] = nc.dram_tensor(k, shp, F32, kind="ExternalInput").ap()
    for k, (shp, dt) in _const_specs().items():
        di[k] = nc.dram_tensor(k, shp, dt, kind="ExternalInput").ap()
    out_d = nc.dram_tensor("out", [S, D], F32, kind="ExternalOutput").ap()
    skind = "ExternalOutput" if debug else "Internal"
    xT_d = nc.dram_tensor("xT_d", [D, T], F32, kind=skind).ap()
    pT_d = nc.dram_tensor("pT_d", [IN_DIM, T], F32, kind=skind).ap()
    vna_d = nc.dram_tensor("vna_d", [T, 4, 128], BF16, kind=skind).ap()
    mixT_d = nc.dram_tensor("mixT_d", [D, T], BF16, kind=skind).ap()
    dbg_d = nc.dram_tensor("dbg_d", [2, 96, T], BF16, kind=skind).ap(); B_dbg = Buf("dbg")
    B_x = Buf("xT_d"); B_p = Buf("pT_d"); B_vna = Buf("vna_d"); B_mix = Buf("mixT_d"); B_out = Buf("out")

    st = contextlib.ExitStack()
    with st:
        def sbt(name, shape, dt):
            return st.enter_context(nc.sbuf_tensor(name, shape, dt))

        XA = sbt("XA", [128, 18432], F32)
        HB = sbt("HB", [128, 9216], F32)
        MC = sbt("MC", [128, 9216], F32)
        WB = sbt("WB", [128, 8192], F32)
        SG = sbt("SG", [128, 2, 2312], F32)
        MI = sbt("MI", [128, 3072], F32)
        PS_ALL = st.enter_context(nc.psum_tensor("psall", [128, 4096], F32))
        ps_t = [PS_ALL[:, i * 512:(i + 1) * 512] for i in range(8)]
        PSB = [Buf("ps%d" % i) for i in range(8)]

        def view(ar, off, nw, dt=F32, pat=None, **kw):
            v = ar[:, off:off + nw]
            if dt == BF16:
                v = v.bitcast(BF16)
            if pat:
                v = v.rearrange(pat, **kw)
            return v

        mi_off = [0]

        def mi(nw, dt=F32, pat=None, **kw):
            o = mi_off[0]; mi_off[0] += nw
            assert mi_off[0] <= 3072
            return view(MI, o, nw, dt, pat, **kw)

        identf = mi(128); B_identf = Buf("identf")
        identb = mi(64, BF16); B_identb = Buf("identb")
        ident2 = mi(64, BF16); B_ident2 = Buf("ident2")
        ones_b = mi(64, BF16); B_ones = Buf("ones")
        blk64 = mi(64, BF16); B_blk64 = Buf("blk64")
        o96 = mi(64, BF16); B_o96 = Buf("o96")
        permT = mi(48, BF16); B_perm = Buf("permT")
        scT = mi(8, BF16, "p (c r) -> p c r", r=2); B_sc = Buf("scT")
        modTs = [mi(96, F32, "p (j r) -> p j r", r=2), mi(96, F32, "p (j r) -> p j r", r=2)]
        B_mods = [Buf("modT0"), Buf("modT1")]
        modT = modTs[0]; B_mod = B_mods[0]
        gsc1 = mi(16, F32, "p (c r) -> p c r", r=2); gsc2 = mi(16, F32, "p (c r) -> p c r", r=2)
        gb2 = mi(16, F32, "p (c r) -> p c r", r=2); B_mder = Buf("modder")
        g12 = mi(16, F32, "p (k c) -> p k c", c=8); B_g12 = Buf("g12")
        b2t = mi(8); B_b2 = Buf("b2")
        b1ts = [mi(32), mi(32)]; B_b1s = [Buf("b1a"), Buf("b1b")]
        hcw = mi(24, F32, "p (c j) -> p c j", j=4); B_hcw = Buf("hcw")
        hyb = mi(4, F32, "p (o c) -> p o c", c=2); B_hyb = Buf("hyb")
        nag = mi(2); B_nag = Buf("nag")
        mlg = mi(5); B_mlg = Buf("mlg")
        filp = mi(8); B_filp = Buf("filp")
        namask = mi(128, F32, "p (a k) -> p a k", k=64); B_namask = Buf("namask")
        eps_t = mi(1); B_eps = Buf("eps")
        cvt = mi(16, F32, "p (c r) -> p c r", r=2)
        alphaL = mi(48, F32, "p (a c) -> p a c", c=16); alphaC = mi(6, F32, "p (a c) -> p a c", c=2); B_alpha = Buf("alpha")
        rn_t = mi(512, F32, "p (o c) -> p o c", c=256); B_rn = Buf("rn")

        def ld(q, dst, src, wb, rb=()):
            P.dma(q, lambda e: e.dma_start(out=dst, in_=src), reads=rb, writes=[wb])

        ld("sp", identf, di["identf"], B_identf)
        ld("sp", identb, di["identb"], B_identb)
        ld("sp", ident2, di["ident2"], B_ident2)
        ld("sp", permT[0:96, 0:96], di["permT"], B_perm)
        ld("sp", namask, di["namask"], B_namask)
        ld("sp", alphaL, di["alphaL"], B_alpha)
        ld("sp", alphaC, di["alphaC"], B_alpha)
        P.op("pool", lambda e: e.memset(ones_b, 1.0), writes=[B_ones])
        P.op("pool", lambda e: e.memset(blk64, 0.0), writes=[B_blk64])
        P.op("pool", lambda e: e.memset(blk64[0:64, 0:64], 1.0 / 64), writes=[B_blk64])
        P.op("pool", lambda e: e.memset(blk64[64:128, 64:128], 1.0 / 64), writes=[B_blk64])
        P.op("pool", lambda e: e.memset(o96, 1.0 / 96), writes=[B_o96])
        P.op("pool", lambda e: e.memset(eps_t, EPS), writes=[B_eps])

        def rstd_from_ps(ps_ap, out_ap, psb, outb, scale, p0=0, p1=128):
            P.op("act", lambda e: e.activation(out=out_ap, in_=ps_ap, func=AF.Ln, bias=eps_t[p0:p1, 0:1], scale=scale),
                 reads=[psb, B_eps], writes=[outb])
            P.op("act", lambda e: e.activation(out=out_ap, in_=out_ap, func=AF.Exp, scale=-0.5), reads=[outb], writes=[outb])

        xT = view(XA, 0, 18432, F32, "p (c t) -> p c t", t=T); B_xT = Buf("xT")
        sg = [SG[:, 0, :], SG[:, 1, :]]; B_sg = [Buf("sg0"), Buf("sg1")]
        for ti in range(18):
            src = di["x"][ti * 128:(ti + 1) * 128, :] if ti < 16 else di["ctx"][(ti - 16) * 128:(ti - 15) * 128, :]
            s_ap = sg[ti % 2][:, 0:1024]; sb_ = B_sg[ti % 2]
            ld("sp", s_ap, src, sb_)
            for c in range(8):
                pi = c % 4 + (ti % 2) * 4
                P.op("pe", lambda e, c=c, s_ap=s_ap, pi=pi: e.transpose(out=ps_t[pi][:, 0:128], in_=s_ap[:, c * 128:(c + 1) * 128], identity=identf),
                     reads=[sb_, B_identf], writes=[PSB[pi]])
                eng = "act" if c % 2 == 0 else "dve"
                if eng == "act":
                    P.op("act", lambda e, c=c, pi=pi, ti=ti: e.copy(out=xT[:, c, ti * 128:(ti + 1) * 128], in_=ps_t[pi][:, 0:128]), reads=[PSB[pi]], writes=[B_xT])
                else:
                    P.op("dve", lambda e, c=c, pi=pi, ti=ti: e.tensor_copy(out=xT[:, c, ti * 128:(ti + 1) * 128], in_=ps_t[pi][:, 0:128]), reads=[PSB[pi]], writes=[B_xT])
        xTd_v = xT_d.rearrange("(c p) t -> p c t", p=128)
        P.dma("sp", lambda e: e.dma_start(out=xTd_v, in_=xT), reads=[B_xT], writes=[B_x])
        for r_ in range(2):
            P.dma("sp", lambda e, r_=r_: e.dma_start(out=cvt[:, :, r_], in_=di["cvec"][r_].rearrange("(c p) -> p c", p=128), allow_slow_non_contiguous=True), writes=[B_sc])
        P.op("act", lambda e: e.activation(out=scT, in_=cvt, func=AF.Silu), reads=[B_sc], writes=[B_sc])

        def small_loads(l):
            def nld(dst, src, wb):
                P.dma("sp", lambda e: e.dma_start(out=dst, in_=src, allow_slow_non_contiguous=True), writes=[wb])
            nld(g12[:, 0, :], di["g_norm1"][l].rearrange("(c p) -> p c", p=128), B_g12)
            nld(g12[:, 1, :], di["g_norm2"][l].rearrange("(c p) -> p c", p=128), B_g12)
            nld(b2t, di["b_ff2"][l].rearrange("(c p) -> p c", p=128), B_b2)
            nld(b1ts[l % 2], di["b_ff1"][l].rearrange("(c p) -> p c", p=128), B_b1s[l % 2])
            for j_ in range(3):
                nld(hcw[:, :, j_], di["hy_conv_w"][l, j_].rearrange("(c p) -> p c", p=128), B_hcw)
            nld(hcw[:, :, 3], di["hy_conv_b"][l].rearrange("(c p) -> p c", p=128), B_hcw)
            for o_ in range(2):
                nld(hyb[:, o_, :], di["hy_bias"][l, o_].rearrange("(c p) -> p c", p=128), B_hyb)
            for hh in range(2):
                nld(nag[hh * 64:(hh + 1) * 64, 0:1], di["na_g_q"][l].rearrange("(p o) -> p o", o=1), B_nag)
                nld(nag[hh * 64:(hh + 1) * 64, 1:2], di["na_g_k"][l].rearrange("(p o) -> p o", o=1), B_nag)
            nld(mlg[:, 0:2], di["mla_g_qa"][l].rearrange("(c p) -> p c", p=128), B_mlg)
            nld(mlg[:, 2:3], di["mla_g_kva"][l].rearrange("(p o) -> p o", o=1), B_mlg)
            nld(mlg[0:96, 3:4], di["mla_g_q"][l].rearrange("(p o) -> p o", o=1), B_mlg)
            nld(mlg[0:96, 4:5], di["mla_g_k"][l].rearrange("(p o) -> p o", o=1), B_mlg)
            nld(filp[0:64, 0:1], di["hy_f_b1"][l].rearrange("(p o) -> p o", o=1), B_filp)
            nld(filp[0:64, 1:2], di["hy_f_b2"][l].rearrange("(p o) -> p o", o=1), B_filp)
            nld(filp[0:64, 2:4], di["hy_freq"][l].rearrange("k p -> p k"), B_filp)
            P.op("dve", lambda e: e.tensor_tensor(out=filp[0:64, 4:6], in0=filp[0:64, 0:2], in1=filp[0:64, 2:4], op=ALU.mult), reads=[B_filp], writes=[B_filp])

        mw_buf = [view(MC, 3072, 2048, BF16, "p (k n) -> p k n", n=512), view(MC, 5120, 2048, BF16, "p (k n) -> p k n", n=512)]
        B_mw = [Buf("mw0"), Buf("mw1")]

        def mod_begin(l, mT, B_m):
            P.dma("sp", lambda e: e.dma_start(out=mT[:, :, 0], in_=di["b_mod"][l].rearrange("(j p) -> p j", p=128), allow_slow_non_contiguous=True), writes=[B_m])
            P.op("dve", lambda e: e.tensor_copy(out=mT[:, :, 1], in_=mT[:, :, 0]), reads=[B_m], writes=[B_m])
            for q in range(2):
                mod_dma(l, q)

        def mod_dma(l, q):
            wv = di["w_mod"][l].rearrange("(kc p) n -> p kc n", p=128)
            wt = mw_buf[q % 2]
            P.dma("pool", lambda e: e.dma_start(out=wt, in_=wv[:, :, q * 512:(q + 1) * 512]), writes=[B_mw[q % 2]])

        def mod_piece(l, q, pi, mT, B_m):
            wt = mw_buf[q % 2]; wbb = B_mw[q % 2]
            for jj in range(4):
                for kc in range(8):
                    P.op("pe", lambda e, jj=jj, kc=kc: e.matmul(ps_t[pi][:, jj * 2:jj * 2 + 2], lhsT=wt[:, kc, jj * 128:(jj + 1) * 128], rhs=scT[:, kc, :], start=(kc == 0), stop=(kc == 7)),
                         reads=[wbb, B_sc], writes=[PSB[pi]])
            j0 = q * 4
            P.op("dve", lambda e: e.tensor_tensor(out=mT[:, j0:j0 + 4, :], in0=mT[:, j0:j0 + 4, :], in1=ps_t[pi][:, 0:8].rearrange("p (j r) -> p j r", r=2), op=ALU.add),
                 reads=[PSB[pi], B_m], writes=[B_m])
            if q + 2 < 12:
                mod_dma(l, q + 2)

        def modulation_all(l, mT, B_m):
            P.retire([B_MC], B_mw)
            mod_begin(l, mT, B_m)
            for q in range(12):
                mod_piece(l, q, q % 2, mT, B_m)
            P.retire(B_mw, [B_MC])

        def mod_finish(mT, B_m):
            for k, gt, so in ((0, gsc1, 8), (1, gsc2, 32)):
                for r in range(2):
                    P.op("dve", lambda e, k=k, gt=gt, so=so, r=r: e.scalar_tensor_tensor(out=gt[:, :, r], in0=mT[:, so:so + 8, r], scalar=1.0, in1=g12[:, k, :], op0=ALU.add, op1=ALU.mult),
                         reads=[B_m, B_g12], writes=[B_mder])
            for r in range(2):
                P.op("dve", lambda e, r=r: e.tensor_tensor(out=gb2[:, :, r], in0=mT[:, 40:48, r], in1=b2t, op=ALU.mult), reads=[B_m, B_b2], writes=[B_mder])

        B_wb = [Buf("wb0"), Buf("wb1")]
        B_HB = Buf("HB"); B_MC = Buf("MC")

        def norm_adaln(which, nblk=5):
            hT = view(HB, 0, 9216, BF16, "p (c t) -> p c t", t=T)
            mT = modT; B_m = B_mod
            gt = gsc1 if which == 0 else gsc2
            sh0 = 0 if which == 0 else 24
            sq = [view(MC, 0, 2048, BF16, "p (c t) -> p c t", t=512), view(MC, 2048, 2048, BF16, "p (c t) -> p c t", t=512)]
            rs = [view(MC, 4096, 512), view(MC, 4608, 512)]
            B_sq = [Buf("sq0"), Buf("sq1")]; B_rs = [Buf("rs0"), Buf("rs1")]
            P.retire([B_MC], B_sq + B_rs + B_nt)
            def stage1(bi):
                t0, n = TB[bi]; k = bi % 2
                for c in range(8):
                    if c % 2 == 0:
                        P.op("act", lambda e, c=c: e.activation(out=sq[k][:, c, 0:n], in_=xT[:, c, t0:t0 + n], func=AF.Square), reads=[B_xT], writes=[B_sq[k]])
                    else:
                        P.op("dve", lambda e, c=c: e.tensor_tensor(out=sq[k][:, c, 0:n], in0=xT[:, c, t0:t0 + n], in1=xT[:, c, t0:t0 + n], op=ALU.mult), reads=[B_xT], writes=[B_sq[k]])
                for c in range(8):
                    P.op("pe", lambda e, c=c: e.matmul(ps_t[k][:, 0:n], lhsT=ones_b, rhs=sq[k][:, c, 0:n], start=(c == 0), stop=(c == 7)), reads=[B_sq[k], B_ones], writes=[PSB[k]])

            def stage2(bi):
                t0, n = TB[bi]; k = bi % 2
                rstd_from_ps(ps_t[k][:, 0:n], rs[k][:, 0:n], PSB[k], B_rs[k], 1.0 / D)
                r = 0 if t0 < S else 1
                for c in range(8):
                    kk = c % 2
                    P.op("dve", lambda e, c=c, kk=kk: e.tensor_tensor(out=nrm_tmp[kk][:, 0:n], in0=xT[:, c, t0:t0 + n], in1=rs[k][:, 0:n], op=ALU.mult),
                         reads=[B_xT, B_rs[k]], writes=[B_nt[kk]])
                    P.op("act", lambda e, c=c, kk=kk: e.activation(out=hT[:, c, t0:t0 + n], in_=nrm_tmp[kk][:, 0:n], func=AF.Identity, bias=mT[:, sh0 + c, r:r + 1], scale=gt[:, c, r:r + 1]),
                         reads=[B_nt[kk], B_m, B_mder], writes=[B_HB])

            stage1(0)
            for bi in range(nblk):
                if bi + 1 < nblk:
                    stage1(bi + 1)
                stage2(bi)
            P.retire(B_sq + B_rs + B_nt, [B_MC])
            return hT

        nrm_tmp = [view(MC, 5120, 512), view(MC, 5632, 512)]; B_nt = [Buf("nt0"), Buf("nt1")]

        win = view(WB, 0, 7808, BF16, "p (k n) -> p k n", n=IN_DIM)
        B_win = Buf("win")

        def prefetch_win(l):
            wv = di["w_in"][l].rearrange("(kc p) n -> p kc n", p=128)
            P.retire(B_wb, [B_win])
            for (c0, c1) in ((0, 512), (512, 1024), (1024, 1536), (1536, IN_DIM)):
                P.dma("pool", lambda e, c0=c0, c1=c1: e.dma_start(out=win[:, :, c0:c1], in_=wv[:, :, c0:c1]), writes=[B_win])

        def in_proj(l, hT):
            stg = [SG[:, 0, :], SG[:, 1, :]]
            cvb = [view(MC, 0, 2312), view(MC, 2312, 2312)]; B_cv = [Buf("cv0"), Buf("cv1")]
            P.retire([B_MC], B_cv)
            pidx = 0
            for cc in range(16):
                m = 128 if cc < 15 else 32
                k = cc % 2
                sgt = stg[k]; sgb = B_sg[k]
                for bi, (t0, n) in enumerate(TB):
                    pi = pidx % 8; pidx += 1
                    for kc in range(8):
                        P.op("pe", lambda e, cc=cc, m=m, kc=kc, t0=t0, n=n, pi=pi: e.matmul(ps_t[pi][0:m, 0:n], lhsT=win[:, kc, cc * 128:cc * 128 + m], rhs=hT[:, kc, t0:t0 + n], start=(kc == 0), stop=(kc == 7)),
                             reads=[B_win, B_HB], writes=[PSB[pi]])
                    o0 = 1 + t0 if t0 < S else 3 + t0
                    if bi % 2 == 0:
                        P.op("act", lambda e, m=m, sgt=sgt, o0=o0, n=n, pi=pi: e.copy(out=sgt[0:m, o0:o0 + n], in_=ps_t[pi][0:m, 0:n]), reads=[PSB[pi]], writes=[sgb])
                    else:
                        P.op("dve", lambda e, m=m, sgt=sgt, o0=o0, n=n, pi=pi: e.tensor_copy(out=sgt[0:m, o0:o0 + n], in_=ps_t[pi][0:m, 0:n]), reads=[PSB[pi]], writes=[sgb])
                if cc < 6:
                    cv = cvb[k]; cb = B_cv[k]
                    for pz in (0, 2049, 2050, 2307):
                        P.op("pool", lambda e, sgt=sgt, pz=pz: e.memset(sgt[:, pz:pz + 1], 0.0), writes=[sgb])
                    P.op("dve", lambda e, cc=cc, sgt=sgt, cv=cv: e.tensor_scalar(out=cv[:, 1:2307], in0=sgt[:, 1:2307], scalar1=hcw[:, cc, 1:2], scalar2=hcw[:, cc, 3:4], op0=ALU.mult, op1=ALU.add),
                         reads=[sgb, B_hcw], writes=[cb])
                    P.op("dve", lambda e, cc=cc, sgt=sgt, cv=cv: e.scalar_tensor_tensor(out=cv[:, 1:2307], in0=sgt[:, 0:2306], scalar=hcw[:, cc, 0:1], in1=cv[:, 1:2307], op0=ALU.mult, op1=ALU.add),
                         reads=[sgb, B_hcw, cb], writes=[cb])
                    P.op("dve", lambda e, cc=cc, sgt=sgt, cv=cv: e.scalar_tensor_tensor(out=cv[:, 1:2307], in0=sgt[:, 2:2308], scalar=hcw[:, cc, 2:3], in1=cv[:, 1:2307], op0=ALU.mult, op1=ALU.add),
                         reads=[sgb, B_hcw, cb], writes=[cb])
                    src_t = cv; src_b = cb
                else:
                    src_t = sgt; src_b = sgb
                P.dma("sp", lambda e, cc=cc, m=m, src_t=src_t: e.dma_start(out=pT_d[cc * 128:cc * 128 + m, 0:S], in_=src_t[0:m, 1:1 + S]), reads=[src_b], writes=[B_p])
                P.dma("sp", lambda e, cc=cc, m=m, src_t=src_t: e.dma_start(out=pT_d[cc * 128:cc * 128 + m, S:T], in_=src_t[0:m, 2051:2051 + LC]), reads=[src_b], writes=[B_p])
            vst = [view(MC, 4624, 256, BF16, "p (h j) -> p h j", j=128), view(MC, 4880, 256, BF16, "p (h j) -> p h j", j=128)]
            B_vst = [Buf("vst0"), Buf("vst1")]
            P.retire([B_MC], B_vst)
            for k in range(2):
                P.op("pool", lambda e, k=k: e.memset(vst[k], 1.0), writes=[B_vst[k]])
            for ti in range(18):
                pi = pidx % 8; pidx += 1
                k = ti % 2
                for kc in range(8):
                    P.op("pe", lambda e, kc=kc, ti=ti, pi=pi: e.matmul(ps_t[pi][:, 0:256], lhsT=hT[:, kc, ti * 128:(ti + 1) * 128], rhs=win[:, kc, 1280:1536], start=(kc == 0), stop=(kc == 7)),
                         reads=[B_win, B_HB], writes=[PSB[pi]])
                for h in range(4):
                    o = 0 if h % 2 == 0 else 64
                    if h < 2:
                        P.op("act", lambda e, h=h, o=o, k=k, pi=pi: e.copy(out=vst[k][:, h, o:o + 64], in_=ps_t[pi][:, h * 64:(h + 1) * 64]), reads=[PSB[pi]], writes=[B_vst[k]])
                    else:
                        P.op("dve", lambda e, h=h, o=o, k=k, pi=pi: e.tensor_copy(out=vst[k][:, h, o:o + 64], in_=ps_t[pi][:, h * 64:(h + 1) * 64]), reads=[PSB[pi]], writes=[B_vst[k]])
                P.dma("sp", lambda e, ti=ti, k=k: e.dma_start(out=vna_d[ti * 128:(ti + 1) * 128, :, :], in_=vst[k]), reads=[B_vst[k]], writes=[B_vna])
            P.retire([B_win], B_wb)
            P.retire(B_cv + B_vst, [B_MC])

        wo = view(HB, 4096, 4096, BF16, "p (k n) -> p k n", n=D)
        B_wo = Buf("wo")

        def prefetch_wo(l):
            wv = di["w_out"][l].rearrange("(kc p) n -> p kc n", p=128)
            P.retire([B_HB], [B_wo])
            for kh in range(2):
                P.dma("pool", lambda e, kh=kh: e.dma_start(out=wo[:, kh * 4:(kh + 1) * 4, :], in_=wv[:, kh * 4:(kh + 1) * 4, :]), writes=[B_wo])

        def out_proj_residual(l, nblk=5):
            mT = modT; B_m = B_mod
            mixT = view(MC, 0, 9216, BF16, "p (c t) -> p c t", t=T)
            P.dma("sp", lambda e: e.dma_start(out=mixT, in_=mixT_d.rearrange("(c p) t -> p c t", p=128)), reads=[B_mix], writes=[B_MC])
            pidx = 0
            for dc in range(8):
                for (t0, n) in TB[:nblk]:
                    pi = pidx % 8; pidx += 1
                    r = 0 if t0 < S else 1
                    for kc in range(8):
                        P.op("pe", lambda e, dc=dc, kc=kc, t0=t0, n=n, pi=pi: e.matmul(ps_t[pi][:, 0:n], lhsT=wo[:, kc, dc * 128:(dc + 1) * 128], rhs=mixT[:, kc, t0:t0 + n], start=(kc == 0), stop=(kc == 7)),
                             reads=[B_wo, B_MC], writes=[PSB[pi]])
                    P.op("dve", lambda e, dc=dc, t0=t0, n=n, pi=pi, r=r: e.scalar_tensor_tensor(out=xT[:, dc, t0:t0 + n], in0=ps_t[pi][:, 0:n], scalar=mT[:, 16 + dc, r:r + 1], in1=xT[:, dc, t0:t0 + n], op0=ALU.mult, op1=ALU.add),
                         reads=[PSB[pi], B_m, B_xT], writes=[B_xT])

        w1b = [view(WB, 0, 2048, BF16, "p (k n) -> p k n", n=512), view(WB, 2048, 2048, BF16, "p (k n) -> p k n", n=512)]
        w2b = [view(WB, 4096, 2048, BF16, "p (k n) -> p k n", n=D), view(WB, 6144, 2048, BF16, "p (k n) -> p k n", n=D)]
        B_w1 = [Buf("w1a"), Buf("w1b")]; B_w2 = [Buf("w2a"), Buf("w2b")]

        def ffn_wdma(l, j):
            w1v = di["w_ff1"][l].rearrange("(kc p) n -> p kc n", p=128)
            w2v = di["w_ff2"][l].rearrange("(hc p) n -> p hc n", p=128)
            k = j % 2
            P.dma("pool", lambda e: e.dma_start(out=w1b[k], in_=w1v[:, :, j * 512:(j + 1) * 512]), writes=[B_w1[k]])
            P.dma("pool", lambda e: e.dma_start(out=w2b[k], in_=w2v[:, j * 4:(j + 1) * 4, :]), writes=[B_w2[k]])

        def prefetch_ffn0(l):
            P.retire(B_wb, B_w1 + B_w2)
            ffn_wdma(l, 0)

        def ffn(l, hT, nxt=None, nblk=5):
            mT = modT; B_m = B_mod
            b1l = b1ts[l % 2]; B_b1l = B_b1s[l % 2]
            aT = [view(MC, 0, 1024, BF16, "p (c t) -> p c t", t=512), view(MC, 1024, 1024, BF16, "p (c t) -> p c t", t=512)]
            rl = [view(MC, 2048, 512), view(MC, 2560, 512)]
            B_a = [Buf("aT0"), Buf("aT1")]; B_rl = [Buf("rl0"), Buf("rl1")]
            P.retire([B_MC], B_a + B_rl)
            pidx = [0]
            items = [(j, t0, n) for j in range(8) for (t0, n) in TB[:nblk]]

            def ff1(idx):
                j, t0, n = items[idx]
                k = j % 2; a = idx % 2
                if t0 == 0 and j > 0:
                    ffn_wdma(l, j)
                for hc in range(4):
                    pi = pidx[0] % 8; pidx[0] += 1
                    for kc in range(8):
                        P.op("pe", lambda e, k=k, hc=hc, kc=kc, t0=t0, n=n, pi=pi: e.matmul(ps_t[pi][:, 0:n], lhsT=w1b[k][:, kc, hc * 128:(hc + 1) * 128], rhs=hT[:, kc, t0:t0 + n], start=(kc == 0), stop=(kc == 7)),
                             reads=[B_w1[k], B_HB], writes=[PSB[pi]])
                    rr = hc % 2
                    P.op("dve", lambda e, j=j, hc=hc, n=n, pi=pi, rr=rr: e.tensor_scalar(out=rl[rr][:, 0:n], in0=ps_t[pi][:, 0:n], scalar1=b1l[:, j * 4 + hc:j * 4 + hc + 1], scalar2=0.0, op0=ALU.add, op1=ALU.max),
                         reads=[PSB[pi], B_b1l], writes=[B_rl[rr]])
                    P.op("act", lambda e, a=a, hc=hc, n=n, rr=rr: e.activation(out=aT[a][:, hc, 0:n], in_=rl[rr][:, 0:n], func=AF.Square),
                         reads=[B_rl[rr]], writes=[B_a[a]])

            def ff2(idx):
                j, t0, n = items[idx]
                k = j % 2; a = idx % 2
                r = 0 if t0 < S else 1
                for dc in range(8):
                    pi = pidx[0] % 8; pidx[0] += 1
                    for hc in range(4):
                        P.op("pe", lambda e, k=k, a=a, dc=dc, hc=hc, n=n, pi=pi: e.matmul(ps_t[pi][:, 0:n], lhsT=w2b[k][:, hc, dc * 128:(dc + 1) * 128], rhs=aT[a][:, hc, 0:n], start=(hc == 0), stop=(hc == 3)),
                             reads=[B_w2[k], B_a[a]], writes=[PSB[pi]])
                    P.op("dve", lambda e, dc=dc, t0=t0, n=n, pi=pi, r=r: e.scalar_tensor_tensor(out=xT[:, dc, t0:t0 + n], in0=ps_t[pi][:, 0:n], scalar=mT[:, 40 + dc, r:r + 1], in1=xT[:, dc, t0:t0 + n], op0=ALU.mult, op1=ALU.add),
                         reads=[PSB[pi], B_m, B_xT], writes=[B_xT])
                    if j == 0:
                        P.op("dve", lambda e, dc=dc, t0=t0, n=n, r=r: e.tensor_scalar(out=xT[:, dc, t0:t0 + n], in0=xT[:, dc, t0:t0 + n], scalar1=gb2[:, dc, r:r + 1], scalar2=None, op0=ALU.add),
                             reads=[B_xT, B_mder], writes=[B_xT])

            if nxt is not None:
                P.retire([B_MC], B_mw)
                mod_begin(nxt[0], nxt[1], nxt[2])
            qn_ = [0]
            ff1(0)
            for idx in range(len(items)):
                if idx + 1 < len(items):
                    ff1(idx + 1)
                ff2(idx)
                if nxt is not None and items[idx][1] == S and qn_[0] < 12:
                    for _ in range(2):
                        pi = pidx[0] % 8; pidx[0] += 1
                        mod_piece(nxt[0], qn_[0], pi, nxt[1], nxt[2]); qn_[0] += 1
            if nxt is not None:
                assert qn_[0] == 12
                P.retire(B_mw, [B_MC])
            P.retire(B_w1 + B_w2, B_wb)
            P.retire(B_a + B_rl, [B_MC])

        WBALL = B_wb

        def MM(out, lhsT, rhs, start, stop, reads, wbuf):
            P.op("pe", lambda e: e.matmul(out, lhsT=lhsT, rhs=rhs, start=start, stop=stop), reads=reads, writes=[wbuf])

        def MMH(out, lhsT, rhs, start, stop, reads, wbuf):
            P.op("pe", lambda e: e.matmul(out, lhsT=lhsT, rhs=rhs, start=start, stop=stop), reads=reads, writes=[wbuf])

        def MMN(out, lhsT, rhs, start, stop, reads, wbuf):
            P.op("pe", lambda e: e.matmul(out, lhsT=lhsT, rhs=rhs, start=start, stop=stop), reads=reads, writes=[wbuf])

        def MMF(out, lhsT, rhs, start, stop, reads, wbuf):
            P.op("pe", lambda e: e.matmul(out, lhsT=lhsT, rhs=rhs, start=start, stop=stop), reads=reads, writes=[wbuf])

        def MMI(out, lhsT, rhs, start, stop, reads, wbuf):
            P.op("pe", lambda e: e.matmul(out, lhsT=lhsT, rhs=rhs, start=start, stop=stop), reads=reads, writes=[wbuf])

        def MMS(out, lhsT, rhs, start, stop, reads, wbuf):
            P.op("pe", lambda e: e.matmul(out, lhsT=lhsT, rhs=rhs, start=start, stop=stop), reads=reads, writes=[wbuf])

        def TT(eng, out, in0, in1, op, reads, wbuf):
            P.op(eng, lambda e: e.tensor_tensor(out=out, in0=in0, in1=in1, op=op), reads=reads, writes=[wbuf])

        def STT(eng, out, in0, scalar, in1, op0, op1, reads, wbuf):
            P.op(eng, lambda e: e.scalar_tensor_tensor(out=out, in0=in0, scalar=scalar, in1=in1, op0=op0, op1=op1), reads=reads, writes=[wbuf])

        def TS(eng, out, in0, s1, s2, op0, op1, reads, wbuf):
            if s2 is None:
                P.op(eng, lambda e: e.tensor_scalar(out=out, in0=in0, scalar1=s1, scalar2=None, op0=op0), reads=reads, writes=[wbuf])
            else:
                P.op(eng, lambda e: e.tensor_scalar(out=out, in0=in0, scalar1=s1, scalar2=s2, op0=op0, op1=op1), reads=reads, writes=[wbuf])

        def ACTF(out, in_, func, reads, wbuf, **kw):
            P.op("act", lambda e: e.activation(out=out, in_=in_, func=func, **kw), reads=reads, writes=[wbuf])

        def CP(eng, out, in_, reads, wbuf):
            if eng == "act":
                P.op("act", lambda e: e.copy(out=out, in_=in_), reads=reads, writes=[wbuf])
            else:
                P.op(eng, lambda e: e.tensor_copy(out=out, in_=in_), reads=reads, writes=[wbuf])

        def sin_p1(arg_ps, psb, f_ap, fb_ap, tA, tB, tC, Bt, n):
            a = tA[0:64, 0:n]; b_ = tB[0:64, 0:n]; c_ = tC[0:64, 0:n]
            TS("dve", c_, arg_ps, f_ap, fb_ap, ALU.mult, ALU.add, [psb, B_filp], Bt)
            ACTF(a, c_, AF.Sin, [Bt], Bt, scale=0.125)
            ACTF(b_, c_, AF.Sin, [Bt], Bt, scale=0.25)

        def sin_p2(out_ap, outb, tA, tB, tC, Bt, n):
            a = tA[0:64, 0:n]; b_ = tB[0:64, 0:n]
            TT("dve", a, a, a, ALU.mult, [Bt], Bt)
            TS("dve", a, a, -2.0, 1.0, ALU.mult, ALU.add, [Bt], Bt)
            TT("dve", a, b_, a, ALU.mult, [Bt], Bt)
            TT("dve", b_, b_, b_, ALU.mult, [Bt], Bt)
            TS("dve", b_, b_, -2.0, 1.0, ALU.mult, ALU.add, [Bt], Bt)
            STT("dve", out_ap, a, 4.0, b_, ALU.mult, ALU.mult, [Bt], outb)

        def hyena(l, L, last):
            lat = (L == S)
            ntc = L // 128; GJ = min(4, ntc); G = ntc // GJ
            tok0 = 0 if lat else S
            dft = di["dftL"] if lat else di["dftC"]
            alpha = alphaL if lat else alphaC
            zT_d = di["zTL"] if lat else di["zTC"]
            dec_d = di["decL"] if lat else di["decC"]
            stw = GJ * ntc * 128
            STf = [view(XA, k * 8192, stw, BF16) for k in range(2)] if lat else [view(MC, 3072 + k * 512, stw, BF16) for k in range(2)]
            STv = [v.rearrange("p (m c j f) -> p m c j f", j=GJ, m=2, c=ntc) for v in STf]
            B_st = [Buf("st0"), Buf("st1")]
            KT = view(HB, 0, ntc * 512, BF16, "p (c m o f) -> p c m o f", c=ntc, m=2, o=2); B_kt = Buf("KT")
            AB = view(MC, 0, ntc * 512, BF16, "p (c m o f) -> p c m o f", c=ntc, m=2, o=2); B_ab = Buf("AB")
            P.retire([B_xT] if lat else [B_MC], B_st); P.retire([B_HB], [B_kt]); P.retire([B_MC], [B_ab])
            sti = [0]

            def load_st(g):
                k = sti[0] % 2; sti[0] += 1
                P.dma("sp", lambda e: e.dma_start(out=STf[k], in_=dft[g]), writes=[B_st[k]])
                return STv[k], B_st[k]

            hdn1 = view(WB, 0, 2049)
            hdn2a = view(WB, 2050, 1025, BF16)
            w3a = view(WB, 3076, 512, BF16)
            zbs = [view(WB, 3588, 512), view(WB, 4100, 512)]
            w1t = view(WB, 4612, 64); w2t = view(WB, 4676, 64)
            hd = view(WB, 4740, 1024); habs = view(WB, 5764, 512, BF16); hb1 = view(WB, 6276, 512)
            dect = [view(WB, 6788, 512, F32, "p (a f) -> p a f", a=2), view(WB, 7300, 512, F32, "p (a f) -> p a f", a=2)]
            B_f = Buf("fw"); B_h1 = Buf("hdn1"); B_h2 = Buf("hdn2"); B_zb = [Buf("zb0"), Buf("zb1")]
            B_hd = Buf("hd"); B_habs = Buf("habs"); B_hb1 = Buf("hb1"); B_dec = [Buf("dec0"), Buf("dec1")]; B_tmp = Buf("sintmp")
            P.retire(WBALL, [B_f, B_h1, B_h2, B_hd, B_habs, B_hb1, B_tmp] + B_zb + B_dec)
            ld("sp", w1t[0:17, 0:64], di["hy_f_w1"][l], B_f)
            ld("sp", w2t[0:64, 0:64], di["hy_f_w2"][l], B_f)
            P.dma("pool", lambda e: e.dma_start(out=w3a[0:64, :], in_=di["hy_f_w3"][l]), writes=[B_f])
            P.dma("pool", lambda e: e.dma_start(out=w3a[64:65, :], in_=di["hy_f_b3"][l].rearrange("(o n) -> o n", o=1)), writes=[B_f])
            P.op("pool", lambda e: e.memset(hdn2a[64:65, 0:L + 1], 1.0), writes=[B_h2])
            P.op("pool", lambda e: e.memset(hdn2a[0:64, L:L + 1], 0.0), writes=[B_h2])
            bn = min(512, L); nb = L // bn
            B_tmp2 = Buf("sintmp2"); P.retire(B_sg, [B_tmp2])
            tsets = [(hd[:, 0:512], hd[:, 512:1024], hb1, B_tmp), (SG[:, 0, 0:512], SG[:, 0, 512:1024], SG[:, 0, 1024:1536], B_tmp2)]
            B_h1b = [Buf("hdn1_%d" % i_) for i_ in range(nb)]
            P.retire([B_h1], B_h1b)
            for pr in range(0, nb, 2):
                blks = [pr] + ([pr + 1] if pr + 1 < nb else [])
                for i_, blk in enumerate(blks):
                    zb = zbs[blk % 2]; zbb = B_zb[blk % 2]
                    ld("sp", zb[0:17, 0:bn], zT_d[:, blk * bn:(blk + 1) * bn], zbb)
                    MMH(ps_t[i_][0:64, 0:bn], w1t[0:17, 0:64], zb[0:17, 0:bn], True, True, [B_f, zbb], PSB[i_])
                    sin_p1(ps_t[i_][0:64, 0:bn], PSB[i_], filp[0:64, 2:3], filp[0:64, 4:5], *tsets[i_], bn)
                for i_, blk in enumerate(blks):
                    sin_p2(hdn1[0:64, blk * bn:(blk + 1) * bn], B_h1b[blk], *tsets[i_], bn)
                for i_, blk in enumerate(blks):
                    MMH(ps_t[2 + i_][0:64, 0:bn], w2t[0:64, 0:64], hdn1[0:64, blk * bn:(blk + 1) * bn], True, True, [B_f, B_h1b[blk]], PSB[2 + i_])
                    sin_p1(ps_t[2 + i_][0:64, 0:bn], PSB[2 + i_], filp[0:64, 3:4], filp[0:64, 5:6], *tsets[i_], bn)
                for i_, blk in enumerate(blks):
                    sin_p2(hdn2a[0:64, blk * bn:(blk + 1) * bn], B_h2, *tsets[i_], bn)
            P.retire(B_h1b, [B_h1]); P.retire([B_tmp2], B_sg)
            if HSTOP == 1:
                return
            P.retire([B_tmp], [B_hd, B_hb1])
            hdS = [hd, SG[:, 0, 0:1024]]; hb1S = [hb1, SG[:, 0, 1024:1536]]; habsS = [habs, SG[:, 1, 0:512].bitcast(BF16)]
            B_hdS = [B_hd, Buf("hd2")]; B_hb1S = [B_hb1, Buf("hb12")]; B_habsS = [B_habs, Buf("habs2")]
            P.retire(B_sg, [B_hdS[1], B_hb1S[1], B_habsS[1]])
            banksets = [(2, 3, 4), (0, 1, 7)]
            pend = []

            def flush_pend():
                for (tcp, kp) in pend:
                    MMH(ps_t[5][:, 0:512], ones_b, habsS[kp][:, 0:512], tcp == 0, tcp == ntc - 1, [B_ones, B_habsS[kp]], PSB[5])
                    MMH(ps_t[6][:, 0:512], ones_b, habsS[kp][:, 512:1024], tcp == 0, tcp == ntc - 1, [B_ones, B_habsS[kp]], PSB[6])
                del pend[:]

            for tc in range(ntc):
                k = tc % 2
                b0, b1_, b2_ = banksets[k]
                hd_ = hdS[k]; hb_1 = hb1S[k]; habs_ = habsS[k]
                ld("sp", dect[k], dec_d[tc * 128:(tc + 1) * 128], B_dec[k])
                lt = hdn2a[0:65, tc * 128:(tc + 1) * 128]
                MMH(ps_t[b0][:, 0:512], lt, w3a[0:65, 0:512], True, True, [B_h2, B_f], PSB[b0])
                MMH(ps_t[b1_][:, 0:512], lt, w3a[0:65, 512:1024], True, True, [B_h2, B_f], PSB[b1_])
                lts = hdn2a[0:65, tc * 128 + 1:(tc + 1) * 128 + 1]
                for o in range(2):
                    MMH(ps_t[b2_][:, o * 256:(o + 1) * 256], lts, w3a[0:65, o * 512 + 256:(o + 1) * 512], True, True, [B_h2, B_f], PSB[b2_])
                flush_pend()
                d0 = dect[k][:, 0:1, :].to_broadcast([128, 2, 256]); d1 = dect[k][:, 1:2, :].to_broadcast([128, 2, 256])
                for hb_, bb in enumerate((b0, b1_)):
                    TT("dve", hd_[:, hb_ * 512:(hb_ + 1) * 512].rearrange("p (g f) -> p g f", f=256), ps_t[bb][:, 0:512].rearrange("p (g f) -> p g f", f=256), d0, ALU.mult,
                       [PSB[bb], B_dec[k]], B_hdS[k])
                TT("dve", hb_1.rearrange("p (g f) -> p g f", f=256), ps_t[b2_][:, 0:512].rearrange("p (g f) -> p g f", f=256), d1, ALU.mult, [PSB[b2_], B_dec[k]], B_hb1S[k])
                ACTF(habs_, hd_, AF.Abs, [B_hdS[k]], B_habsS[k])
                pend.append((tc, k))
                for o in range(2):
                    TT("pool", AB[:, tc, 0, o, :], hd_[:, o * 512:o * 512 + 256], hb_1[:, o * 256:(o + 1) * 256], ALU.add, [B_hdS[k], B_hb1S[k]], B_ab)
                    TT("dve", AB[:, tc, 1, o, :], hb_1[:, o * 256:(o + 1) * 256], hd_[:, o * 512:o * 512 + 256], ALU.subtract, [B_hdS[k], B_hb1S[k]], B_ab)
            flush_pend()
            P.retire([B_hdS[1], B_hb1S[1], B_habsS[1]], B_sg)
            for o in range(2):
                CP("dve", rn_t[:, o, :], ps_t[5 + o][:, 256:512], [PSB[5 + o]], B_rn)
                TT("dve", rn_t[:, o, :], rn_t[:, o, :], ps_t[5 + o][:, 0:256], ALU.add, [PSB[5 + o], B_rn], B_rn)
            rnf = rn_t.rearrange("p o c -> p (o c)")
            TS("dve", rnf, rnf, EPS, float(L), ALU.add, ALU.mult, [B_rn], B_rn)
            P.op("dve", lambda e: e.reciprocal(out=rnf, in_=rnf), reads=[B_rn], writes=[B_rn])
            if HSTOP == 2:
                return
            kt1 = [SG[:, 0, 0:512], SG[:, 0, 512:1024]]; B_k1 = [Buf("kt1a"), Buf("kt1b")]
            P.retire(B_sg, B_k1)
            for g in range(G):
                stv, stb = load_st(g)
                for j in range(GJ):
                    fc = g * GJ + j
                    br, bi_ = (0, 1) if fc % 2 == 0 else (2, 3)
                    for m, bk in ((0, br), (1, bi_)):
                        for tc in range(ntc):
                            MMS(ps_t[bk][:, 0:512], stv[:, m, tc, j, :], AB[:, tc, m, :, :].rearrange("p o f -> p (o f)"), tc == 0, tc == ntc - 1, [stb, B_ab], PSB[bk])
                    for mo, (a1, a2) in enumerate(((0, 2), (1, 0))):
                        kk = mo
                        TS("dve", kt1[kk], ps_t[br][:, 0:512], alpha[:, a1, fc:fc + 1], None, ALU.mult, None, [PSB[br], B_alpha], B_k1[kk])
                        STT("dve", kt1[kk], ps_t[bi_][:, 0:512], alpha[:, a2, fc:fc + 1], kt1[kk], ALU.mult, ALU.add, [PSB[bi_], B_alpha, B_k1[kk]], B_k1[kk])
                        TT("pool", KT[:, fc, mo, :, :].rearrange("p o f -> p (o f)"), kt1[kk], rnf, ALU.mult, [B_k1[kk], B_rn], B_kt)
            if HSTOP == 3:
                return
            P.retire(B_k1, B_sg)
            P.retire([B_f, B_h1, B_h2, B_hd, B_habs, B_hb1] + B_zb + B_dec, WBALL)
            u = view(WB, 0, 2 * L, F32, "p (c t) -> p c t", c=2); gate = view(WB, 4096, 2 * L, F32, "p (c t) -> p c t", c=2)
            B_u = Buf("u"); B_g = Buf("gate")
            P.retire(WBALL, [B_u, B_g])
            utm = view(MC, 0, ntc * 128, BF16, "p (c f) -> p c f", c=ntc); B_utm = Buf("utm")
            Z = view(MC, 2048, ntc * 256, BF16, "p (c m f) -> p c m f", c=ntc, m=2); B_z = Buf("Z")
            ubf = view(MC, 6144, L, BF16, "p (c t) -> p c t", c=2); B_ubf = Buf("ubf")
            P.retire([B_ab], [B_utm, B_z, B_ubf])
            T4 = [SG[:, 0, 0:1024].rearrange("p (a f) -> p a f", a=4), SG[:, 0, 1024:2048].rearrange("p (a f) -> p a f", a=4)]
            B_t4 = [Buf("t4a"), Buf("t4b")]
            etmp = [SG[:, 1, 0:512], SG[:, 1, 512:1024]]; B_et = [Buf("eta"), Buf("etb")]
            ohs = [SG[:, 1, 1024:1280].bitcast(BF16), SG[:, 1, 1280:1536].bitcast(BF16)]; B_oh = [Buf("oha"), Buf("ohb")]
            P.retire(B_sg, B_t4 + B_et + B_oh)
            P.dma("sp", lambda e: e.dma_start(out=u, in_=pT_d[0:256, tok0:tok0 + L].rearrange("(c p) t -> p c t", p=128)), reads=[B_p], writes=[B_u])
            cnt = [0]
            for o in range(2):
                P.dma("sp", lambda e, o=o: e.dma_start(out=gate, in_=pT_d[256 * (o + 1):256 * (o + 2), tok0:tok0 + L].rearrange("(c p) t -> p c t", p=128)), reads=[B_p], writes=[B_g])
                for cc in range(2):
                    CP("act" if cc == 0 else "dve", ubf[:, cc, :], u[:, cc, :], [B_u], B_ubf)
                for tc in range(ntc):
                    for cc in range(2):
                        bk = 6 + (cnt[0] % 2); cnt[0] += 1
                        pb16 = ps_t[bk][:, 0:64].bitcast(BF16)
                        P.op("pe", lambda e, pb16=pb16, tc=tc, cc=cc: e.transpose(out=pb16, in_=ubf[:, cc, tc * 128:(tc + 1) * 128], identity=identb), reads=[B_ubf, B_identb], writes=[PSB[bk]])
                        CP("act" if cc == 0 else "dve", utm[:, tc, cc * 128:(cc + 1) * 128], pb16, [PSB[bk]], B_utm)
                if HSTOP == 4:
                    return
                for g in range(G):
                    stv, stb = load_st(g)
                    for j in range(GJ):
                        fc = g * GJ + j
                        bk = 4 + fc % 2
                        for m in range(2):
                            for tc in range(ntc):
                                MMF(ps_t[bk][:, m * 256:(m + 1) * 256], stv[:, m, tc, j, :], utm[:, tc, :], tc == 0, tc == ntc - 1, [stb, B_utm], PSB[bk])
                        Pp = ps_t[bk][:, 0:256]; Qq = ps_t[bk][:, 256:512]
                        Kr = KT[:, fc, 0, o, :]; Ki = KT[:, fc, 1, o, :]
                        t4 = T4[fc % 2]; tb_ = B_t4[fc % 2]
                        TT("dve", t4[:, 0, :], Pp, Kr, ALU.mult, [PSB[bk], B_kt], tb_)
                        TT("dve", t4[:, 1, :], Qq, Ki, ALU.mult, [PSB[bk], B_kt], tb_)
                        TT("dve", t4[:, 2, :], Qq, Kr, ALU.mult, [PSB[bk], B_kt], tb_)
                        TT("dve", t4[:, 3, :], Pp, Ki, ALU.mult, [PSB[bk], B_kt], tb_)
                        TT("pool", Z[:, fc, 0, :], t4[:, 0, :], t4[:, 1, :], ALU.add, [tb_], B_z)
                        TT("pool", Z[:, fc, 1, :], t4[:, 2, :], t4[:, 3, :], ALU.subtract, [tb_], B_z)
                if HSTOP == 5:
                    return
                wdt = GJ * 128
                for g in range(G):
                    stv, stb = load_st(g)
                    for cc in range(2):
                        bk = (g * 2 + cc) % 4
                        ops = ps_t[bk][:, 0:wdt]
                        for fc in range(ntc):
                            for m in range(2):
                                MMI(ops, Z[:, fc, m, cc * 128:(cc + 1) * 128], stv[:, m, fc, :, :].rearrange("p j f -> p (j f)"), fc == 0 and m == 0, fc == ntc - 1 and m == 1, [stb, B_z], PSB[bk])
                        t0 = g * wdt
                        kk = (g * 2 + cc) % 2
                        et = etmp[kk][:, 0:wdt]
                        STT("dve", et, u[:, cc, t0:t0 + wdt], hyb[:, o, cc:cc + 1], ps_t[bk][:, 0:wdt], ALU.mult, ALU.add, [B_u, B_hyb, PSB[bk]], B_et[kk])
                        if o == 0:
                            TT("pool", u[:, cc, t0:t0 + wdt], et, gate[:, cc, t0:t0 + wdt], ALU.mult, [B_et[kk], B_g], B_u)
                        else:
                            TT("pool", ubf[:, cc, t0:t0 + wdt], et, gate[:, cc, t0:t0 + wdt], ALU.mult, [B_et[kk], B_g], B_ubf)
            P.dma("sp", lambda e: e.dma_start(out=mixT_d[0:256, tok0:tok0 + L].rearrange("(c p) t -> p c t", p=128), in_=ubf), reads=[B_ubf], writes=[B_mix])
            P.retire(B_t4 + B_et + B_oh, B_sg)
            P.retire([B_u, B_g], WBALL)
            P.retire([B_utm, B_z, B_ubf], [B_MC])
            P.retire(B_st, [B_xT] if lat else [B_MC]); P.retire([B_kt], [B_HB])

        def rms_feat(raw, rawb, c_out, outb, lhs_ones, m, scale, gain_ap, sqv, B_sqv, rsv, B_rsv, psbanks, nchunks=1, raws=None, outs=None, gains=None):
            raws = raws or [raw]; outs = outs or [c_out]; gains = gains or [gain_ap]

            def st1(bi):
                t0, n = TB[bi]; k = bi % 2
                for ci, rw in enumerate(raws):
                    ACTF(sqv[k][0:m, ci, 0:n], rw[0:m, t0:t0 + n], AF.Square, [rawb], B_sqv[k])
                bk = psbanks[k]
                for ci in range(len(raws)):
                    MM(ps_t[bk][0:m, 0:n], lhs_ones, sqv[k][0:m, ci, 0:n], ci == 0, ci == len(raws) - 1, [B_sqv[k], B_ones, B_blk64, B_o96], PSB[bk])

            def st2(bi):
                t0, n = TB[bi]; k = bi % 2
                bk = psbanks[k]
                rstd_from_ps(ps_t[bk][0:m, 0:n], rsv[k][0:m, 0:n], PSB[bk], B_rsv[k], scale, 0, m)
                for ci, rw in enumerate(raws):
                    STT("dve", outs[ci][0:m, t0:t0 + n], rw[0:m, t0:t0 + n], gains[ci], rsv[k][0:m, 0:n], ALU.mult, ALU.mult, [rawb, B_rsv[k], B_nag, B_mlg], outb)

            st1(0)
            for bi in range(len(TB)):
                if bi + 1 < len(TB):
                    st1(bi + 1)
                st2(bi)

        def attn_core(keys, kT_of, q_ap, n, v_of, scale, sbanks, obank, PTs, B_PT, ctr, kreads, vreads, look=2, tick=None):
            nk = len(keys); slots = []

            def s_part(i):
                sb_ = sbanks[ctr[0] % len(sbanks)]; pk = ctr[0] % len(PTs); ctr[0] += 1
                MM(ps_t[sb_][:, 0:n], kT_of(keys[i]), q_ap, True, True, kreads, PSB[sb_])
                ACTF(PTs[pk][:, 0:n], ps_t[sb_][:, 0:n], AF.Exp, [PSB[sb_]], B_PT[pk], scale=scale)
                slots.append(pk)

            def v_part(i):
                pk = slots[i]
                MM(ps_t[obank][:, 0:n], v_of(keys[i]), PTs[pk][:, 0:n], i == 0, i == nk - 1, vreads + [B_PT[pk]], PSB[obank])

            for i in range(min(look, nk)):
                s_part(i)
            for i in range(nk):
                if i + look < nk:
                    s_part(i + look)
                v_part(i)
                if tick is not None:
                    tick()

        def attn_core_pair(keys, kT_of, q_ap, v_of, scale, obank, PTs, B_PT, ctr, kreads, vreads, tick=None):
            n = 512
            npair = len(keys) // 2
            assert len(keys) % 2 == 0
            slots = []

            def s_part(pi_):
                pr = ctr[0] % 2; ctr[0] += 1
                b0 = 2 * pr
                for u_ in range(2):
                    MM(ps_t[b0 + u_][:, 0:n], kT_of(keys[2 * pi_ + u_]), q_ap, True, True, kreads, PSB[b0 + u_])
                pt = view(XA, PTs[0], 512, BF16) if pr == 0 else view(XA, PTs[1], 512, BF16)
                P.op("act", lambda e: e.activation(out=pt, in_=PS_ALL[:, b0 * 512:(b0 + 2) * 512], func=AF.Exp, scale=scale), reads=[PSB[b0], PSB[b0 + 1]], writes=[B_PT[2 * pr], B_PT[2 * pr + 1]])
                slots.append((pt, pr))

            def v_part(pi_):
                pt, pr = slots[pi_]
                for u_ in range(2):
                    i = 2 * pi_ + u_
                    MM(ps_t[obank][:, 0:n], v_of(keys[i]), pt[:, u_ * 512:(u_ + 1) * 512], i == 0, i == len(keys) - 1, vreads + [B_PT[2 * pr], B_PT[2 * pr + 1]], PSB[obank])
                    if tick is not None:
                        tick()

            s_part(0)
            for pi_ in range(npair):
                if pi_ + 1 < npair:
                    s_part(pi_ + 1)
                v_part(pi_)

        def normalize(obank, n, odd, out_ap, outb, rden, B_rd, use_dve=False):
            nu = slice(64, 128) if odd else slice(0, 64)
            de = slice(0, 64) if odd else slice(64, 128)
            if use_dve:
                P.op("dve", lambda e: e.reciprocal(out=rden[de, 0:n], in_=ps_t[obank][de, 0:n]), reads=[PSB[obank]], writes=[B_rd])
                TT("dve", out_ap, ps_t[obank][nu, 0:n], rden[de, 0:n], ALU.mult, [PSB[obank], B_rd], outb)
                return
            P.op("act", lambda e: e.activation(out=rden[de, 0:n], in_=ps_t[obank][de, 0:n], func=AF.Ln), reads=[PSB[obank]], writes=[B_rd])
            P.op("act", lambda e: e.activation(out=rden[de, 0:n], in_=rden[de, 0:n], func=AF.Exp, scale=-1.0), reads=[B_rd], writes=[B_rd])
            TT("dve", out_ap, ps_t[obank][nu, 0:n], rden[de, 0:n], ALU.mult, [PSB[obank], B_rd], outb)

        def na(l, last):
            qn = view(HB, 0, 2304, BF16, "p (c t) -> p c t", c=2); kn = view(HB, 2304, 2304, BF16, "p (c t) -> p c t", c=2)
            B_qk = Buf("naqk")
            Vt0 = view(MC, 0, 4608, BF16, "p (t h j) -> p t h j", t=18, h=4); Vt1 = view(MC, 4608, 3840, BF16, "p (t h j) -> p t h j", t=15, h=4)
            B_v = Buf("naV")
            Btab = view(WB, 0, 1920, BF16, "p (h k) -> p h k", h=60); Traw = view(WB, 1920, 3840, F32, "p (h k) -> p h k", h=60)
            B_bt = Buf("Btab"); B_tr = Buf("Traw")
            rsv = [view(WB, 5760, 512), view(WB, 6272, 512)]; B_rsv = [Buf("nrs0"), Buf("nrs1")]
            sqv = [view(WB, 6784, 256, BF16, "p (c t) -> p c t", c=1), view(WB, 7040, 256, BF16, "p (c t) -> p c t", c=1)]; B_sqv = [Buf("nsq0"), Buf("nsq1")]
            onaT = view(XA, 0, 2304, BF16, "p (c t) -> p c t", c=2); B_o = Buf("onaT")
            PTs = [view(XA, 2304 + 256 * i_, 256, BF16) for i_ in range(4)]; B_PT = [Buf("npt%d" % i_) for i_ in range(4)]
            rden = [view(XA, 3328, 512), view(XA, 3840, 512)]; B_rd = [Buf("nrd0"), Buf("nrd1")]; B_ptc = Buf("ptc")
            PTc = view(XA, 4352, 256, BF16)
            P.retire([B_HB], [B_qk]); P.retire([B_MC], [B_v]); P.retire(WBALL, [B_bt, B_tr] + B_rsv + B_sqv)
            P.retire([B_xT], [B_o, B_ptc] + B_PT + B_rd)
            for c in range(2):
                P.dma("sp", lambda e, c=c: e.dma_start(out=SG[:, c, 0:T], in_=pT_d[768 + c * 128:768 + (c + 1) * 128, :]), reads=[B_p], writes=[B_sg[c]])
            rp = di["na_rpb"]
            src = bass.AP(rp.tensor, l * RPB_PAD + 16, [[1, 64], [31, 60], [1, 64]])
            for hh in range(2):
                P.dma("sp", lambda e, hh=hh: e.dma_start(out=Traw[hh * 64:(hh + 1) * 64], in_=src), writes=[B_tr])
            TT("dve", Traw, Traw, namask[:, 0:1, :].to_broadcast([128, 60, 64]), ALU.mult, [B_tr, B_namask], B_tr)
            TT("dve", Btab, Traw, namask[:, 1:2, :].to_broadcast([128, 60, 64]), ALU.add, [B_tr, B_namask], B_bt)
            P.dma("sp", lambda e: e.dma_start(out=Vt0, in_=vna_d.rearrange("(t p) h j -> p t h j", p=128)), reads=[B_vna], writes=[B_v])
            P.dma("sp", lambda e: e.dma_start(out=Vt1, in_=vna_d[64:64 + 1920].rearrange("(t p) h j -> p t h j", p=128)), reads=[B_vna], writes=[B_v])
            for wi, (row0, dst) in enumerate(((768, qn), (1024, kn))):
                for c in range(2):
                    raw = SG[:, c, 0:T]; rb = B_sg[c]
                    if wi == 1:
                        P.dma("sp", lambda e, raw=raw, row0=row0, c=c: e.dma_start(out=raw, in_=pT_d[row0 + c * 128:row0 + (c + 1) * 128, :]), reads=[B_p], writes=[rb])
                    rms_feat(raw, rb, dst[:, c, :], B_qk, blk64, 128, 1.0, nag[:, wi:wi + 1], sqv, B_sqv, rsv, B_rsv, (6, 7))
            ctr = [0]; og = [0]
            sbanks = (0, 1, 2, 3)
            LOOK = 2
            for h in range(4):
                c = h // 2; pb = (h % 2) * 64; odd = (h % 2 == 1)
                ps_ = slice(pb, pb + 64)
                slots = {}

                def s_part(r, h=h, c=c, ps_=ps_, slots=slots):
                    a = min(max(r - 4, 0), 24)
                    sb_ = sbanks[ctr[0] % 4]; pk = ctr[0] % 4; ctr[0] += 1
                    qa = qn[ps_, c, r * 64:(r + 1) * 64]
                    for i in range(4):
                        kr0 = a + 2 * i; dr0 = kr0 - r + 7
                        MMN(ps_t[sb_][:, i * 64:(i + 1) * 64], kn[ps_, c, kr0 * 64:kr0 * 64 + 128], qa, True, False, [B_qk], PSB[sb_])
                        MMN(ps_t[sb_][:, i * 64:(i + 1) * 64], Btab[ps_, h * 15 + dr0:h * 15 + dr0 + 2, :].rearrange("p a k -> p (a k)"), ident2[ps_, 0:64], False, True, [B_bt, B_ident2], PSB[sb_])
                    for j in range(2):
                        MMN(ps_t[sb_][:, (4 + j) * 64:(5 + j) * 64], kn[ps_, c, S + 128 * j:S + 128 * (j + 1)], qa, True, True, [B_qk], PSB[sb_])
                    ACTF(PTs[pk][:, 0:384], ps_t[sb_][:, 0:384], AF.Exp, [PSB[sb_]], B_PT[pk], scale=0.125)
                    slots[r] = pk

                def v_part(r, h=h, c=c, ps_=ps_, odd=odd, slots=slots):
                    a = min(max(r - 4, 0), 24)
                    pk = slots[r]
                    rr = r % 8
                    obank = 4 + (og[0] % 2)
                    for i in range(6):
                        if i < 4:
                            vt = Vt0[:, a // 2 + i, h, :] if a % 2 == 0 else Vt1[:, (a - 1) // 2 + i, h, :]
                        else:
                            vt = Vt0[:, 16 + (i - 4), h, :]
                        MMN(ps_t[obank][:, rr * 64:(rr + 1) * 64], vt, PTs[pk][:, i * 64:(i + 1) * 64], i == 0, i == 5, [B_v, B_PT[pk]], PSB[obank])
                    if rr == 7:
                        rk = og[0] % 2; og[0] += 1
                        r8 = r // 8
                        normalize(obank, 512, odd, onaT[ps_, c, r8 * 512:(r8 + 1) * 512], B_o, rden[rk], B_rd[rk])

                for r in range(LOOK):
                    s_part(r)
                for r in range(32):
                    if r + LOOK < 32:
                        s_part(r + LOOK)
                    v_part(r)
                if not last:
                    obank = 4 + og[0] % 2; rk = og[0] % 2; og[0] += 1
                    sb_ = sbanks[ctr[0] % 4]; ctr[0] += 1
                    for j in range(2):
                        MMN(ps_t[sb_][:, j * 256:(j + 1) * 256], kn[ps_, c, S + 128 * j:S + 128 * (j + 1)], qn[ps_, c, S:T], True, True, [B_qk], PSB[sb_])
                    ACTF(PTc[:, 0:512], ps_t[sb_][:, 0:512], AF.Exp, [PSB[sb_]], B_ptc, scale=0.125)
                    for j in range(2):
                        MMN(ps_t[obank][:, 0:256], Vt0[:, 16 + j, h, :], PTc[:, j * 256:(j + 1) * 256], j == 0, j == 1, [B_v, B_ptc], PSB[obank])
                    normalize(obank, 256, odd, onaT[ps_, c, S:T], B_o, rden[rk], B_rd[rk])
            ncol = S if last else T
            P.dma("sp", lambda e: e.dma_start(out=mixT_d[256:512, 0:ncol].rearrange("(c p) t -> p c t", p=128), in_=onaT[:, :, 0:ncol]), reads=[B_o], writes=[B_mix])
            P.retire([B_qk], [B_HB]); P.retire([B_v], [B_MC]); P.retire([B_bt, B_tr] + B_rsv + B_sqv, WBALL)
            P.retire([B_o, B_ptc] + B_PT + B_rd, [B_xT])

        def mla(l, last):
            cqn = view(HB, 0, 2304, BF16, "p (c t) -> p c t", c=2); ckvn = view(HB, 2304, 1152, BF16)
            krf = view(HB, 4608, 2304)
            B_cq = Buf("cqn"); B_ckv = Buf("ckvn"); B_kr = Buf("krf")
            Vm = view(MC, 0, 9216, BF16, "p (t h j) -> p t h j", t=18, h=8); B_vm = Buf("Vm")
            wq = view(WB, 0, 768, BF16, "p (k n) -> p k n", k=2); wkv = view(WB, 768, 512, BF16); B_w = Buf("mlaw")
            rsv = [view(WB, 2048, 512), view(WB, 2560, 512)]; B_rsv = [Buf("mrs0"), Buf("mrs1")]
            rt = [view(WB, 3072, 512), view(WB, 3584, 512)]; B_rt = [Buf("mrt0"), Buf("mrt1")]
            sqv = [view(WB, 4096, 512, BF16, "p (c t) -> p c t", c=2), view(WB, 4608, 512, BF16, "p (c t) -> p c t", c=2)]; B_sqv = [Buf("msq0"), Buf("msq1")]
            sql = [view(WB, 5120 + 256 * i_, 256, BF16) for i_ in range(10)]; B_sql = [Buf("sql%d" % i_) for i_ in range(10)]
            ropeC = view(XA, 0, 2304); ropeS = view(XA, 2304, 2304); B_rope = Buf("rope")
            qk = [view(XA, 4608 + i * 1152, 1152, BF16) for i in range(4)]; B_qkf = [Buf("qkf%d" % i) for i in range(4)]
            xhat = [view(XA, 9216, 1152, BF16), view(XA, 10368, 1152, BF16)]; B_xh = [Buf("xh0"), Buf("xh1")]
            omT = view(XA, 11520, 4608, BF16, "p (c t) -> p c t", c=4); B_om = Buf("omT")
            PTs = [view(XA, 16128 + 256 * i_, 256, BF16) for i_ in range(4)]; B_PT = [Buf("mpt%d" % i_) for i_ in range(4)]
            rden = [view(XA, 17152, 512), view(XA, 17664, 512)]; B_rd = [Buf("mrd0"), Buf("mrd1")]
            P.retire([B_HB], [B_cq, B_ckv, B_kr]); P.retire([B_MC], [B_vm]); P.retire(WBALL, [B_w] + B_rsv + B_rt + B_sqv + B_sql)
            P.retire([B_xT], [B_rope, B_om] + B_qkf + B_xh + B_PT + B_rd)
            P.dma("pool", lambda e: e.dma_start(out=wq, in_=di["mla_w_q_up"][l].rearrange("(k p) n -> p k n", p=128)), writes=[B_w])
            P.dma("pool", lambda e: e.dma_start(out=wkv, in_=di["mla_w_kv_up"][l]), writes=[B_w])
            raws = [SG[:, 0, 0:T], SG[:, 1, 0:T]]
            for c in range(2):
                P.dma("sp", lambda e, c=c: e.dma_start(out=raws[c], in_=pT_d[1536 + c * 128:1536 + (c + 1) * 128, :]), reads=[B_p], writes=[B_sg[c]])
            ld("sp", ropeC[0:96, :], di["ropeC"], B_rope); ld("sp", ropeS[0:96, :], di["ropeS"], B_rope)
            B_raw2 = Buf("raw2"); P.retire(B_sg, [B_raw2])
            rms_feat(None, B_raw2, None, B_cq, ones_b, 128, 1.0 / 256, None, sqv, B_sqv, rsv, B_rsv, (6, 7),
                     raws=raws, outs=[cqn[:, 0, :], cqn[:, 1, :]], gains=[mlg[:, 0:1], mlg[:, 1:2]])
            P.retire([B_raw2], B_sg)
            P.dma("sp", lambda e: e.dma_start(out=raws[0], in_=pT_d[1792:1920, :]), reads=[B_p], writes=[B_sg[0]])
            rms_feat(raws[0], B_sg[0], ckvn, B_ckv, ones_b, 128, 1.0 / 128, mlg[:, 2:3], sqv, B_sqv, rsv, B_rsv, (6, 7))
            P.dma("sp", lambda e: e.dma_start(out=krf[0:32, :], in_=pT_d[1920:1952, :]), reads=[B_p], writes=[B_kr])
            P.op("pool", lambda e: e.memset(view(MC, 0, 9216, BF16), 1.0), writes=[B_vm])
            for ti in range(18):
                for hf in range(2):
                    bk = 6 + hf
                    MM(ps_t[bk][:, 0:512], ckvn[:, ti * 128:(ti + 1) * 128], wkv[:, hf * 512:(hf + 1) * 512], True, True, [B_ckv, B_w], PSB[bk])
                    for hh in range(4):
                        h = hf * 4 + hh
                        o = 0 if h % 2 == 0 else 64
                        CP("act" if hh % 2 == 0 else "dve", Vm[:, ti, h, o:o + 64], ps_t[bk][:, hh * 128 + 64:hh * 128 + 128], [PSB[bk]], B_vm)
            qraw = SG[:, 0, 0:T]; kraw = SG[:, 1, 0:T]
            rden = [view(HB, 6912, 512), view(HB, 7424, 512)]
            t2v = [view(WB, 4096, 512), view(WB, 4608, 512)]
            PTp = [view(XA, 16128 + 512 * i_, 512, BF16) for i_ in range(3)]; B_PTp = [Buf("ptp%d" % i_) for i_ in range(3)]
            P.retire(B_PT + B_rd, B_PTp)
            P.retire([B_cq], B_rd)

            B_rl = [[Buf("rl%d_%d" % (wi, bi)) for bi in range(5)] for wi in range(2)]
            B_xl = [[Buf("xl%d_%d" % (wi, bi)) for bi in range(5)] for wi in range(2)]
            P.retire(B_sg, B_rl[0] + B_rl[1]); P.retire(B_xh, B_xl[0] + B_xl[1])
            CP("dve", kraw[64:96, :], krf[0:32, :], [B_kr], B_rl[1][0])
            P.retire([B_rl[1][0]], B_rl[1][1:])
            SK = 2

            def prep(h):
                par = h % 2
                lanes = [(wi, bi, t0, n) for bi, (t0, n) in enumerate(TB) for wi in range(2)]
                raws_ = (qraw, kraw)
                for bi, (t0, n) in enumerate(TB):
                    bq = bi % 2; bk_ = 2 + bi % 2
                    for kc in range(2):
                        MM(ps_t[bq][0:96, 0:n], wq[:, kc, h * 96:(h + 1) * 96], cqn[:, kc, t0:t0 + n], kc == 0, kc == 1, [B_w, B_cq], PSB[bq])
                    CP("act", qraw[0:96, t0:t0 + n], ps_t[bq][0:96, 0:n], [PSB[bq]], B_rl[0][bi])
                    TT("pool", sql[2 * bi][0:96, 0:n], qraw[0:96, t0:t0 + n], qraw[0:96, t0:t0 + n], ALU.mult, [B_rl[0][bi]], B_sql[2 * bi])
                    MM(ps_t[bk_][0:64, 0:n], wkv[:, h * 128:h * 128 + 64], ckvn[:, t0:t0 + n], True, True, [B_w, B_ckv], PSB[bk_])
                    CP("act", kraw[0:64, t0:t0 + n], ps_t[bk_][0:64, 0:n], [PSB[bk_]], B_rl[1][bi])
                    TT("dve", sql[2 * bi + 1][0:96, 0:n], kraw[0:96, t0:t0 + n], kraw[0:96, t0:t0 + n], ALU.mult, [B_rl[1][bi]], B_sql[2 * bi + 1])

                def stage_b(li):
                    wi, bi, t0, n = lanes[li]
                    raw = raws_[wi]; xh = xhat[wi]
                    k = li % 2; bk = 4 + k
                    MM(ps_t[bk][0:96, 0:n], ones_b[0:96, 0:96], sql[li][0:96, 0:n], True, True, [B_sql[li], B_ones], PSB[bk])
                    rstd_from_ps(ps_t[bk][0:96, 0:n], rsv[k][0:96, 0:n], PSB[bk], B_rsv[k], 1.0 / 96, 0, 96)
                    STT("dve", xh[0:96, t0:t0 + n], raw[0:96, t0:t0 + n], mlg[0:96, 3 + wi:4 + wi], rsv[k][0:96, 0:n], ALU.mult, ALU.mult, [B_rl[wi][bi], B_rsv[k], B_mlg], B_xl[wi][bi])

                def stage_d(li):
                    wi, bi, t0, n = lanes[li]
                    xh = xhat[wi]; fin = qk[par * 2 + wi]; fb = B_qkf[par * 2 + wi]
                    k = li % 2; bk = li % 4
                    MM(ps_t[bk][0:96, 0:n], permT[0:96, 0:96], xh[0:96, t0:t0 + n], True, True, [B_perm, B_xl[wi][bi]], PSB[bk])
                    rtb = rt[k].bitcast(BF16); t2b = t2v[k].bitcast(BF16)
                    TT("dve", rtb[0:96, 0:n], ps_t[bk][0:96, 0:n], ropeS[0:96, t0:t0 + n], ALU.mult, [PSB[bk], B_rope], B_rt[k])
                    TT("pool", t2b[0:96, 0:n], xh[0:96, t0:t0 + n], ropeC[0:96, t0:t0 + n], ALU.mult, [B_xl[wi][bi], B_rope], B_sqv[k])
                    TT("dve" if li % 2 == 0 else "pool", fin[0:96, t0:t0 + n], rtb[0:96, 0:n], t2b[0:96, 0:n], ALU.add, [B_rt[k], B_sqv[k]], fb)

                for step in range(len(lanes) + SK):
                    if step < len(lanes):
                        stage_b(step)
                    if step >= SK:
                        stage_d(step - SK)

            sc = 96.0 ** -0.5
            blocks = TB[:4] if last else TB
            ctr = {"pair": 0, "ob": 0}
            LOOK = 2

            def attn(h):
                par = h % 2
                qf = qk[par * 2]; kf = qk[par * 2 + 1]; kr_ = [B_qkf[par * 2], B_qkf[par * 2 + 1]]
                odd = (h % 2 == 1); pb = 64 if odd else 0
                items = []
                for (t0, n) in blocks:
                    keys = list(range(18)) if t0 < S else [16, 17]
                    npair = len(keys) // 2
                    for pi_ in range(npair):
                        items.append((t0, n, keys[2 * pi_], keys[2 * pi_ + 1], pi_ == 0, pi_ == npair - 1))
                slots = {}

                def s_part(ix):
                    t0, n, k0, k1, first, lastp = items[ix]
                    sl = ctr["pair"] % 3; ctr["pair"] += 1; slots[ix] = sl
                    b0 = 2 * sl
                    for u_, kc in enumerate((k0, k1)):
                        MM(ps_t[b0 + u_][:, 0:n], kf[0:96, kc * 128:(kc + 1) * 128], qf[0:96, t0:t0 + n], True, True, kr_, PSB[b0 + u_])
                    if n == 512:
                        P.op("act", lambda e: e.activation(out=PTp[sl], in_=PS_ALL[:, b0 * 512:(b0 + 2) * 512], func=AF.Exp, scale=sc), reads=[PSB[b0], PSB[b0 + 1]], writes=[B_PTp[sl]])
                    else:
                        src = PS_ALL[:, b0 * 512:(b0 + 2) * 512].rearrange("p (b f) -> p b f", b=2)[:, :, 0:n]
                        dst = PTp[sl].rearrange("p (b f) -> p b f", b=2)[:, :, 0:n]
                        P.op("act", lambda e: e.activation(out=dst, in_=src, func=AF.Exp, scale=sc), reads=[PSB[b0], PSB[b0 + 1]], writes=[B_PTp[sl]])

                def v_part(ix):
                    t0, n, k0, k1, first, lastp = items[ix]
                    sl = slots[ix]
                    if first:
                        ctr["ob"] += 1
                    obank = 6 + ctr["ob"] % 2; rk = ctr["ob"] % 2
                    for u_, kc in enumerate((k0, k1)):
                        MM(ps_t[obank][:, 0:n], Vm[:, kc, h, :], PTp[sl][:, u_ * 512:u_ * 512 + n], first and u_ == 0, lastp and u_ == 1, [B_vm, B_PTp[sl]], PSB[obank])
                    if lastp:
                        normalize(obank, n, odd, omT[pb:pb + 64, h // 2, t0:t0 + n], B_om, rden[rk], B_rd[rk], use_dve=True)

                for ix in range(min(LOOK, len(items))):
                    s_part(ix)
                for ix in range(len(items)):
                    if ix + LOOK < len(items):
                        s_part(ix + LOOK)
                    v_part(ix)

            prep(0)
            for h in range(8):
                if h + 1 < 8:
                    prep(h + 1)
                attn(h)
            P.retire(B_PTp, B_PT + B_rd)
            P.retire(B_rl[0] + B_rl[1], B_sg); P.retire(B_xl[0] + B_xl[1], B_xh)
            if debug:
                for i_ in range(2):
                    P.dma("sp", lambda e, i_=i_: e.dma_start(out=dbg_d[i_], in_=qk[2 + i_][0:96, :]), reads=[B_qkf[2 + i_]], writes=[B_dbg])
            ncol = S if last else T
            P.dma("sp", lambda e: e.dma_start(out=mixT_d[512:1024, 0:ncol].rearrange("(c p) t -> p c t", p=128), in_=omT[:, :, 0:ncol]), reads=[B_om], writes=[B_mix])
            P.retire([B_cq, B_ckv, B_kr] + B_rd, [B_HB]); P.retire([B_vm], [B_MC]); P.retire([B_w] + B_rsv + B_rt + B_sqv + B_sql, WBALL)
            P.retire([B_rope, B_om] + B_qkf + B_xh + B_PT + B_rd, [B_xT])

        class _M:
            pass
        mix = _M()
        mix.hyena = hyena; mix.na = na; mix.mla = mla

        for l in range(n_layers):
            last = (l == NL - 1)
            modT = modTs[l % 2]; B_mod = B_mods[l % 2]
            if l == 0:
                small_loads(0)
                prefetch_win(0)
                modulation_all(0, modT, B_mod)
                mod_finish(modT, B_mod)
            else:
                prefetch_win(l)
            hT = norm_adaln(0)
            in_proj(l, hT)
            if stop == "inproj":
                break
            mix.hyena(l, S, last)
            if stop == "hyena":
                break
            mix.na(l, last)
            if stop == "na":
                break
            mix.mla(l, last)
            if stop == "mla":
                break
            prefetch_wo(l)
            P.dma("act", lambda e: e.dma_start(out=xT, in_=xTd_v), reads=[B_x], writes=[B_xT])
            if not last:
                mix.hyena(l, LC, last)
            nb_ = 4 if last else 5
            prefetch_ffn0(l)
            out_proj_residual(l, nb_)
            P.retire([B_wo], [B_HB])
            hT = norm_adaln(1, nb_)
            if l + 1 < n_layers:
                small_loads(l + 1)
            ffn(l, hT, nxt=((l + 1, modTs[(l + 1) % 2], B_mods[(l + 1) % 2]) if l + 1 < n_layers else None), nblk=nb_)
            if l + 1 < n_layers:
                mod_finish(modTs[(l + 1) % 2], B_mods[(l + 1) % 2])
            if not last:
                P.dma("sp", lambda e: e.dma_start(out=xTd_v, in_=xT), reads=[B_xT], writes=[B_x])

        if stop is None:
            osb = [view(MC, 0, 1024), view(MC, 1024, 1024)]; B_os = [Buf("os0"), Buf("os1")]
            P.retire([B_MC], B_os)
            for ti in range(16):
                k = ti % 2
                for c in range(8):
                    pi = c
                    P.op("pe", lambda e, c=c, ti=ti, pi=pi: e.transpose(out=ps_t[pi][:, 0:128], in_=xT[:, c, ti * 128:(ti + 1) * 128], identity=identf), reads=[B_xT, B_identf], writes=[PSB[pi]])
                    if c % 2 == 0:
                        P.op("act", lambda e, c=c, k=k, pi=pi: e.copy(out=osb[k][:, c * 128:(c + 1) * 128], in_=ps_t[pi][:, 0:128]), reads=[PSB[pi]], writes=[B_os[k]])
                    else:
                        P.op("dve", lambda e, c=c, k=k, pi=pi: e.tensor_copy(out=osb[k][:, c * 128:(c + 1) * 128], in_=ps_t[pi][:, 0:128]), reads=[PSB[pi]], writes=[B_os[k]])
                P.dma("sp", lambda e, ti=ti, k=k: e.dma_start(out=out_d[ti * 128:(ti + 1) * 128, :], in_=osb[k]), reads=[B_os[k]], writes=[B_out])
        P.final_wait("sp", [B_out, B_x, B_p, B_vna, B_mix, B_dbg])
        P.emit()
    return nc


def make_in_maps(inputs):
    c = _consts()
    shared = {}
    for k in W_SPECS:
        a = np.ascontiguousarray(np.asarray(inputs[k], dtype=np.float32))
        if k == "na_rpb":
            a = np.ascontiguousarray(np.pad(a.reshape(NL, -1), ((0, 0), (64, 64))))
        shared[k] = a
    shared.update(c)
    maps = []
    x = np.asarray(inputs["x"], dtype=np.float32); ctx = np.asarray(inputs["ctx"], dtype=np.float32)
    cc = np.asarray(inputs["c"], dtype=np.float32); c_ctx = np.asarray(inputs["c_ctx"], dtype=np.float32)
    for b in range(x.shape[0]):
        m = dict(shared)
        m["x"] = np.ascontiguousarray(x[b]); m["ctx"] = np.ascontiguousarray(ctx[b])
        m["cvec"] = np.ascontiguousarray(np.stack([cc[b], c_ctx], axis=0))
        maps.append(m)
    return maps


_NC = None


def kernel(**inputs):
    global _NC
    if _NC is None:
        _NC = build_nc()
    maps = make_in_maps(inputs)
    res = run_bass_kernel_spmd(_NC, maps, core_ids=list(range(8)))
    return np.stack([np.asarray(r["out"], dtype=np.float32) for r in res.results], axis=0)
```
